# Optimizing a Trainium2 kernel written in Bass

```python
import math
import jax, jax.numpy as jnp
from jax import lax
import numpy as np

D_MODEL = 2048
BATCH = 8
SEQ = 4096
DEPTH = 2
DEC_BATCH = 8
DEC_SEQ = 16
PAST_LEN = 1024

CHUNK = 64
Q_BLOCK = 128
N_A_LAYERS = DEPTH // 2
N_B_LAYERS = DEPTH - N_A_LAYERS
D_FF = 5632
RWKV_HEAD = 64
RWKV_HEADS = D_MODEL // RWKV_HEAD
DECAY_LORA = 96
ICLR_LORA = 96
GATE_LORA = 256
GN_EPS = 64e-5
DIFF_HEADS = 16
DIFF_DH = D_MODEL // DIFF_HEADS // 2
QK_DIM = 2 * DIFF_HEADS * DIFF_DH
V_DIM = DIFF_HEADS * 2 * DIFF_DH
KV_DIM = QK_DIM + V_DIM
LN_EPS = 1e-5
ALPHA = (2.0 * DEPTH) ** 0.25
BETA = (8.0 * DEPTH) ** -0.25

kernel_name = "rwkv7_diffattn_yoco_stream_step"


def layer_norm(x, g, b):
    xf = x.astype(jnp.float32)
    mu = jnp.mean(xf, -1, keepdims=True)
    var = jnp.mean(jnp.square(xf - mu), -1, keepdims=True)
    return ((xf - mu) * lax.rsqrt(var + LN_EPS) * g + b).astype(x.dtype)


def deepnorm(x, delta, g, b):
    return layer_norm(ALPHA * x + delta, g, b)


def swiglu(x, w_in, w_out):
    gate, up = jnp.split(x @ w_in, 2, axis=-1)
    return (jax.nn.silu(gate) * up) @ w_out


def rwkv7_time_mix(x, prev, s0, mu, w_rkv, w0, w1, w2, a0, a1, a2, g1, g2,
                   k_k, k_a, r_k, lnx_g, lnx_b, w_o):
    f32 = jnp.float32
    b, t, d = x.shape
    xx = prev - x
    mix = lambda j: x + xx * mu[j]
    heads = lambda z: z.reshape(b, t, RWKV_HEADS, RWKV_HEAD)
    r = heads(mix(0) @ w_rkv[0])
    w_log = -jax.nn.softplus(-(w0 + jnp.tanh(mix(1) @ w1) @ w2)) - 0.5
    k = mix(2) @ w_rkv[1]
    v = heads(mix(3) @ w_rkv[2])
    a = jax.nn.sigmoid(a0 + (mix(4) @ a1) @ a2)
    g = jax.nn.sigmoid(mix(5) @ g1) @ g2
    kk = heads(k * k_k).astype(f32)
    kk = kk * lax.rsqrt(jnp.maximum(jnp.sum(kk * kk, -1, keepdims=True), 1e-24))
    k = heads(k * (1.0 + (a - 1.0) * k_a))
    a = heads(a)
    decay = jnp.exp(-jnp.exp(heads(w_log).astype(f32)))

    def step(s, inp):
        r_t, d_t, k_t, v_t, kk_t, a_t = inp
        sa = jnp.einsum('bhvk,bhk->bhv', s, kk_t)
        s = (s * d_t[:, :, None, :]
             - sa[..., None] * (kk_t * a_t)[:, :, None, :]
             + v_t[..., None] * k_t[:, :, None, :])
        return s, jnp.einsum('bhvk,bhk->bhv', s, r_t)

    xs = tuple(jnp.swapaxes(z.astype(f32), 0, 1) for z in (r, decay, k, v, kk, a))
    s_final, y = lax.scan(step, s0.astype(f32), xs)
    y = jnp.swapaxes(y, 0, 1)
    y_mu = jnp.mean(y, -1, keepdims=True)
    y_var = jnp.mean(jnp.square(y - y_mu), -1, keepdims=True)
    y = ((y - y_mu) * lax.rsqrt(y_var + GN_EPS)).reshape(b, t, d) * lnx_g + lnx_b
    bonus = jnp.sum(r.astype(f32) * k.astype(f32) * r_k, -1, keepdims=True) * v.astype(f32)
    y = (y + bonus.reshape(b, t, d)) * g
    return y.astype(x.dtype) @ w_o, s_final.astype(s0.dtype)


def diff_lambda_full(lam, lambda_init):
    lam = lam.astype(jnp.float32)
    return jnp.exp(jnp.sum(lam[0] * lam[1])) - jnp.exp(jnp.sum(lam[2] * lam[3])) + lambda_init


def diff_attend(q, k, v, mask, lam_full, subln_g, lambda_init):
    b, tq = q.shape[:2]
    tk = k.shape[1]
    s = jnp.einsum('bqhd,bkhd->bhqk', q, k).astype(jnp.float32) * (DIFF_DH ** -0.5)
    if mask is not None:
        s = jnp.where(mask, s, -jnp.inf)
    p = jax.nn.softmax(s, axis=-1).reshape(b, DIFF_HEADS, 2, tq, tk)
    p = p[:, :, 0] - lam_full * p[:, :, 1]
    o = jnp.einsum('bhqk,bkhe->bqhe', p.astype(v.dtype), v).astype(jnp.float32)
    o = o * lax.rsqrt(jnp.mean(o * o, -1, keepdims=True) + LN_EPS) * subln_g * (1.0 - lambda_init)
    return o.reshape(b, tq, V_DIM).astype(q.dtype)


def diff_attn_prompt(q, k, v, lam_full, subln_g, lambda_init):
    b, t = q.shape[:2]
    nblk = t // Q_BLOCK
    q_blocks = jnp.swapaxes(q.reshape(b, nblk, Q_BLOCK, 2 * DIFF_HEADS, DIFF_DH), 0, 1)
    key_chunk = jnp.arange(t) // CHUNK

    def one_block(args):
        q_blk, blk = args
        q_chunk = (blk * Q_BLOCK + jnp.arange(Q_BLOCK)) // CHUNK
        mask = key_chunk[None, :] <= q_chunk[:, None]
        return diff_attend(q_blk, k, v, mask, lam_full, subln_g, lambda_init)

    out = lax.map(one_block, (q_blocks, jnp.arange(nblk)))
    return jnp.swapaxes(out, 0, 1).reshape(b, t, V_DIM)


def setup_inputs(seed: int = 0) -> dict:
    key = jax.random.key(seed)
    ks = iter(jax.random.split(key, 48))
    nrm = lambda shape, scale: scale * jax.random.normal(next(ks), shape, jnp.float32)
    D = D_MODEL
    nA, nB = N_A_LAYERS, N_B_LAYERS
    return {
        "x_prompt": nrm((BATCH, SEQ, D), 1.0),
        "x_sample": nrm((DEC_BATCH, DEC_SEQ, D), 1.0),
        "cache_k": nrm((DEC_BATCH, PAST_LEN, 2 * DIFF_HEADS, DIFF_DH), 1.0),
        "cache_v": nrm((DEC_BATCH, PAST_LEN, DIFF_HEADS, 2 * DIFF_DH), 1.0),
        "state_wkv": nrm((nA, DEC_BATCH, RWKV_HEADS, RWKV_HEAD, RWKV_HEAD), 0.5),
        "state_shift": nrm((nA, DEC_BATCH, D), 1.0),
        "ln_g": 1.0 + nrm((DEPTH, 3, D), 0.02),
        "ln_b": nrm((DEPTH, 3, D), 0.02),
        "ffn_w_in": nrm((DEPTH, 2, D, 2 * D_FF), D ** -0.5),
        "ffn_w_out": nrm((DEPTH, 2, D_FF, D), BETA * D_FF ** -0.5),
        "rwkv_mu": jax.random.uniform(next(ks), (nA, 6, D), jnp.float32),
        "rwkv_w_rkv": nrm((nA, 3, D, D), D ** -0.5),
        "rwkv_w0": -1.0 + nrm((nA, D), 1.0),
        "rwkv_w1": nrm((nA, D, DECAY_LORA), D ** -0.5),
        "rwkv_w2": nrm((nA, DECAY_LORA, D), 0.1 * DECAY_LORA ** -0.5),
        "rwkv_a0": nrm((nA, D), 0.1),
        "rwkv_a1": nrm((nA, D, ICLR_LORA), D ** -0.5),
        "rwkv_a2": nrm((nA, ICLR_LORA, D), 0.5 * ICLR_LORA ** -0.5),
        "rwkv_g1": nrm((nA, D, GATE_LORA), D ** -0.5),
        "rwkv_g2": nrm((nA, GATE_LORA, D), GATE_LORA ** -0.5),
        "rwkv_k_k": 0.85 + nrm((nA, D), 0.05),
        "rwkv_k_a": 1.0 + nrm((nA, D), 0.05),
        "rwkv_r_k": nrm((nA, RWKV_HEADS, RWKV_HEAD), 0.1),
        "rwkv_lnx_g": 1.0 + nrm((nA, D), 0.02),
        "rwkv_lnx_b": nrm((nA, D), 0.02),
        "rwkv_w_o": nrm((nA, D, D), BETA * D ** -0.5),
        "kv_w": nrm((D, KV_DIM), D ** -0.5),
        "diff_w_q": nrm((nB, D, QK_DIM), D ** -0.5),
        "diff_lambda": nrm((nB, 4, DIFF_DH), 0.1),
        "diff_subln_g": 1.0 + nrm((nB, 2 * DIFF_DH), 0.02),
        "diff_w_o": nrm((nB, V_DIM, D), BETA * V_DIM ** -0.5),
    }


def reference(x_prompt, x_sample, cache_k, cache_v, state_wkv, state_shift,
              ln_g, ln_b, ffn_w_in, ffn_w_out,
              rwkv_mu, rwkv_w_rkv, rwkv_w0, rwkv_w1, rwkv_w2, rwkv_a0, rwkv_a1, rwkv_a2,
              rwkv_g1, rwkv_g2, rwkv_k_k, rwkv_k_a, rwkv_r_k, rwkv_lnx_g, rwkv_lnx_b, rwkv_w_o,
              kv_w, diff_w_q, diff_lambda, diff_subln_g, diff_w_o):

    def trunk(x, past_k, past_v, wkv0, shift0):
        b, t, _ = x.shape
        new_wkv, new_shift = [], []
        k_sh = v_sh = None
        for i in range(DEPTH):
            x = deepnorm(x, 0.5 * swiglu(x, ffn_w_in[i, 0], ffn_w_out[i, 0]), ln_g[i, 0], ln_b[i, 0])
            if i < N_A_LAYERS:
                j = i
                prev = jnp.concatenate([shift0[j][:, None, :], x[:, :-1]], axis=1)
                h, s_new = rwkv7_time_mix(
                    x, prev, wkv0[j], rwkv_mu[j], rwkv_w_rkv[j], rwkv_w0[j], rwkv_w1[j], rwkv_w2[j],
                    rwkv_a0[j], rwkv_a1[j], rwkv_a2[j], rwkv_g1[j], rwkv_g2[j], rwkv_k_k[j],
                    rwkv_k_a[j], rwkv_r_k[j], rwkv_lnx_g[j], rwkv_lnx_b[j], rwkv_w_o[j])
                new_wkv.append(s_new)
                new_shift.append(x[:, -1])
            else:
                j = i - N_A_LAYERS
                lambda_init = 0.8 - 0.6 * math.exp(-0.3 * i)
                lam_full = diff_lambda_full(diff_lambda[j], lambda_init)
                q = (x @ diff_w_q[j]).reshape(b, t, 2 * DIFF_HEADS, DIFF_DH)
                if past_k is None:
                    o = diff_attn_prompt(q, k_sh, v_sh, lam_full, diff_subln_g[j], lambda_init)
                else:
                    keys = jnp.concatenate([past_k, k_sh], axis=1)
                    vals = jnp.concatenate([past_v, v_sh], axis=1)
                    o = diff_attend(q, keys, vals, None, lam_full, diff_subln_g[j], lambda_init)
                h = o @ diff_w_o[j]
            x = deepnorm(x, h, ln_g[i, 1], ln_b[i, 1])
            x = deepnorm(x, 0.5 * swiglu(x, ffn_w_in[i, 1], ffn_w_out[i, 1]), ln_g[i, 2], ln_b[i, 2])
            if i == N_A_LAYERS - 1:
                kv = x @ kv_w
                k_sh = kv[..., :QK_DIM].reshape(b, t, 2 * DIFF_HEADS, DIFF_DH)
                v_sh = kv[..., QK_DIM:].reshape(b, t, DIFF_HEADS, 2 * DIFF_DH)
        return x, k_sh, v_sh, jnp.stack(new_wkv), jnp.stack(new_shift)

    bp = x_prompt.shape[0]
    wkv_zero = jnp.zeros((N_A_LAYERS, bp, RWKV_HEADS, RWKV_HEAD, RWKV_HEAD), x_prompt.dtype)
    shift_zero = jnp.zeros((N_A_LAYERS, bp, D_MODEL), x_prompt.dtype)
    y_prompt, k_prompt, v_prompt, wkv_prompt, shift_prompt = trunk(
        x_prompt, None, None, wkv_zero, shift_zero)
    y_sample, k_sample, v_sample, wkv_sample, shift_sample = trunk(
        x_sample, cache_k, cache_v, state_wkv, state_shift)
    return (y_prompt, y_sample, k_prompt, v_prompt, wkv_prompt, shift_prompt,
            k_sample, v_sample, wkv_sample, shift_sample)
```

```python
import math
from contextlib import ExitStack
import numpy as np
import concourse.bass as bass
import concourse.mybir as mybir
from concourse.bass_utils import run_bass_kernel_spmd

F32 = mybir.dt.float32
BF16 = mybir.dt.bfloat16
ALU = mybir.AluOpType
AF = mybir.ActivationFunctionType
AX = mybir.AxisListType

ENGS = ("pe", "act", "dve", "pool", "sp")
N_DMA_SEMS = 48
DMA_HALF = 24

D = 2048
DC = 16
FF = 5632
MC = 44
TWP = 256
PAST = 1024
NSAMP = 16
ALPHA = (2.0 * 2) ** 0.25
LN_EPS = 1e-5
GN_EPS = 64e-5
LAMBDA_INIT = 0.8 - 0.6 * math.exp(-0.3 * 1)
NEG_E = -math.exp(-0.5)

PV_LNG, PV_LNB = 0, 6
PV_A0, PV_KK, PV_KA, PV_RK, PV_LXG, PV_LXB, PV_MU = 12, 13, 14, 15, 16, 17, 18
NPV = 24
C_ID, C_BO, C_MSL, C_MK, C_MB, C_TI, C_TE, C_W = 0, 128, 256, 384, 640, 896, 960, 1024

UA_FFN = 0
UA_RKV = 176
UA_RKVS = 200
UA_L1, UA_G1, UA_L1S, UA_G1S = 224, 225, 226, 227
UA_WO, UA_KV, UA_Q, UA_DWO = 228, 236, 252, 260
NUA = 268
SA_FFN, SA_RKV, SA_L1, SA_G1, SA_WO, SA_KV, SA_Q, SA_DWO, NSA = 0, 176, 200, 201, 202, 210, 226, 234, 242
NUB = 128
BW = 2816


class Buf:
    __slots__ = ("name", "lw", "rd")

    def __init__(self, name=""):
        self.name = name
        self.lw = None
        self.rd = {}


class Prog:
    def __init__(self, nc):
        self.nc = nc
        self.streams = {e: [] for e in ENGS}
        self.cnt = {e: 0 for e in ENGS}
        self.known = {e: {} for e in ENGS}
        self.sem = {}
        for e in ("pe", "act", "dve", "pool"):
            self.sem[e] = nc.alloc_semaphore(name="c_" + e)
        self.dsem = [nc.alloc_semaphore(name="d_%d" % i) for i in range(N_DMA_SEMS)]
        self.dval = [0] * N_DMA_SEMS
        self.drr = {"sp": 0, "pool": 0}
        self.out_events = {}
        self.n_ops = 0
        self.dry = False

    def _deps(self, eng, reads, writes):
        deps = {}
        for b in list(reads) + list(writes):
            if b.lw is not None:
                k, v = b.lw
                if deps.get(k, 0) < v:
                    deps[k] = v
        for b in writes:
            for k, v in b.rd.items():
                if deps.get(k, 0) < v:
                    deps[k] = v
        waits = []
        kn = self.known[eng]
        for k, v in deps.items():
            if eng == "pe" and k == "pe":
                continue
            if kn.get(k, 0) >= v:
                continue
            kn[k] = v
            waits.append((k, v))
        return waits

    def _commit(self, ev, reads, writes):
        k, v = ev
        for b in reads:
            if b.rd.get(k, 0) < v:
                b.rd[k] = v
        for b in writes:
            b.lw = ev
            b.rd = {}

    def _semh(self, k):
        return self.sem[k] if isinstance(k, str) else self.dsem[k]

    def op(self, eng, fn, reads=(), writes=()):
        if self.dry:
            return
        waits = self._deps(eng, reads, writes)
        self.cnt[eng] += 1
        ev = (eng, self.cnt[eng])
        self.streams[eng].append((waits, fn, (eng, 1)))
        self._commit(ev, reads, writes)
        self.n_ops += 1

    def dma(self, q, out_ap, in_ap, reads=(), writes=(), is_output=False, **kw):
        if self.dry:
            return
        i = self.drr[q] + (0 if q == "sp" else DMA_HALF)
        self.drr[q] = (self.drr[q] + 1) % DMA_HALF
        waits = self._deps(q, reads, writes)
        kn = self.known[q]
        if self.dval[i] > 0 and kn.get(i, 0) < self.dval[i]:
            kn[i] = self.dval[i]
            waits.append((i, self.dval[i]))
        self.dval[i] += 16
        ev = (i, self.dval[i])

        def fn(e, out_ap=out_ap, in_ap=in_ap, kw=kw):
            return e.dma_start(out=out_ap, in_=in_ap, **kw)
        self.streams[q].append((waits, fn, (i, 16)))
        self._commit(ev, reads, writes)
        if is_output:
            self.out_events[i] = self.dval[i]
        self.n_ops += 1

    def barrier(self):
        for e in ENGS:
            waits = []
            kn = self.known[e]
            for k in ("pe", "act", "dve", "pool"):
                if k != e and kn.get(k, 0) < self.cnt[k]:
                    kn[k] = self.cnt[k]
                    waits.append((k, self.cnt[k]))
            for i in range(N_DMA_SEMS):
                if kn.get(i, 0) < self.dval[i]:
                    kn[i] = self.dval[i]
                    waits.append((i, self.dval[i]))
            self.streams[e].append((waits, None, None))

    def finish(self):
        waits = []
        for i, v in self.out_events.items():
            if self.known["sp"].get(i, 0) < v:
                waits.append((i, v))
        self.streams["sp"].append((waits, None, None))

    def _replay(self, eng, e):
        for waits, fn, inc in self.streams[eng]:
            for k, v in waits:
                e.wait_ge(self._semh(k), v)
            if fn is not None:
                ins = fn(e)
                ins.then_inc(self._semh(inc[0]), inc[1])

    def emit(self):
        self.finish()
        nc = self.nc
        with nc.Block() as block:
            @block.tensor
            def _(e):
                self._replay("pe", e)

            @block.scalar
            def _(e):
                self._replay("act", e)

            @block.vector
            def _(e):
                self._replay("dve", e)

            @block.gpsimd
            def _(e):
                self._replay("pool", e)

            @block.sync
            def _(e):
                self._replay("sp", e)


class Rot:
    def __init__(self, items):
        self.items = items
        self.i = 0

    def next(self):
        it = self.items[self.i % len(self.items)]
        self.i += 1
        return it


def build(SEQ):
    import os
    STOP = int(os.environ.get('MK_STOP', '99'))
    REAL = [False]
    NT = SEQ // TWP
    nc = bass.Bass("TRN2", target_bir_lowering=False)

    def din(name, shape, dt=F32):
        return nc.dram_tensor(name, shape, dt, kind="ExternalInput").ap()

    def dout(name, shape):
        return nc.dram_tensor(name, shape, F32, kind="ExternalOutput").ap()

    def dscr(name, shape, dt):
        return nc.dram_tensor(name, shape, dt, kind="Internal").ap()

    xp = din("xp", [SEQ, D]); xs = din("xs", [NSAMP, D])
    ck = din("ck", [PAST, D]); cv = din("cv", [PAST, D])
    swkv = din("swkv", [128, 16, 64]); sshift = din("sshift", [128, 16])
    pcols_d = din("pcols", [128, NPV, 16])
    wA_src = din("wA_src", [NSA, 128, 4096]); wB_src = din("wB_src", [NUB, 128, BW])
    w2aug_d = din("w2aug", [128, D]); a2p_d = din("a2p", [128, D]); g2p_d = din("g2p", [128, 2, D])
    consts_d = din("consts", [128, C_W]); dlam_d = din("dlam", [1, 256]); subg_d = din("subg", [1, 128])

    yp = dout("yp", [SEQ, D]); ys = dout("ys", [NSAMP, D])
    kp = dout("kp", [SEQ, D]); vp = dout("vp", [SEQ, D])
    wkvp = dout("wkvp", [128, 16, 64]); shp = dout("shp", [128, 16])
    ksm = dout("ksm", [NSAMP, D]); vsm = dout("vsm", [NSAMP, D])
    wkvs = dout("wkvs", [128, 16, 64]); shs = dout("shs", [128, 16])

    wA_parts = [dscr("wA%d" % i, [67, 128, 4096], BF16) for i in range(4)]
    wA = [wA_parts[i // 67][i % 67] for i in range(NUA)]
    wB = dscr("wB", [NUB, 128, BW], BF16)
    KTscr = dscr("KTscr", [D, SEQ], BF16); Vscr = dscr("Vscr", [SEQ, D], BF16)
    KTs = dscr("KTs", [D, 64], BF16); Vs = dscr("Vs", [64, D], BF16)
    B_wA = [Buf() for _ in range(NUA)]; B_wB = [Buf() for _ in range(NUB)]
    B_KTscr = Buf(); B_Vscr = Buf(); B_KTs = Buf(); B_Vs = Buf()

    P = Prog(nc)

    def stage(n):
        if REAL[0] and STOP == n:
            P.dry = True

    def bar():
        if not P.dry:
            P.barrier()

    def mm(out, lhsT, rhs, start, stop, R, W):
        P.op("pe", lambda e: e.matmul(out, lhsT, rhs, start=start, stop=stop), R, W)

    def tr(out, in_, ident, R, W):
        P.op("pe", lambda e: e.transpose(out, in_, ident), R, W)

    def tt(eng, out, a, b, op, R, W):
        P.op(eng, lambda e: e.tensor_tensor(out, a, b, op), R, W)

    def ts(eng, out, a, s1, s2, op0, op1, R, W):
        if op1 is None:
            P.op(eng, lambda e: e.tensor_scalar(out, a, s1, None, op0), R, W)
        else:
            P.op(eng, lambda e: e.tensor_scalar(out, a, s1, s2, op0, op1), R, W)

    def stt(eng, out, in0, scalar, in1, op0, op1, R, W):
        P.op(eng, lambda e: e.scalar_tensor_tensor(out, in0, scalar, in1, op0, op1), R, W)

    def act(out, in_, func, R, W, **kw):
        P.op("act", lambda e: e.activation(out, in_, func, **kw), R, W)

    def cp(eng, out, in_, R, W):
        if eng == "act":
            P.op("act", lambda e: e.copy(out, in_), R, W)
        else:
            P.op(eng, lambda e: e.tensor_copy(out, in_), R, W)

    def mset(eng, ap, val, W):
        P.op(eng, lambda e: e.memset(ap, val), (), W)

    def recip(out, in_, R, W):
        P.op("dve", lambda e: e.reciprocal(out, in_), R, W)

    with ExitStack() as es0:
        def sb0(name, shape, dt=F32):
            return es0.enter_context(nc.sbuf_tensor(name, shape, dt))
        mu_t = sb0("mu_t", [128, 6, 16]); B_mu = Buf()
        P.dma("sp", mu_t[:], pcols_d[:, PV_MU:PV_MU + 6, :], writes=[B_mu])
        st32 = [(sb0("st32_%d" % i, [128, 4096]), Buf()) for i in range(3)]
        st16 = [(sb0("st16_%d" % i, [128, 4096], BF16), Buf()) for i in range(4)]
        r32 = Rot(st32); r16 = Rot(st16); reng = Rot(["dve", "act", "pool"])
        rq = Rot(["sp", "pool"])

        def conv_A(src_idx, dst_idx, scale=None):
            t32, b32 = r32.next()
            P.dma("sp", t32[:, 0:4096], wA_src[src_idx], writes=[b32])
            t16, b16 = r16.next()
            eng = reng.next()
            cp(eng, t16[:, 0:4096], t32[:, 0:4096], [b32], [b16])
            P.dma(rq.next(), wA[dst_idx], t16[:, 0:4096], reads=[b16], writes=[B_wA[dst_idx]])
            if scale is not None:
                dsts, specs = scale
                t16s, b16s = r16.next()
                for dc in range(DC):
                    for (c0, c1, mj) in specs:
                        eng = "dve" if (dc % 2 == 0) else "pool"
                        ts(eng, t16s[:, dc * 256 + c0: dc * 256 + c1], t32[:, dc * 256 + c0: dc * 256 + c1],
                           mu_t[:, mj, dc:dc + 1], None, ALU.mult, None, [b32, B_mu], [b16s])
                P.dma(rq.next(), wA[dsts], t16s[:, 0:4096], reads=[b16s], writes=[B_wA[dsts]])

        for u in range(176):
            conv_A(SA_FFN + u, UA_FFN + u)
        mu_of = [0, 2, 3]
        for j in range(3):
            for blk in range(8):
                conv_A(SA_RKV + j * 8 + blk, UA_RKV + j * 8 + blk, (UA_RKVS + j * 8 + blk, [(0, 256, mu_of[j])]))
        conv_A(SA_L1, UA_L1, (UA_L1S, [(0, 128, 1), (128, 256, 4)]))
        conv_A(SA_G1, UA_G1, (UA_G1S, [(0, 256, 5)]))
        for blk in range(8):
            conv_A(SA_WO + blk, UA_WO + blk)
        for blk in range(16):
            conv_A(SA_KV + blk, UA_KV + blk)
        for blk in range(8):
            conv_A(SA_Q + blk, UA_Q + blk)
        for blk in range(8):
            conv_A(SA_DWO + blk, UA_DWO + blk)
        for u in range(NUB):
            t32, b32 = r32.next()
            P.dma("sp", t32[:, 0:BW], wB_src[u], writes=[b32])
            t16, b16 = r16.next()
            if u % 2 == 0:
                P.op("act", lambda e, o=t16, i=t32: e.mul(o[:, 0:BW], i[:, 0:BW], 0.5), [b32], [b16])
            else:
                ts("dve", t16[:, 0:BW], t32[:, 0:BW], 0.5, None, ALU.mult, None, [b32], [b16])
            P.dma(rq.next(), wB[u], t16[:, 0:BW], reads=[b16], writes=[B_wB[u]])
        P.barrier()
    if STOP == 0:
        P.dry = True

    es = ExitStack()

    def sb(name, shape, dt=F32):
        return es.enter_context(nc.sbuf_tensor(name, shape, dt))

    with es:
        xres = sb("xres", [128, DC, TWP]); Bx = [Buf() for _ in range(DC)]
        xb = sb("xb", [128, DC, TWP], BF16); Bxb = [Buf() for _ in range(DC)]
        hid = sb("hid", [128, 48, TWP], BF16); Bh = [Buf() for _ in range(48)]
        sA = sb("sA", [128, DC, TWP], BF16); BsA = [Buf() for _ in range(DC)]
        sB = sb("sB", [128, DC, TWP], BF16); BsB = [Buf() for _ in range(DC)]
        tokst = sb("tokst", [128, 2, D]); Btk = [[Buf() for _ in range(DC)] for _ in range(2)]
        wslots = [(sb("wslot%d" % i, [128, 4096], BF16), Buf()) for i in range(3)]
        NS = len(wslots)
        banks = [(es.enter_context(nc.psum_tensor("bank%d" % i, [128, 512], F32)), Buf()) for i in range(8)]
        G = Rot([banks[i] for i in (0, 1, 2, 3)])
        Obanks = [banks[4], banks[5], banks[6], banks[7]]
        cst = sb("cst", [128, C_W]); Bc = Buf()
        pc = sb("pc", [128, NPV, 16]); Bpc = Buf()
        omka = sb("omka", [128, 16]); Bomka = Buf()
        w2aug = sb("w2aug_s", [128, D]); Bw2 = Buf()
        a2b = sb("a2b", [128, D], BF16); Ba2 = Buf()
        g2b = sb("g2b", [128, 2, D], BF16); Bg2 = Buf()
        Hst = sb("Hst", [128, 16, 128]); BH = [Buf() for _ in range(16)]
        carry = sb("carry", [128, 16]); Bcar = Buf()
        negv = sb("negv", [128, 2]); Bnegv = Buf()
        lamt = sb("lamt", [128, 256]); Blam = Buf()
        lamc = sb("lamc", [128, 8]); Blamc = Buf()
        subg = sb("subg_s", [128, 128]); Bsubg = Buf()
        hTw = sb("hTw", [128, TWP]); BhTw = Buf()
        hTa = sb("hTa", [128, TWP], BF16); BhTa = Buf()
        hTg = sb("hTg", [128, 2, TWP], BF16); BhTg = Buf()
        stat = [(sb("stat%d" % i, [128, TWP]), Buf()) for i in range(4)]
        tmpA = Rot([(sb("tmpA%d" % i, [128, TWP]), Buf()) for i in range(3)])
        tmpB = Rot([(sb("tmpB%d" % i, [128, TWP], BF16), Buf()) for i in range(3)])
        RW = {n: (sb("rw_" + n, [128, TWP]), Buf()) for n in
              ["a", "g", "pinc", "pexc", "pinv", "kkr", "rn", "kk", "kmod", "b", "bv", "yT", "yc"]}
        RW["sq"], RW["tmp"], RW["rk"], RW["rstd"] = stat[0], stat[1], stat[2], stat[3]
        logd = sb("logd", [128, 2, 128]); Blogd = Buf()
        NCHM = TWP // 64
        RKfp = [(sb("RKfp%d" % i, [128, NCHM, 2, 128]), Buf()) for i in range(1)]
        Ktfp = [(sb("Ktfp%d" % i, [128, NCHM, 128]), Buf()) for i in range(1)]
        Btfp = [(sb("Btfp%d" % i, [128, NCHM, 128]), Buf()) for i in range(1)]
        Vtfp = [(sb("Vtfp%d" % i, [128, NCHM, 128]), Buf()) for i in range(1)]
        cmL = [(sb("cmL%d" % i, [128, 256]), Buf()) for i in range(4)]
        cmX = Rot([(sb("cmX%d" % i, [128, 128]), Buf()) for i in range(3)])
        cmM = Rot([(sb("cmM%d" % i, [128, 256]), Buf()) for i in range(3)])
        ktp = Rot([(sb("ktp%d" % i, [128, 1024], BF16), Buf()) for i in range(2)])
        vtp_items = [(sb("vtp%d" % i, [128, 8, 129], BF16), Buf()) for i in range(2)]
        vtp = Rot(vtp_items)
        ptp = Rot([(sb("ptp%d" % i, [128, 512], BF16), Buf()) for i in range(3)])
        ckst = sb("ckst", [128, 8, 128]); Bckst = Buf()
        osm = Rot([(sb("osm%d" % i, [128, 136]), Buf()) for i in range(4)])

        ones_full = sb("ones_full", [128, 128]); Bones = Buf()
        mset("pool", ones_full[:], 1.0, [Bones])
        ident = cst[:, C_ID:C_ID + 128]
        bones = cst[:, C_BO:C_BO + 128]
        mSLn = cst[:, C_MSL:C_MSL + 128]
        mK = cst[:, C_MK:C_MK + 256]
        mBn = cst[:, C_MB:C_MB + 256]
        triI = cst[:, C_TI:C_TI + 64]
        triE = cst[:, C_TE:C_TE + 64]

        P.dma("sp", cst[:], consts_d, writes=[Bc])
        P.dma("sp", pc[:], pcols_d, writes=[Bpc])
        P.dma("sp", w2aug[:], w2aug_d, writes=[Bw2])
        P.dma("sp", tokst[:, 0, :], a2p_d, writes=Btk[0])
        cp("dve", a2b[:], tokst[:, 0, :], Btk[0], [Ba2])
        for kc in range(2):
            P.dma("sp", tokst[:, 1, :], g2p_d[:, kc, :], writes=Btk[1])
            cp("dve", g2b[:, kc, :], tokst[:, 1, :], Btk[1], [Bg2])
        ts("dve", omka[:], pc[:, PV_KA, :], -1.0, 1.0, ALU.mult, ALU.add, [Bpc], [Bomka])
        mset("pool", negv[:], NEG_E, [Bnegv])
        P.op("pool", lambda e: e.affine_select(negv[:, 1:2], negv[:, 1:2], pattern=[[0, 1]], compare_op=ALU.is_gt,
                                               fill=0.0, base=NSAMP, channel_multiplier=-1), [Bnegv], [Bnegv])
        for (t_, b_) in RKfp + Ktfp + Btfp + Vtfp:
            mset("pool", t_[:], 0.0, [b_])
        for (t_, b_) in vtp_items:
            mset("pool", t_[:], 1.0, [b_])
        P.dma("sp", lamt[:], dlam_d.partition_broadcast(128), writes=[Blam])
        P.dma("sp", subg[:], subg_d.partition_broadcast(128), writes=[Bsubg])
        tt("dve", lamt[:, 0:64], lamt[:, 0:64], lamt[:, 64:128], ALU.mult, [Blam], [Blam])
        tt("dve", lamt[:, 128:192], lamt[:, 128:192], lamt[:, 192:256], ALU.mult, [Blam], [Blam])
        P.op("dve", lambda e: e.reduce_sum(lamc[:, 0:1], lamt[:, 0:64], axis=AX.X), [Blam], [Blamc])
        P.op("dve", lambda e: e.reduce_sum(lamc[:, 1:2], lamt[:, 128:192], axis=AX.X), [Blam], [Blamc])
        act(lamc[:, 2:4], lamc[:, 0:2], AF.Exp, [Blamc], [Blamc])
        tt("dve", lamc[:, 4:5], lamc[:, 3:4], lamc[:, 2:3], ALU.subtract, [Blamc], [Blamc])
        ts("dve", lamc[:, 5:6], lamc[:, 4:5], -LAMBDA_INIT, None, ALU.add, None, [Blamc], [Blamc])
        ts("dve", subg[:], subg[:], 1.0 - LAMBDA_INIT, None, ALU.mult, None, [Bsubg], [Bsubg])
        neglam = lamc[:, 5:6]

        class WS:
            seq = []
            pos = 0
            issued = 0

        def wissue(i):
            kind, idx = WS.seq[i]
            t_, b_ = wslots[i % NS]
            if kind == "A":
                P.dma("sp", t_[:, 0:4096], wA[idx], reads=[B_wA[idx]], writes=[b_])
            else:
                P.dma("sp", t_[:, 0:BW], wB[idx], reads=[B_wB[idx]], writes=[b_])

        def wget(kind, idx):
            if P.dry:
                WS.seq.append((kind, idx))
                return wslots[0]
            assert WS.seq[WS.pos] == (kind, idx)
            while WS.issued < min(len(WS.seq), WS.pos + NS - 1):
                wissue(WS.issued)
                WS.issued += 1
            r = wslots[WS.pos % NS]
            WS.pos += 1
            return r

        def uA(slot):
            return slot[:, 0:4096].rearrange("p (c f) -> p c f", c=DC)

        def uB(slot):
            return slot[:, 0:BW].rearrange("p (m d) -> p m d", m=11)

        def proj_fm(unit_list, TW, evac):
            for blk, parts in enumerate(unit_list):
                slots = [(wget("A", ui), src, sbufs) for (ui, src, sbufs) in parts]
                for half in range(2):
                    bank, bb = G.next()
                    n = len(slots) * DC
                    k = 0
                    for (st, sbf), src, sbufs in slots:
                        u = uA(st)
                        for dc in range(DC):
                            mm(bank[:, 0:TW], u[:, dc, half * 128:(half + 1) * 128], src[:, dc, 0:TW],
                               k == 0, k == n - 1, [sbf, sbufs[dc]], [bb])
                            k += 1
                    evac(2 * blk + half, bank, bb)

        def layer_norm(li, TW):
            S, Sb = G.next()
            S2, S2b = G.next()
            for c in range(DC):
                sq, sqb = tmpA.next()
                act(sq[:, 0:TW], xres[:, c, 0:TW], AF.Square, [Bx[c]], [sqb])
                mm(S[:, 0:TW], ones_full[:], xres[:, c, 0:TW], c == 0, c == DC - 1, [Bones, Bx[c]], [Sb])
                mm(S2[:, 0:TW], ones_full[:], sq[:, 0:TW], c == 0, c == DC - 1, [Bones, sqb], [S2b])
            (m_, mb), (q_, qb_), (v_, vb), (r_, rb) = stat[0], stat[1], stat[2], stat[3]
            ts("dve", m_[:, 0:TW], S[:, 0:TW], 1.0 / D, None, ALU.mult, None, [Sb], [mb])
            tt("dve", q_[:, 0:TW], m_[:, 0:TW], m_[:, 0:TW], ALU.mult, [mb], [qb_])
            ts("dve", v_[:, 0:TW], S2[:, 0:TW], 1.0 / D, LN_EPS, ALU.mult, ALU.add, [S2b], [vb])
            tt("dve", v_[:, 0:TW], v_[:, 0:TW], q_[:, 0:TW], ALU.subtract, [vb, qb_], [vb])
            P.op("act", lambda e: e.sqrt(v_[:, 0:TW], v_[:, 0:TW]), [vb], [vb])
            recip(r_[:, 0:TW], v_[:, 0:TW], [vb], [rb])
            for c in range(DC):
                t1, t1b = tmpA.next()
                tt("dve", t1[:, 0:TW], xres[:, c, 0:TW], m_[:, 0:TW], ALU.subtract, [Bx[c], mb], [t1b])
                tt("pool", t1[:, 0:TW], t1[:, 0:TW], r_[:, 0:TW], ALU.mult, [t1b, rb], [t1b])
                act(xres[:, c, 0:TW], t1[:, 0:TW], AF.Identity, [t1b, Bpc], [Bx[c]],
                    scale=pc[:, PV_LNG + li, c:c + 1], bias=pc[:, PV_LNB + li, c:c + 1])
                ts("dve", xb[:, c, 0:TW], t1[:, 0:TW], pc[:, PV_LNG + li, c:c + 1], pc[:, PV_LNB + li, c:c + 1],
                   ALU.mult, ALU.add, [t1b, Bpc], [Bxb[c]])


        def ffn(fi, TW):
            for mp in range(22):
                sg, sgb = wget("A", UA_FFN + fi * 44 + mp)
                su, sub_ = wget("A", UA_FFN + fi * 44 + 22 + mp)
                ug, uu = uA(sg), uA(su)
                for half in range(2):
                    m = 2 * mp + half
                    bank, bb = G.next()
                    bank2, bb2 = G.next()
                    for dc in range(DC):
                        mm(bank[:, 0:TW], ug[:, dc, half * 128:(half + 1) * 128], xb[:, dc, 0:TW], dc == 0, dc == DC - 1, [sgb, Bxb[dc]], [bb])
                    for dc in range(DC):
                        mm(bank2[:, 0:TW], uu[:, dc, half * 128:(half + 1) * 128], xb[:, dc, 0:TW], dc == 0, dc == DC - 1, [sub_, Bxb[dc]], [bb2])
                    t1, t1b = tmpA.next()
                    act(t1[:, 0:TW], bank[:, 0:TW], AF.Silu, [bb], [t1b])
                    tt("dve", hid[:, m, 0:TW], t1[:, 0:TW], bank2[:, 0:TW], ALU.mult, [t1b, bb2], [Bh[m]])
            for g in range(8):
                (bA, bAb) = G.next()
                (bB, bBb) = G.next()
                accs = [(bA[:, 0:TW], bAb), (bB[:, 0:TW], bBb)]
                for q in range(4):
                    sw, swb = wget("B", fi * 32 + g * 4 + q)
                    u = uB(sw)
                    for oo in range(2):
                        for mm_ in range(11):
                            m = q * 11 + mm_
                            mm(accs[oo][0], u[:, mm_, oo * 128:(oo + 1) * 128], hid[:, m, 0:TW],
                               q == 0 and mm_ == 0, q == 3 and mm_ == 10, [swb, Bh[m]], [accs[oo][1]])
                for oo in range(2):
                    c = g * 2 + oo
                    stt("dve", xres[:, c, 0:TW], xres[:, c, 0:TW], ALPHA, accs[oo][0], ALU.mult, ALU.add, [Bx[c], accs[oo][1]], [Bx[c]])

        def resid_evac(TW):
            def f(c, bank, bb):
                stt("dve", xres[:, c, 0:TW], xres[:, c, 0:TW], ALPHA, bank[:, 0:TW], ALU.mult, ALU.add, [Bx[c], bb], [Bx[c]])
            return f

        def rwkv(TW, ntok, nvcol, shift_out):
            nch = TW // 64
            nb = (TW + 127) // 128
            ntp = min(TW, 128)
            tt("dve", sA[:, :, 1:TW], xres[:, :, 0:TW - 1], xres[:, :, 1:TW], ALU.subtract, Bx, BsA)
            tt("dve", sA[:, :, 0:1], carry[:, :].unsqueeze(2), xres[:, :, 0:1], ALU.subtract, Bx + [Bcar], BsA)
            cp("pool", carry[:, :].unsqueeze(2), xres[:, :, ntok - 1:ntok], Bx, [Bcar])
            if shift_out is not None:
                P.dma("pool", shift_out, carry[:], reads=[Bcar], is_output=True)
            def sr(n):
                if TW == TWP:
                    stage(n)
            sr(201)
            for j in range(3):
                ul = [[(UA_RKV + j * 8 + blk, xb, Bxb), (UA_RKVS + j * 8 + blk, sA, BsA)] for blk in range(8)]

                def ev(c, bank, bb, j=j):
                    eng = "act" if c % 2 == 0 else "dve"
                    cp(eng, hid[:, j * 16 + c, 0:TW], bank[:, 0:TW], [bb], [Bh[j * 16 + c]])
                proj_fm(ul, TW, ev)
            if ntok < TW:
                mset("pool", hid[:, 16:32, ntok:TW], 0.0, Bh[16:32])
            sr(202)
            for which in range(2):
                (ua_, uab), (ub_, ubb) = (wget("A", UA_L1), wget("A", UA_L1S)) if which == 0 else (wget("A", UA_G1), wget("A", UA_G1S))
                for half in range(2):
                    bank, bb = G.next()
                    k = 0
                    for (st, sbf, src, sbufs) in [(ua_, uab, xb, Bxb), (ub_, ubb, sA, BsA)]:
                        u = uA(st)
                        for dc in range(DC):
                            mm(bank[:, 0:TW], u[:, dc, half * 128:(half + 1) * 128], src[:, dc, 0:TW], k == 0, k == 2 * DC - 1, [sbf, sbufs[dc]], [bb])
                            k += 1
                    if which == 0 and half == 0:
                        act(hTw[:, 0:TW], bank[:, 0:TW], AF.Tanh, [bb], [BhTw])
                        mset("pool", hTw[0:32, 0:TW], 0.0, [BhTw])
                        mset("pool", hTw[0:1, 0:TW], 1.0, [BhTw])
                    elif which == 0:
                        cp("dve", hTa[:, 0:TW], bank[:, 0:TW], [bb], [BhTa])
                    else:
                        act(hTg[:, half, 0:TW], bank[:, 0:TW], AF.Sigmoid, [bb], [BhTg])
            sr(203)
            for c in range(16):
                if c == 1:
                    sr(207)
                if c == 2:
                    sr(208)
                bar()
                rT = hid[:, c, 0:TW]; kT = hid[:, 16 + c, 0:TW]; vT = hid[:, 32 + c, 0:TW]
                Br, Bk, Bv = Bh[c], Bh[16 + c], Bh[32 + c]
                cs = slice(c * 128, (c + 1) * 128)
                a_, ab = RW["a"]; g_, gb = RW["g"]
                bank, bb = G.next()
                mm(bank[:, 0:TW], a2b[:, cs], hTa[:, 0:TW], True, True, [Ba2, BhTa], [bb])
                for kc in range(2):
                    mm(bank[:, 256:256 + TW], g2b[:, kc, cs], hTg[:, kc, 0:TW], kc == 0, kc == 1, [Bg2, BhTg], [bb])
                act(a_[:, 0:TW], bank[:, 0:TW], AF.Sigmoid, [bb, Bpc], [ab], bias=pc[:, PV_A0, c:c + 1], scale=1.0)
                cp("act", g_[:, 0:TW], bank[:, 256:256 + TW], [bb], [gb])
                bank, bb = G.next()
                for tb in range(nb):
                    mm(bank[0:ntp, tb * 128:(tb + 1) * 128], hTw[:, tb * 128:tb * 128 + ntp], w2aug[:, cs], True, True, [BhTw, Bw2], [bb])
                for tb in range(nb):
                    act(logd[0:ntp, tb, :], bank[0:ntp, tb * 128:(tb + 1) * 128], AF.Sigmoid, [bb], [Blogd])
                    ts("dve", logd[0:ntp, tb, :], logd[0:ntp, tb, :], negv[0:ntp, nvcol:nvcol + 1], None, ALU.mult, None, [Blogd, Bnegv], [Blogd])
                bank, bb = G.next()
                for tb in range(nb):
                    mm(bank[:, tb * 128:tb * 128 + ntp], logd[0:ntp, tb, :], cst[0:ntp, C_MK + 128:C_MK + 128 + ntp], True, True, [Blogd, Bc], [bb])
                    mm(bank[:, 256 + tb * 128:256 + tb * 128 + ntp], logd[0:ntp, tb, :], cst[0:ntp, C_MK:C_MK + ntp], True, True, [Blogd, Bc], [bb])
                pinc, pincb = RW["pinc"]; pexc, pexcb = RW["pexc"]; pinv, pinvb = RW["pinv"]
                act(pinc[:, 0:TW], bank[:, 0:TW], AF.Exp, [bb], [pincb])
                act(pexc[:, 0:TW], bank[:, 256:256 + TW], AF.Exp, [bb], [pexcb])
                act(pinv[:, 0:TW], bank[:, 0:TW], AF.Exp, [bb], [pinvb], scale=-1.0)
                sr(204)
                kkr, kkrb = RW["kkr"]; sq, sqb = RW["sq"]; rn, rnb = RW["rn"]; kk, kkb = RW["kk"]
                ts("dve", kkr[:, 0:TW], kT, pc[:, PV_KK, c:c + 1], None, ALU.mult, None, [Bk, Bpc], [kkrb])
                act(sq[:, 0:TW], kkr[:, 0:TW], AF.Square, [kkrb], [sqb])
                bank, bb = G.next()
                mm(bank[:, 0:TW], bones, sq[:, 0:TW], True, True, [Bc, sqb], [bb])
                ts("dve", rn[:, 0:TW], bank[:, 0:TW], 1e-24, None, ALU.max, None, [bb], [rnb])
                P.op("act", lambda e, o=rn, w=TW: e.sqrt(o[:, 0:w], o[:, 0:w]), [rnb], [rnb])
                recip(rn[:, 0:TW], rn[:, 0:TW], [rnb], [rnb])
                tt("dve", kk[:, 0:TW], kkr[:, 0:TW], rn[:, 0:TW], ALU.mult, [kkrb, rnb], [kkb])
                tmp, tmpb = RW["tmp"]; kmod, kmodb = RW["kmod"]; b_, bbf = RW["b"]; rk, rkb = RW["rk"]; bv, bvb = RW["bv"]
                ts("pool", tmp[:, 0:TW], a_[:, 0:TW], pc[:, PV_KA, c:c + 1], omka[:, c:c + 1], ALU.mult, ALU.add, [ab, Bpc, Bomka], [tmpb])
                tt("pool", kmod[:, 0:TW], kT, tmp[:, 0:TW], ALU.mult, [Bk, tmpb], [kmodb])
                tt("pool", b_[:, 0:TW], kk[:, 0:TW], a_[:, 0:TW], ALU.mult, [kkb, ab], [bbf])
                stt("dve", rk[:, 0:TW], rT, pc[:, PV_RK, c:c + 1], kmod[:, 0:TW], ALU.mult, ALU.mult, [Br, Bpc, kmodb], [rkb])
                mm(bank[:, 256:256 + TW], bones, rk[:, 0:TW], True, True, [Bc, rkb], [bb])
                tt("dve", bv[:, 0:TW], bank[:, 256:256 + TW], vT, ALU.mult, [bb, Bv], [bvb])
                (RK, RKb), (Kt, Ktb), (Bt, Btb), (Vt, Vtb) = RKfp[0], Ktfp[0], Btfp[0], Vtfp[0]
                for h in range(2):
                    hs = slice(h * 64, h * 64 + 64)

                    def v3(ap):
                        return ap.rearrange("p (j t) -> p j t", t=64)
                    tt("dve", RK[hs, 0:nch, 0, hs], v3(kk[hs, 0:TW]), v3(pexc[hs, 0:TW]), ALU.mult, [kkb, pexcb], [RKb])
                    tt("pool", RK[hs, 0:nch, 1, hs], v3(hid[hs, c, 0:TW]), v3(pinc[hs, 0:TW]), ALU.mult, [Br, pincb], [RKb])
                    tt("dve", Kt[hs, 0:nch, hs], v3(kmod[hs, 0:TW]), v3(pinv[hs, 0:TW]), ALU.mult, [kmodb, pinvb], [Ktb])
                    tt("pool", Bt[hs, 0:nch, hs], v3(b_[hs, 0:TW]), v3(pinv[hs, 0:TW]), ALU.mult, [bbf, pinvb], [Btb])
                    cp("act", Vt[hs, 0:nch, hs], v3(hid[hs, 32 + c, 0:TW]), [Bv], [Vtb])
                sr(205)
                yT, yTb = RW["yT"]
                H0 = Hst[:, c, :]
                for j in range(nch):
                    bank, bb = G.next()
                    tr(bank[:, 0:128], Kt[:, j, :], ident, [Ktb, Bc], [bb])
                    tr(bank[:, 128:256], Bt[:, j, :], ident, [Btb, Bc], [bb])
                    tr(bank[:, 256:384], Vt[:, j, :], ident, [Vtb, Bc], [bb])
                    KB, KBb = cmL[0]; VM, Vbdb = cmL[1]; Vbd = VM
                    cp("act", KB[:, 0:128], bank[:, 0:128], [bb], [KBb])
                    P.op("act", lambda e, o=KB, i=bank: e.mul(o[:, 128:256], i[:, 128:256], -1.0), [bb], [KBb])
                    cp("act", Vbd[:, 0:128], bank[:, 256:384], [bb], [Vbdb])
                    bank, bb = G.next()
                    mm(bank[:, 0:128], RK[:, j, 0, :], Bt[:, j, :], True, True, [RKb, Btb], [bb])
                    M0, M0b = _Shift(VM), Vbdb
                    tt("dve", M0[:, 0:128], bank[:, 0:128], mSLn, ALU.mult, [bb, Bc], [M0b])
                    bank, bb = G.next()
                    mm(bank[:, 0:256], Kt[:, j, :], RK[:, j, :, :].rearrange("p a b -> p (a b)"), True, True, [Ktb, RKb], [bb])
                    GK, GKb = cmL[2]
                    tt("dve", GK[:, 0:256], bank[:, 0:256], mK, ALU.mult, [bb, Bc], [GKb])
                    bank, bb = G.next()
                    mm(bank[:, 0:256], Bt[:, j, :], RK[:, j, :, :].rearrange("p a b -> p (a b)"), True, True, [Btb, RKb], [bb])
                    GB, GBb = cmL[3]
                    tt("dve", GB[:, 0:256], bank[:, 0:256], mBn, ALU.mult, [bb, Bc], [GBb])
                    bank, bb = G.next()
                    mm(bank[:, 0:128], RK[:, j, 0, :], H0, True, False, [RKb, BH[c]], [bb])
                    mm(bank[:, 0:128], GK[:, 0:128], Vbd[:, 0:128], False, True, [GKb, Vbdb], [bb])
                    X, Xb = cmX.next()
                    cp("act", X[:, 0:128], bank[:, 0:128], [bb], [Xb])
                    Mc, Mcb = M0, M0b
                    McT, McTb = GB, GBb
                    for lvl in range(6):
                        bank, bb = G.next()
                        mm(bank[:, 0:128], McT[:, 0:128], X[:, 0:128], True, True, [McTb, Xb], [bb])
                        if lvl < 5:
                            mm(bank[:, 128:256], Mc[:, 0:128], McT[:, 0:128], True, True, [Mcb, McTb], [bb])
                            if lvl < 4:
                                mm(bank[:, 256:384], McT[:, 0:128], Mc[:, 0:128], True, True, [Mcb, McTb], [bb])
                        Xn, Xnb = cmX.next()
                        tt("dve", Xn[:, 0:128], bank[:, 0:128], X[:, 0:128], ALU.add, [bb, Xb], [Xnb])
                        X, Xb = Xn, Xnb
                        if lvl < 5:
                            Mn, Mnb = cmM.next()
                            cp("dve", Mn[:, 0:128], bank[:, 128:256], [bb], [Mnb])
                            if lvl < 4:
                                cp("dve", Mn[:, 128:256], bank[:, 256:384], [bb], [Mnb])
                            McT, McTb = Mn, Mnb
                            Mc, Mcb = _Shift(Mn), Mnb
                    U, Ub = X, Xb
                    bank, bb = G.next()
                    mm(bank[:, 0:128], H0, RK[:, j, 1, :], True, False, [BH[c], RKb], [bb])
                    mm(bank[:, 0:128], Vbd[:, 0:128], GK[:, 128:256], False, False, [Vbdb, GKb], [bb])
                    mm(bank[:, 0:128], U[:, 0:128], GB[:, 128:256], False, True, [Ub, GBb], [bb])
                    mm(bank[:, 128:256], KB[:, 0:128], Vbd[:, 0:128], True, False, [KBb, Vbdb], [bb])
                    mm(bank[:, 128:256], KB[:, 128:256], U[:, 0:128], False, False, [KBb, Ub], [bb])
                    mm(bank[:, 128:256], ident, H0, False, True, [Bc, BH[c]], [bb])
                    cp("act", yT[0:64, j * 64:(j + 1) * 64], bank[0:64, 0:64], [bb], [yTb])
                    cp("act", yT[64:128, j * 64:(j + 1) * 64], bank[64:128, 64:128], [bb], [yTb])
                    act(H0, bank[:, 128:256], AF.Identity, [bb, pincb], [BH[c]], scale=pinc[:, j * 64 + 63:j * 64 + 64])
                sr(206)
                yc, ycb = RW["yc"]; rstd, rstdb = RW["rstd"]
                bank, bb = G.next()
                mm(bank[:, 0:TW], bones, yT[:, 0:TW], True, True, [Bc, yTb], [bb])
                stt("dve", yc[:, 0:TW], bank[:, 0:TW], -1.0 / 64, yT[:, 0:TW], ALU.mult, ALU.add, [bb, yTb], [ycb])
                act(sq[:, 0:TW], yc[:, 0:TW], AF.Square, [ycb], [sqb])
                mm(bank[:, 256:256 + TW], bones, sq[:, 0:TW], True, True, [Bc, sqb], [bb])
                ts("dve", rstd[:, 0:TW], bank[:, 256:256 + TW], 1.0 / 64, GN_EPS, ALU.mult, ALU.add, [bb], [rstdb])
                P.op("act", lambda e, o=rstd, w=TW: e.sqrt(o[:, 0:w], o[:, 0:w]), [rstdb], [rstdb])
                recip(rstd[:, 0:TW], rstd[:, 0:TW], [rstdb], [rstdb])
                tt("dve", yc[:, 0:TW], yc[:, 0:TW], rstd[:, 0:TW], ALU.mult, [ycb, rstdb], [ycb])
                ts("pool", yc[:, 0:TW], yc[:, 0:TW], pc[:, PV_LXG, c:c + 1], pc[:, PV_LXB, c:c + 1], ALU.mult, ALU.add, [ycb, Bpc], [ycb])
                tt("pool", yc[:, 0:TW], yc[:, 0:TW], bv[:, 0:TW], ALU.add, [ycb, bvb], [ycb])
                tt("dve", sB[:, c, 0:TW], yc[:, 0:TW], g_[:, 0:TW], ALU.mult, [ycb, gb], [BsB[c]])
            proj_fm([[(UA_WO + blk, sB, BsB)] for blk in range(8)], TW, resid_evac(TW))

        class _Shift:
            def __init__(self, base):
                self.base = base

            def __getitem__(self, idx):
                assert idx == (slice(None), slice(0, 128))
                return self.base[:, 128:256]

        def attention(TW, prompt, t0):
            nqb = (TW + 127) // 128
            qw = min(TW, 128)
            QT, BQ = sA, BsA
            if prompt:
                nkb = (t0 + TW) // 128
                groups = [("scr", n0, min(8, nkb - n0), 128) for n0 in range(0, nkb, 8)]
            else:
                groups = [("cache", 0, 8, 128), ("own", 0, 1, NSAMP)]

            def valid(qb, n):
                return (not prompt) or n <= (t0 // 128 + qb)
            total = {}
            for qb in range(nqb):
                total[qb] = sum(1 for (_, n0, nblk, _) in groups for n in range(n0, n0 + nblk) if valid(qb, n))
            for c in range(16):
                bar()
                cs = slice(c * 128, (c + 1) * 128)
                done = {(qb, h): 0 for qb in range(nqb) for h in range(2)}
                for (kind, n0, nblk, nk) in groups:
                    KT, KTb = ktp.next(); V, Vb = vtp.next()
                    if kind == "scr":
                        P.dma("pool", KT[:, 0:nblk * 128], KTscr[cs, n0 * 128:(n0 + nblk) * 128], reads=[B_KTscr], writes=[KTb])
                        P.dma("pool", V[:, 0:nblk, 0:128], Vscr[n0 * 128:(n0 + nblk) * 128, cs].rearrange("(n p) e -> p n e", p=128), reads=[B_Vscr], writes=[Vb])
                    elif kind == "cache":
                        P.dma("pool", ckst[:], ck[:, cs].rearrange("(n p) e -> p n e", p=128), writes=[Bckst])
                        for gq in range(2):
                            bank, bb = G.next()
                            for i in range(4):
                                tr(bank[:, i * 128:(i + 1) * 128], ckst[:, gq * 4 + i, :], ident, [Bckst, Bc], [bb])
                            cp("act" if gq == 0 else "dve", KT[:, gq * 512:(gq + 1) * 512], bank[:, 0:512], [bb], [KTb])
                        P.dma("pool", V[:, 0:8, 0:128], cv[:, cs].rearrange("(n p) e -> p n e", p=128), writes=[Vb])
                    else:
                        P.dma("pool", KT[:, 0:64], KTs[cs, :], reads=[B_KTs], writes=[KTb])
                        P.dma("pool", V[0:64, 0, 0:128], Vs[:, cs], reads=[B_Vs], writes=[Vb])
                    for qb in range(nqb):
                        vblocks = [n for n in range(n0, n0 + nblk) if valid(qb, n)]
                        if not vblocks:
                            continue
                        for h in range(2):
                            Ob, Obb = Obanks[qb * 2 + h]
                            hs = slice(h * 64, h * 64 + 64)
                            for b0 in range(0, len(vblocks), 4):
                                batch = vblocks[b0:b0 + 4]
                                bank, bb = G.next()
                                for i, n in enumerate(batch):
                                    nl = n - n0
                                    mm(bank[0:nk, i * 128:i * 128 + qw], KT[hs, nl * 128:nl * 128 + nk], QT[hs, c, qb * 128:qb * 128 + qw],
                                       True, True, [KTb, BQ[c]], [bb])
                                PT, PTb = ptp.next()
                                if qw == 128 and nk == 128:
                                    act(PT[:, 0:len(batch) * 128], bank[:, 0:len(batch) * 128], AF.Exp, [bb], [PTb], scale=0.125)
                                else:
                                    for i in range(len(batch)):
                                        act(PT[0:nk, i * 128:i * 128 + qw], bank[0:nk, i * 128:i * 128 + qw], AF.Exp, [bb], [PTb], scale=0.125)
                                for i, n in enumerate(batch):
                                    if prompt and n == t0 // 128 + qb:
                                        mset("pool", PT[64:128, i * 128:i * 128 + 64], 0.0, [PTb])
                                for i, n in enumerate(batch):
                                    nl = n - n0
                                    d_ = done[(qb, h)]
                                    mm(Ob[0:qw, 0:129], PT[0:nk, i * 128:i * 128 + qw], V[0:nk, nl, 0:129],
                                       d_ == 0, d_ == total[qb] - 1, [PTb, Vb], [Obb])
                                    done[(qb, h)] = d_ + 1
                for qb in range(nqb):
                    Ob, Obb = Obanks[qb * 2]
                    Ob2, Ob2b = Obanks[qb * 2 + 1]
                    o_, ob_ = osm.next()
                    sm, smb = osm.next()
                    recip(sm[0:qw, 0:1], Ob[0:qw, 128:129], [Obb], [smb])
                    recip(sm[0:qw, 1:2], Ob2[0:qw, 128:129], [Ob2b], [smb])
                    tt("dve", sm[0:qw, 2:3], sm[0:qw, 1:2], neglam[0:qw, :], ALU.mult, [smb, Blamc], [smb])
                    ts("dve", o_[0:qw, 0:128], Ob[0:qw, 0:128], sm[0:qw, 0:1], None, ALU.mult, None, [Obb, smb], [ob_])
                    stt("dve", o_[0:qw, 0:128], Ob2[0:qw, 0:128], sm[0:qw, 2:3], o_[0:qw, 0:128], ALU.mult, ALU.add, [Ob2b, smb, ob_], [ob_])
                    sq2, sq2b = osm.next()
                    tt("dve", sq2[0:qw, 0:128], o_[0:qw, 0:128], o_[0:qw, 0:128], ALU.mult, [ob_], [sq2b])
                    P.op("dve", lambda e, s_=sm, q_=sq2, w=qw: e.reduce_sum(s_[0:w, 3:4], q_[0:w, 0:128], axis=AX.X), [sq2b], [smb])
                    ts("dve", sm[0:qw, 4:5], sm[0:qw, 3:4], 1.0 / 128, LN_EPS, ALU.mult, ALU.add, [smb], [smb])
                    P.op("act", lambda e, s_=sm, w=qw: e.sqrt(s_[0:w, 4:5], s_[0:w, 4:5]), [smb], [smb])
                    recip(sm[0:qw, 5:6], sm[0:qw, 4:5], [smb], [smb])
                    stt("dve", tokst[0:qw, qb, cs], o_[0:qw, 0:128], sm[0:qw, 5:6], subg[0:qw, :], ALU.mult, ALU.mult, [ob_, smb, Bsubg], [Btk[qb][c]])
            for qb in range(nqb):
                for g4 in range(4):
                    bank, bb = G.next()
                    for i in range(4):
                        c = g4 * 4 + i
                        tr(bank[:, i * 128:(i + 1) * 128], tokst[:, qb, c * 128:(c + 1) * 128], ident, [Btk[qb][c], Bc], [bb])
                    for i in range(4):
                        c = g4 * 4 + i
                        cp("act" if g4 % 2 == 0 else "dve", sB[:, c, qb * 128:qb * 128 + qw], bank[:, i * 128:i * 128 + qw], [bb], [BsB[c]])

        def tile_pass(prompt, it):
            TW = TWP if prompt else 64
            ntok = TW if prompt else NSAMP
            nb = (TW + 127) // 128
            ntp = min(TW, 128)
            t0 = it * TWP

            def st_(n):
                stage(n + (100 if prompt else 0))
                bar()
            for b in range(nb):
                if prompt:
                    P.dma("pool", tokst[:, b, :], xp[t0 + b * 128:t0 + (b + 1) * 128, :], writes=Btk[b])
                else:
                    mset("pool", tokst[:, 0, :], 0.0, Btk[0])
                    P.dma("sp", tokst[0:NSAMP, 0, :], xs, writes=Btk[0])
                for g4 in range(4):
                    bank, bb = G.next()
                    for i in range(4):
                        c = g4 * 4 + i
                        tr(bank[:, i * 128:(i + 1) * 128], tokst[:, b, c * 128:(c + 1) * 128], ident, [Btk[b][c], Bc], [bb])
                    for i in range(4):
                        c = g4 * 4 + i
                        cp("dve", xres[:, c, b * 128:b * 128 + ntp], bank[:, i * 128:i * 128 + ntp], [bb], [Bx[c]])
                        cp("act", xb[:, c, b * 128:b * 128 + ntp], xres[:, c, b * 128:b * 128 + ntp], [Bx[c]], [Bxb[c]])
            st_(2)
            ffn(0, TW); layer_norm(0, TW)
            st_(3)
            rwkv(TW, ntok, 0 if prompt else 1, (shp if it == NT - 1 else None) if prompt else shs)
            st_(4)
            layer_norm(1, TW)
            ffn(1, TW); layer_norm(2, TW)
            st_(5)
            for blk in range(16):
                su, sub_ = wget("A", UA_KV + blk)
                u = uA(su)
                if blk < 8:
                    for half in range(2):
                        c = 2 * blk + half
                        bank, bb = G.next()
                        for dc in range(DC):
                            mm(bank[:, 0:TW], u[:, dc, half * 128:(half + 1) * 128], xb[:, dc, 0:TW], dc == 0, dc == DC - 1, [sub_, Bxb[dc]], [bb])
                        cp("act", sA[:, c, 0:TW], bank[:, 0:TW], [bb], [BsA[c]])
                for b in range(nb):
                    bank, bb = G.next()
                    for dc in range(DC):
                        mm(bank[0:ntp, 0:256], xb[:, dc, b * 128:b * 128 + ntp], u[:, dc, :], dc == 0, dc == DC - 1, [sub_, Bxb[dc]], [bb])
                    t1, t1b = tmpA.next()
                    cp("dve", t1[0:ntp, 0:256], bank[0:ntp, 0:256], [bb], [t1b])
                    col = (blk % 8) * 256
                    if prompt:
                        dst = (kp if blk < 8 else vp)[t0 + b * 128:t0 + (b + 1) * 128, col:col + 256]
                        P.dma("pool", dst, t1[:, 0:256], reads=[t1b], is_output=True)
                    else:
                        dst = (ksm if blk < 8 else vsm)[:, col:col + 256]
                        P.dma("pool", dst, t1[0:NSAMP, 0:256], reads=[t1b], is_output=True)
                    if blk >= 8:
                        t2, t2b = tmpB.next()
                        cp("act", t2[0:ntp, 0:256], t1[0:ntp, 0:256], [t1b], [t2b])
                        if prompt:
                            P.dma("pool", Vscr[t0 + b * 128:t0 + (b + 1) * 128, col:col + 256], t2[:, 0:256], reads=[t2b], writes=[B_Vscr])
                        else:
                            P.dma("pool", Vs[:, col:col + 256], t2[0:64, 0:256], reads=[t2b], writes=[B_Vs])
            if prompt:
                P.dma("pool", KTscr[:, t0:t0 + TW].rearrange("(c p) t -> p c t", p=128), sA[:, :, 0:TW], reads=BsA, writes=[B_KTscr])
            else:
                P.dma("pool", KTs.rearrange("(c p) t -> p c t", p=128), sA[:, :, 0:64], reads=BsA, writes=[B_KTs])
            st_(6)
            ffn(2, TW); layer_norm(3, TW)

            def evq(c, bank, bb):
                cp("act" if c % 2 == 0 else "dve", sA[:, c, 0:TW], bank[:, 0:TW], [bb], [BsA[c]])
            proj_fm([[(UA_Q + blk, xb, Bxb)] for blk in range(8)], TW, evq)
            st_(7)
            attention(TW, prompt, t0)
            st_(8)
            proj_fm([[(UA_DWO + blk, sB, BsB)] for blk in range(8)], TW, resid_evac(TW))
            layer_norm(4, TW)
            ffn(3, TW); layer_norm(5, TW)
            for b in range(nb):
                for g4 in range(4):
                    bank, bb = G.next()
                    for i in range(4):
                        c = g4 * 4 + i
                        tr(bank[0:ntp, i * 128:(i + 1) * 128], xres[:, c, b * 128:b * 128 + ntp], ident, [Bx[c], Bc], [bb])
                    cp("act" if g4 % 2 == 0 else "dve", tokst[0:ntp, b, g4 * 512:(g4 + 1) * 512], bank[0:ntp, 0:512], [bb], Btk[b][g4 * 4:g4 * 4 + 4])
                if prompt:
                    P.dma("pool", yp[t0 + b * 128:t0 + (b + 1) * 128, :], tokst[:, b, :], reads=Btk[b], is_output=True)
                else:
                    P.dma("pool", ys, tokst[0:NSAMP, 0, :], reads=Btk[0], is_output=True)

        def state_in():
            mset("pool", tokst[:, 0, :], 0.0, Btk[0])
            tk = tokst[:, 0, :].rearrange("p (c f) -> p c f", c=16)
            for h in range(2):
                hs = slice(h * 64, h * 64 + 64)
                P.dma("pool", tk[hs, :, h * 64:h * 64 + 64], swkv[hs, :, :], writes=Btk[0])
            for c in range(16):
                bank, bb = G.next()
                tr(bank[:, 0:128], tk[:, c, :], ident, Btk[0] + [Bc], [bb])
                cp("act" if c % 2 == 0 else "dve", Hst[:, c, :], bank[:, 0:128], [bb], [BH[c]])
            P.dma("pool", carry[:], sshift, writes=[Bcar])

        def state_out(dst):
            tk = tokst[:, 1, :].rearrange("p (c f) -> p c f", c=16)
            for c in range(16):
                bank, bb = G.next()
                tr(bank[:, 0:128], Hst[:, c, :], ident, [BH[c], Bc], [bb])
                cp("act" if c % 2 == 0 else "dve", tk[:, c, :], bank[:, 0:128], [bb], Btk[1])
            for h in range(2):
                hs = slice(h * 64, h * 64 + 64)
                P.dma("pool", dst[hs, :, :], tk[hs, :, h * 64:h * 64 + 64], reads=Btk[1], is_output=True)

        def main_all():
            stage(1)
            state_in()
            stage(11)
            tile_pass(False, 0)
            stage(9)
            state_out(wkvs)
            stage(10)
            for c in range(16):
                mset("pool", Hst[:, c, :], 0.0, [BH[c]])
            mset("pool", carry[:], 0.0, [Bcar])
            for it in range(NT):
                tile_pass(True, it)
            state_out(wkvp)

        keep = P.dry
        P.dry = True
        main_all()
        P.dry = keep
        REAL[0] = True
        main_all()
        P.dry = False
        print('OPCOUNTS', P.cnt, flush=True)
        P.emit()
    return nc


def _unitsA(W):
    n = W.shape[1] // 256
    return np.ascontiguousarray(W.reshape(16, 128, n, 256).transpose(2, 1, 0, 3)).reshape(n, 128, 4096)


def _unitsB(W):
    return np.ascontiguousarray(W.reshape(4, 11, 128, 8, 256).transpose(3, 0, 2, 1, 4)).reshape(32, 128, BW)


def _col(v):
    return np.ascontiguousarray(v.reshape(16, 128).T)


def _consts():
    c = np.zeros((128, C_W), np.float32)
    c[:, C_ID:C_ID + 128] = np.eye(128, dtype=np.float32)
    blk = np.zeros((128, 128), np.float32)
    blk[:64, :64] = 1; blk[64:, 64:] = 1
    c[:, C_BO:C_BO + 128] = blk
    i = np.arange(128)
    same = (i[:, None] // 64) == (i[None, :] // 64)
    r, cc = i[:, None] % 64, i[None, :] % 64
    SL = (same & (cc < r)).astype(np.float32)
    SU = (same & (r < cc)).astype(np.float32)
    UI = (same & (r <= cc)).astype(np.float32)
    c[:, C_MSL:C_MSL + 128] = -SL
    c[:, C_MK:C_MK + 128] = SU; c[:, C_MK + 128:C_MK + 256] = UI
    c[:, C_MB:C_MB + 128] = -SU; c[:, C_MB + 128:C_MB + 256] = -UI
    j = np.arange(64)
    c[:, C_TI:C_TI + 64] = np.tile((j[:, None] <= j[None, :]).astype(np.float32), (2, 1))
    c[:, C_TE:C_TE + 64] = np.tile((j[:, None] < j[None, :]).astype(np.float32), (2, 1))
    return c


def _shared_inputs(inp):
    f = lambda a: np.asarray(a, np.float32)
    pv = np.zeros((128, NPV, 16), np.float32)
    ln_g, ln_b = f(inp["ln_g"]).reshape(6, D), f(inp["ln_b"]).reshape(6, D)
    for i in range(6):
        pv[:, PV_LNG + i] = _col(ln_g[i]); pv[:, PV_LNB + i] = _col(ln_b[i])
    pv[:, PV_A0] = _col(f(inp["rwkv_a0"])[0]); pv[:, PV_KK] = _col(f(inp["rwkv_k_k"])[0])
    pv[:, PV_KA] = _col(f(inp["rwkv_k_a"])[0]); pv[:, PV_RK] = _col(f(inp["rwkv_r_k"])[0].reshape(D))
    pv[:, PV_LXG] = _col(f(inp["rwkv_lnx_g"])[0]); pv[:, PV_LXB] = _col(f(inp["rwkv_lnx_b"])[0])
    mu = f(inp["rwkv_mu"])[0]
    for j in range(6):
        pv[:, PV_MU + j] = _col(mu[j])
    wA = np.empty((NSA, 128, 4096), np.float32)
    w_in = f(inp["ffn_w_in"]); w_out = f(inp["ffn_w_out"])
    for i in range(2):
        for s in range(2):
            wA[SA_FFN + (i * 2 + s) * 44: SA_FFN + (i * 2 + s + 1) * 44] = _unitsA(w_in[i, s])
    rkvw = f(inp["rwkv_w_rkv"])[0]
    for j in range(3):
        wA[SA_RKV + j * 8: SA_RKV + (j + 1) * 8] = _unitsA(rkvw[j])
    L1 = np.zeros((D, 256), np.float32)
    L1[:, 32:128] = f(inp["rwkv_w1"])[0]; L1[:, 128:224] = f(inp["rwkv_a1"])[0]
    wA[SA_L1:SA_L1 + 1] = _unitsA(L1)
    wA[SA_G1:SA_G1 + 1] = _unitsA(f(inp["rwkv_g1"])[0])
    wA[SA_WO:SA_WO + 8] = _unitsA(f(inp["rwkv_w_o"])[0])
    wA[SA_KV:SA_KV + 16] = _unitsA(f(inp["kv_w"]))
    wA[SA_Q:SA_Q + 8] = _unitsA(f(inp["diff_w_q"])[0])
    wA[SA_DWO:SA_DWO + 8] = _unitsA(f(inp["diff_w_o"])[0])
    wB = np.empty((NUB, 128, BW), np.float32)
    for i in range(2):
        for s in range(2):
            wB[(i * 2 + s) * 32:(i * 2 + s + 1) * 32] = _unitsB(w_out[i, s])
    w2aug = np.zeros((128, D), np.float32)
    w2aug[0] = f(inp["rwkv_w0"])[0]; w2aug[32:128] = f(inp["rwkv_w2"])[0]
    a2p = np.zeros((128, D), np.float32); a2p[0:96] = f(inp["rwkv_a2"])[0]
    g2p = np.ascontiguousarray(f(inp["rwkv_g2"])[0].reshape(2, 128, D).transpose(1, 0, 2))
    return {"pcols": pv, "wA_src": wA, "wB_src": wB, "w2aug": w2aug, "a2p": a2p, "g2p": g2p, "consts": _consts(),
            "dlam": f(inp["diff_lambda"]).reshape(1, 256), "subg": f(inp["diff_subln_g"]).reshape(1, 128)}


def _state_layout(S):
    return np.ascontiguousarray(S.reshape(16, 2, 64, 64).transpose(1, 2, 0, 3)).reshape(128, 16, 64)


def _state_unlayout(A):
    return np.ascontiguousarray(A.reshape(2, 64, 16, 64).transpose(2, 0, 1, 3)).reshape(32, 64, 64)


_NC_CACHE = {}


def kernel(**inp):
    f = lambda a: np.asarray(a, np.float32)
    x_prompt = f(inp["x_prompt"]); x_sample = f(inp["x_sample"])
    NB, SEQ = x_prompt.shape[0], x_prompt.shape[1]
    shared = _shared_inputs(inp)
    ck = f(inp["cache_k"]).reshape(NB, PAST, D); cv = f(inp["cache_v"]).reshape(NB, PAST, D)
    swkv = f(inp["state_wkv"])[0]; sshift = f(inp["state_shift"])[0]
    in_maps = []
    for i in range(NB):
        m = dict(shared)
        m.update({"xp": x_prompt[i], "xs": x_sample[i], "ck": ck[i], "cv": cv[i],
                  "swkv": _state_layout(swkv[i]), "sshift": _col(sshift[i])})
        in_maps.append(m)
    if SEQ not in _NC_CACHE:
        _NC_CACHE[SEQ] = build(SEQ)
    nc = _NC_CACHE[SEQ]
    res = run_bass_kernel_spmd(nc, in_maps, core_ids=list(range(NB)))
    R = res.results
    g = lambda k: np.stack([np.asarray(R[i][k], np.float32) for i in range(NB)])
    y_prompt = g("yp"); y_sample = g("ys")
    k_prompt = g("kp").reshape(NB, SEQ, 32, 64); v_prompt = g("vp").reshape(NB, SEQ, 16, 128)
    k_sample = g("ksm").reshape(NB, NSAMP, 32, 64); v_sample = g("vsm").reshape(NB, NSAMP, 16, 128)
    wkv_prompt = np.stack([_state_unlayout(np.asarray(R[i]["wkvp"], np.float32)) for i in range(NB)])[None]
    wkv_sample = np.stack([_state_unlayout(np.asarray(R[i]["wkvs"], np.float32)) for i in range(NB)])[None]
    unc = lambda a: np.ascontiguousarray(a.T).reshape(D)
    shift_prompt = np.stack([unc(np.asarray(R[i]["shp"], np.float32)) for i in range(NB)])[None]
    shift_sample = np.stack([unc(np.asarray(R[i]["shs"], np.float32)) for i in range(NB)])[None]
    return (y_prompt, y_sample, k_prompt, v_prompt, wkv_prompt, shift_prompt,
            k_sample, v_sample, wkv_sample, shift_sample)
```

```python
import math
from contextlib import ExitStack
import numpy as np
import concourse.bass as bass
import concourse.mybir as mybir
from concourse.bass_utils import run_bass_kernel_spmd

F32 = mybir.dt.float32
BF16 = mybir.dt.bfloat16
ALU = mybir.AluOpType
AF = mybir.ActivationFunctionType
AX = mybir.AxisListType

ENGS = ("pe", "act", "dve", "pool", "sp")
N_DMA_SEMS = 48
DMA_HALF = 24

D = 2048
DC = 16
FF = 5632
MC = 44
TWP = 256
PAST = 1024
NSAMP = 16
ALPHA = (2.0 * 2) ** 0.25
LN_EPS = 1e-5
GN_EPS = 64e-5
LAMBDA_INIT = 0.8 - 0.6 * math.exp(-0.3 * 1)
NEG_E = -math.exp(-0.5)

PV_LNG, PV_LNB = 0, 6
PV_A0, PV_KK, PV_KA, PV_RK, PV_LXG, PV_LXB, PV_MU = 12, 13, 14, 15, 16, 17, 18
NPV = 24
C_ID, C_BO, C_MSL, C_MK, C_MB, C_TI, C_TE, C_W = 0, 128, 256, 384, 640, 896, 960, 1024

UA_FFN = 0
UA_RKV = 176
UA_RKVS = 200
UA_L1, UA_G1, UA_L1S, UA_G1S = 224, 225, 226, 227
UA_WO, UA_KV, UA_Q, UA_DWO = 228, 236, 252, 260
NUA = 268
SA_FFN, SA_RKV, SA_L1, SA_G1, SA_WO, SA_KV, SA_Q, SA_DWO, NSA = 0, 176, 200, 201, 202, 210, 226, 234, 242
NUB = 128
BW = 2816


class Buf:
    __slots__ = ("name", "lw", "rd")

    def __init__(self, name=""):
        self.name = name
        self.lw = None
        self.rd = {}


class Prog:
    def __init__(self, nc):
        self.nc = nc
        self.streams = {e: [] for e in ENGS}
        self.cnt = {e: 0 for e in ENGS}
        self.known = {e: {} for e in ENGS}
        self.sem = {}
        for e in ("pe", "act", "dve", "pool"):
            self.sem[e] = nc.alloc_semaphore(name="c_" + e)
        self.dsem = [nc.alloc_semaphore(name="d_%d" % i) for i in range(N_DMA_SEMS)]
        self.dval = [0] * N_DMA_SEMS
        self.drr = {"sp": 0, "pool": 0}
        self.out_events = {}
        self.n_ops = 0
        self.dry = False

    def _deps(self, eng, reads, writes):
        deps = {}
        for b in list(reads) + list(writes):
            if b.lw is not None:
                k, v = b.lw
                if deps.get(k, 0) < v:
                    deps[k] = v
        for b in writes:
            for k, v in b.rd.items():
                if deps.get(k, 0) < v:
                    deps[k] = v
        waits = []
        kn = self.known[eng]
        for k, v in deps.items():
            if eng == "pe" and k == "pe":
                continue
            if kn.get(k, 0) >= v:
                continue
            kn[k] = v
            waits.append((k, v))
        return waits

    def _commit(self, ev, reads, writes):
        k, v = ev
        for b in reads:
            if b.rd.get(k, 0) < v:
                b.rd[k] = v
        for b in writes:
            b.lw = ev
            b.rd = {}

    def _semh(self, k):
        return self.sem[k] if isinstance(k, str) else self.dsem[k]

    def op(self, eng, fn, reads=(), writes=()):
        if self.dry:
            return
        waits = self._deps(eng, reads, writes)
        self.cnt[eng] += 1
        ev = (eng, self.cnt[eng])
        self.streams[eng].append((waits, fn, (eng, 1)))
        self._commit(ev, reads, writes)
        self.n_ops += 1

    def dma(self, q, out_ap, in_ap, reads=(), writes=(), is_output=False, **kw):
        if self.dry:
            return
        i = self.drr[q] + (0 if q == "sp" else DMA_HALF)
        self.drr[q] = (self.drr[q] + 1) % DMA_HALF
        waits = self._deps(q, reads, writes)
        kn = self.known[q]
        if self.dval[i] > 0 and kn.get(i, 0) < self.dval[i]:
            kn[i] = self.dval[i]
            waits.append((i, self.dval[i]))
        self.dval[i] += 16
        ev = (i, self.dval[i])

        def fn(e, out_ap=out_ap, in_ap=in_ap, kw=kw):
            return e.dma_start(out=out_ap, in_=in_ap, **kw)
        self.streams[q].append((waits, fn, (i, 16)))
        self._commit(ev, reads, writes)
        if is_output:
            self.out_events[i] = self.dval[i]
        self.n_ops += 1

    def barrier(self):
        for e in ENGS:
            waits = []
            kn = self.known[e]
            for k in ("pe", "act", "dve", "pool"):
                if k != e and kn.get(k, 0) < self.cnt[k]:
                    kn[k] = self.cnt[k]
                    waits.append((k, self.cnt[k]))
            for i in range(N_DMA_SEMS):
                if kn.get(i, 0) < self.dval[i]:
                    kn[i] = self.dval[i]
                    waits.append((i, self.dval[i]))
            self.streams[e].append((waits, None, None))

    def finish(self):
        waits = []
        for i, v in self.out_events.items():
            if self.known["sp"].get(i, 0) < v:
                waits.append((i, v))
        self.streams["sp"].append((waits, None, None))

    def _replay(self, eng, e):
        for waits, fn, inc in self.streams[eng]:
            for k, v in waits:
                e.wait_ge(self._semh(k), v)
            if fn is not None:
                ins = fn(e)
                ins.then_inc(self._semh(inc[0]), inc[1])

    def emit(self):
        self.finish()
        nc = self.nc
        with nc.Block() as block:
            @block.tensor
            def _(e):
                self._replay("pe", e)

            @block.scalar
            def _(e):
                self._replay("act", e)

            @block.vector
            def _(e):
                self._replay("dve", e)

            @block.gpsimd
            def _(e):
                self._replay("pool", e)

            @block.sync
            def _(e):
                self._replay("sp", e)


class Rot:
    def __init__(self, items):
        self.items = items
        self.i = 0

    def next(self):
        it = self.items[self.i % len(self.items)]
        self.i += 1
        return it


def build(SEQ):
    import os
    STOP = int(os.environ.get('MK_STOP', '99'))
    REAL = [False]
    NT = SEQ // TWP
    nc = bass.Bass("TRN2", target_bir_lowering=False)

    def din(name, shape, dt=F32):
        return nc.dram_tensor(name, shape, dt, kind="ExternalInput").ap()

    def dout(name, shape):
        return nc.dram_tensor(name, shape, F32, kind="ExternalOutput").ap()

    def dscr(name, shape, dt):
        return nc.dram_tensor(name, shape, dt, kind="Internal").ap()

    xp = din("xp", [SEQ, D]); xs = din("xs", [NSAMP, D])
    ck = din("ck", [PAST, D]); cv = din("cv", [PAST, D])
    swkv = din("swkv", [128, 16, 64]); sshift = din("sshift", [128, 16])
    pcols_d = din("pcols", [128, NPV, 16])
    wA_src = din("wA_src", [NSA, 128, 4096]); wB_src = din("wB_src", [NUB, 128, BW])
    w2aug_d = din("w2aug", [128, D]); a2p_d = din("a2p", [128, D]); g2p_d = din("g2p", [128, 2, D])
    consts_d = din("consts", [128, C_W]); dlam_d = din("dlam", [1, 256]); subg_d = din("subg", [1, 128])

    yp = dout("yp", [SEQ, D]); ys = dout("ys", [NSAMP, D])
    kp = dout("kp", [SEQ, D]); vp = dout("vp", [SEQ, D])
    wkvp = dout("wkvp", [128, 16, 64]); shp = dout("shp", [128, 16])
    ksm = dout("ksm", [NSAMP, D]); vsm = dout("vsm", [NSAMP, D])
    wkvs = dout("wkvs", [128, 16, 64]); shs = dout("shs", [128, 16])

    wA_parts = [dscr("wA%d" % i, [67, 128, 4096], BF16) for i in range(4)]
    wA = [wA_parts[i // 67][i % 67] for i in range(NUA)]
    wB = dscr("wB", [NUB, 128, BW], BF16)
    KTscr = dscr("KTscr", [D, SEQ], BF16); Vscr = dscr("Vscr", [SEQ, D], BF16)
    KTs = dscr("KTs", [D, 64], BF16); Vs = dscr("Vs", [64, D], BF16)
    B_wA = [Buf() for _ in range(NUA)]; B_wB = [Buf() for _ in range(NUB)]
    B_KTscr = Buf(); B_Vscr = Buf(); B_KTs = Buf(); B_Vs = Buf()

    P = Prog(nc)

    def stage(n):
        if REAL[0] and STOP == n:
            P.dry = True

    def bar():
        if not P.dry:
            P.barrier()

    def mm(out, lhsT, rhs, start, stop, R, W):
        P.op("pe", lambda e: e.matmul(out, lhsT, rhs, start=start, stop=stop), R, W)

    def tr(out, in_, ident, R, W):
        P.op("pe", lambda e: e.transpose(out, in_, ident), R, W)

    def tt(eng, out, a, b, op, R, W):
        P.op(eng, lambda e: e.tensor_tensor(out, a, b, op), R, W)

    def ts(eng, out, a, s1, s2, op0, op1, R, W):
        if op1 is None:
            P.op(eng, lambda e: e.tensor_scalar(out, a, s1, None, op0), R, W)
        else:
            P.op(eng, lambda e: e.tensor_scalar(out, a, s1, s2, op0, op1), R, W)

    def stt(eng, out, in0, scalar, in1, op0, op1, R, W):
        P.op(eng, lambda e: e.scalar_tensor_tensor(out, in0, scalar, in1, op0, op1), R, W)

    def act(out, in_, func, R, W, **kw):
        P.op("act", lambda e: e.activation(out, in_, func, **kw), R, W)

    def cp(eng, out, in_, R, W):
        if eng == "act":
            P.op("act", lambda e: e.copy(out, in_), R, W)
        else:
            P.op(eng, lambda e: e.tensor_copy(out, in_), R, W)

    def mset(eng, ap, val, W):
        P.op(eng, lambda e: e.memset(ap, val), (), W)

    def recip(out, in_, R, W):
        P.op("dve", lambda e: e.reciprocal(out, in_), R, W)

    with ExitStack() as es0:
        def sb0(name, shape, dt=F32):
            return es0.enter_context(nc.sbuf_tensor(name, shape, dt))
        mu_t = sb0("mu_t", [128, 6, 16]); B_mu = Buf()
        P.dma("sp", mu_t[:], pcols_d[:, PV_MU:PV_MU + 6, :], writes=[B_mu])
        st32 = [(sb0("st32_%d" % i, [128, 4096]), Buf()) for i in range(3)]
        st16 = [(sb0("st16_%d" % i, [128, 4096], BF16), Buf()) for i in range(4)]
        r32 = Rot(st32); r16 = Rot(st16); reng = Rot(["dve", "act", "pool"])
        rq = Rot(["sp", "pool"])

        def conv_A(src_idx, dst_idx, scale=None):
            t32, b32 = r32.next()
            P.dma("sp", t32[:, 0:4096], wA_src[src_idx], writes=[b32])
            t16, b16 = r16.next()
            eng = reng.next()
            cp(eng, t16[:, 0:4096], t32[:, 0:4096], [b32], [b16])
            P.dma(rq.next(), wA[dst_idx], t16[:, 0:4096], reads=[b16], writes=[B_wA[dst_idx]])
            if scale is not None:
                dsts, specs = scale
                t16s, b16s = r16.next()
                for dc in range(DC):
                    for (c0, c1, mj) in specs:
                        eng = "dve" if (dc % 2 == 0) else "pool"
                        ts(eng, t16s[:, dc * 256 + c0: dc * 256 + c1], t32[:, dc * 256 + c0: dc * 256 + c1],
                           mu_t[:, mj, dc:dc + 1], None, ALU.mult, None, [b32, B_mu], [b16s])
                P.dma(rq.next(), wA[dsts], t16s[:, 0:4096], reads=[b16s], writes=[B_wA[dsts]])

        for u in range(176):
            conv_A(SA_FFN + u, UA_FFN + u)
        mu_of = [0, 2, 3]
        for j in range(3):
            for blk in range(8):
                conv_A(SA_RKV + j * 8 + blk, UA_RKV + j * 8 + blk, (UA_RKVS + j * 8 + blk, [(0, 256, mu_of[j])]))
        conv_A(SA_L1, UA_L1, (UA_L1S, [(0, 128, 1), (128, 256, 4)]))
        conv_A(SA_G1, UA_G1, (UA_G1S, [(0, 256, 5)]))
        for blk in range(8):
            conv_A(SA_WO + blk, UA_WO + blk)
        for blk in range(16):
            conv_A(SA_KV + blk, UA_KV + blk)
        for blk in range(8):
            conv_A(SA_Q + blk, UA_Q + blk)
        for blk in range(8):
            conv_A(SA_DWO + blk, UA_DWO + blk)
        for u in range(NUB):
            t32, b32 = r32.next()
            P.dma("sp", t32[:, 0:BW], wB_src[u], writes=[b32])
            t16, b16 = r16.next()
            if u % 2 == 0:
                P.op("act", lambda e, o=t16, i=t32: e.mul(o[:, 0:BW], i[:, 0:BW], 0.5), [b32], [b16])
            else:
                ts("dve", t16[:, 0:BW], t32[:, 0:BW], 0.5, None, ALU.mult, None, [b32], [b16])
            P.dma(rq.next(), wB[u], t16[:, 0:BW], reads=[b16], writes=[B_wB[u]])
        P.barrier()
    if STOP == 0:
        P.dry = True

    es = ExitStack()

    def sb(name, shape, dt=F32):
        return es.enter_context(nc.sbuf_tensor(name, shape, dt))

    with es:
        xres = sb("xres", [128, DC, TWP]); Bx = [Buf() for _ in range(DC)]
        xb = sb("xb", [128, DC, TWP], BF16); Bxb = [Buf() for _ in range(DC)]
        hid = sb("hid", [128, 48, TWP], BF16); Bh = [Buf() for _ in range(48)]
        sA = sb("sA", [128, DC, TWP], BF16); BsA = [Buf() for _ in range(DC)]
        sB = sb("sB", [128, DC, TWP], BF16); BsB = [Buf() for _ in range(DC)]
        tokst = sb("tokst", [128, 2, D]); Btk = [[Buf() for _ in range(DC)] for _ in range(2)]
        wslots = [(sb("wslot%d" % i, [128, 4096], BF16), Buf()) for i in range(3)]
        NS = len(wslots)
        banks = [(es.enter_context(nc.psum_tensor("bank%d" % i, [128, 512], F32)), Buf()) for i in range(8)]
        G = Rot([banks[i] for i in (0, 1, 2, 3)])
        Obanks = [banks[4], banks[5], banks[6], banks[7]]
        cst = sb("cst", [128, C_W]); Bc = Buf()
        pc = sb("pc", [128, NPV, 16]); Bpc = Buf()
        omka = sb("omka", [128, 16]); Bomka = Buf()
        w2aug = sb("w2aug_s", [128, D]); Bw2 = Buf()
        a2b = sb("a2b", [128, D], BF16); Ba2 = Buf()
        g2b = sb("g2b", [128, 2, D], BF16); Bg2 = Buf()
        Hst = sb("Hst", [128, 16, 128]); BH = [Buf() for _ in range(16)]
        carry = sb("carry", [128, 16]); Bcar = Buf()
        negv = sb("negv", [128, 2]); Bnegv = Buf()
        lamt = sb("lamt", [128, 256]); Blam = Buf()
        lamc = sb("lamc", [128, 8]); Blamc = Buf()
        subg = sb("subg_s", [128, 128]); Bsubg = Buf()
        hTw = sb("hTw", [128, TWP]); BhTw = Buf()
        hTa = sb("hTa", [128, TWP], BF16); BhTa = Buf()
        hTg = sb("hTg", [128, 2, TWP], BF16); BhTg = Buf()
        stat = [(sb("stat%d" % i, [128, TWP]), Buf()) for i in range(4)]
        tmpA = Rot([(sb("tmpA%d" % i, [128, TWP]), Buf()) for i in range(3)])
        tmpB = Rot([(sb("tmpB%d" % i, [128, TWP], BF16), Buf()) for i in range(3)])
        RW = {n: (sb("rw_" + n, [128, TWP]), Buf()) for n in
              ["a", "g", "pinc", "pexc", "pinv", "kkr", "rn", "kk", "kmod", "b", "bv", "yT", "yc"]}
        RW["sq"], RW["tmp"], RW["rk"], RW["rstd"] = stat[0], stat[1], stat[2], stat[3]
        logd = sb("logd", [128, 2, 128]); Blogd = Buf()
        NCHM = TWP // 64
        RKfp = [(sb("RKfp%d" % i, [128, NCHM, 2, 128]), Buf()) for i in range(1)]
        Ktfp = [(sb("Ktfp%d" % i, [128, NCHM, 128]), Buf()) for i in range(1)]
        Btfp = [(sb("Btfp%d" % i, [128, NCHM, 128]), Buf()) for i in range(1)]
        Vtfp = [(sb("Vtfp%d" % i, [128, NCHM, 128]), Buf()) for i in range(1)]
        cmL = [(sb("cmL%d" % i, [128, 256]), Buf()) for i in range(4)]
        cmX = Rot([(sb("cmX%d" % i, [128, 128]), Buf()) for i in range(3)])
        cmM = Rot([(sb("cmM%d" % i, [128, 256]), Buf()) for i in range(3)])
        ktp = Rot([(sb("ktp%d" % i, [128, 1024], BF16), Buf()) for i in range(2)])
        vtp_items = [(sb("vtp%d" % i, [128, 8, 129], BF16), Buf()) for i in range(2)]
        vtp = Rot(vtp_items)
        ptp = Rot([(sb("ptp%d" % i, [128, 512], BF16), Buf()) for i in range(3)])
        ckst = sb("ckst", [128, 8, 128]); Bckst = Buf()
        osm = Rot([(sb("osm%d" % i, [128, 136]), Buf()) for i in range(4)])

        ones_full = sb("ones_full", [128, 128]); Bones = Buf()
        mset("pool", ones_full[:], 1.0, [Bones])
        ident = cst[:, C_ID:C_ID + 128]
        bones = cst[:, C_BO:C_BO + 128]
        mSLn = cst[:, C_MSL:C_MSL + 128]
        mK = cst[:, C_MK:C_MK + 256]
        mBn = cst[:, C_MB:C_MB + 256]
        triI = cst[:, C_TI:C_TI + 64]
        triE = cst[:, C_TE:C_TE + 64]

        P.dma("sp", cst[:], consts_d, writes=[Bc])
        P.dma("sp", pc[:], pcols_d, writes=[Bpc])
        P.dma("sp", w2aug[:], w2aug_d, writes=[Bw2])
        P.dma("sp", tokst[:, 0, :], a2p_d, writes=Btk[0])
        cp("dve", a2b[:], tokst[:, 0, :], Btk[0], [Ba2])
        for kc in range(2):
            P.dma("sp", tokst[:, 1, :], g2p_d[:, kc, :], writes=Btk[1])
            cp("dve", g2b[:, kc, :], tokst[:, 1, :], Btk[1], [Bg2])
        ts("dve", omka[:], pc[:, PV_KA, :], -1.0, 1.0, ALU.mult, ALU.add, [Bpc], [Bomka])
        mset("pool", negv[:], NEG_E, [Bnegv])
        P.op("pool", lambda e: e.affine_select(negv[:, 1:2], negv[:, 1:2], pattern=[[0, 1]], compare_op=ALU.is_gt,
                                               fill=0.0, base=NSAMP, channel_multiplier=-1), [Bnegv], [Bnegv])
        for (t_, b_) in RKfp + Ktfp + Btfp + Vtfp:
            mset("pool", t_[:], 0.0, [b_])
        for (t_, b_) in vtp_items:
            mset("pool", t_[:], 1.0, [b_])
        P.dma("sp", lamt[:], dlam_d.partition_broadcast(128), writes=[Blam])
        P.dma("sp", subg[:], subg_d.partition_broadcast(128), writes=[Bsubg])
        tt("dve", lamt[:, 0:64], lamt[:, 0:64], lamt[:, 64:128], ALU.mult, [Blam], [Blam])
        tt("dve", lamt[:, 128:192], lamt[:, 128:192], lamt[:, 192:256], ALU.mult, [Blam], [Blam])
        P.op("dve", lambda e: e.reduce_sum(lamc[:, 0:1], lamt[:, 0:64], axis=AX.X), [Blam], [Blamc])
        P.op("dve", lambda e: e.reduce_sum(lamc[:, 1:2], lamt[:, 128:192], axis=AX.X), [Blam], [Blamc])
        act(lamc[:, 2:4], lamc[:, 0:2], AF.Exp, [Blamc], [Blamc])
        tt("dve", lamc[:, 4:5], lamc[:, 3:4], lamc[:, 2:3], ALU.subtract, [Blamc], [Blamc])
        ts("dve", lamc[:, 5:6], lamc[:, 4:5], -LAMBDA_INIT, None, ALU.add, None, [Blamc], [Blamc])
        ts("dve", subg[:], subg[:], 1.0 - LAMBDA_INIT, None, ALU.mult, None, [Bsubg], [Bsubg])
        neglam = lamc[:, 5:6]

        class WS:
            seq = []
            pos = 0
            issued = 0

        def wissue(i):
            kind, idx = WS.seq[i]
            t_, b_ = wslots[i % NS]
            if kind == "A":
                P.dma("sp", t_[:, 0:4096], wA[idx], reads=[B_wA[idx]], writes=[b_])
            else:
                P.dma("sp", t_[:, 0:BW], wB[idx], reads=[B_wB[idx]], writes=[b_])

        def wget(kind, idx, hold=0):
            if P.dry:
                WS.seq.append((kind, idx))
                return wslots[0]
            assert WS.seq[WS.pos] == (kind, idx)
            while WS.issued < min(len(WS.seq), WS.pos + NS - hold):
                wissue(WS.issued)
                WS.issued += 1
            r = wslots[WS.pos % NS]
            WS.pos += 1
            return r

        def uA(slot):
            return slot[:, 0:4096].rearrange("p (c f) -> p c f", c=DC)

        def uB(slot):
            return slot[:, 0:BW].rearrange("p (m d) -> p m d", m=11)

        def proj_fm(unit_list, TW, evac):
            for blk, parts in enumerate(unit_list):
                slots = [(wget("A", ui, hold=pi), src, sbufs) for pi, (ui, src, sbufs) in enumerate(parts)]
                for half in range(2):
                    bank, bb = G.next()
                    n = len(slots) * DC
                    k = 0
                    for (st, sbf), src, sbufs in slots:
                        u = uA(st)
                        for dc in range(DC):
                            mm(bank[:, 0:TW], u[:, dc, half * 128:(half + 1) * 128], src[:, dc, 0:TW],
                               k == 0, k == n - 1, [sbf, sbufs[dc]], [bb])
                            k += 1
                    evac(2 * blk + half, bank, bb)

        def layer_norm(li, TW):
            S, Sb = G.next()
            S2, S2b = G.next()
            for c in range(DC):
                sq, sqb = tmpA.next()
                act(sq[:, 0:TW], xres[:, c, 0:TW], AF.Square, [Bx[c]], [sqb])
                mm(S[:, 0:TW], ones_full[:], xres[:, c, 0:TW], c == 0, c == DC - 1, [Bones, Bx[c]], [Sb])
                mm(S2[:, 0:TW], ones_full[:], sq[:, 0:TW], c == 0, c == DC - 1, [Bones, sqb], [S2b])
            (m_, mb), (q_, qb_), (v_, vb), (r_, rb) = stat[0], stat[1], stat[2], stat[3]
            ts("dve", m_[:, 0:TW], S[:, 0:TW], 1.0 / D, None, ALU.mult, None, [Sb], [mb])
            tt("dve", q_[:, 0:TW], m_[:, 0:TW], m_[:, 0:TW], ALU.mult, [mb], [qb_])
            ts("dve", v_[:, 0:TW], S2[:, 0:TW], 1.0 / D, LN_EPS, ALU.mult, ALU.add, [S2b], [vb])
            tt("dve", v_[:, 0:TW], v_[:, 0:TW], q_[:, 0:TW], ALU.subtract, [vb, qb_], [vb])
            P.op("act", lambda e: e.sqrt(v_[:, 0:TW], v_[:, 0:TW]), [vb], [vb])
            recip(r_[:, 0:TW], v_[:, 0:TW], [vb], [rb])
            for c in range(DC):
                t1, t1b = tmpA.next()
                tt("dve", t1[:, 0:TW], xres[:, c, 0:TW], m_[:, 0:TW], ALU.subtract, [Bx[c], mb], [t1b])
                tt("pool", t1[:, 0:TW], t1[:, 0:TW], r_[:, 0:TW], ALU.mult, [t1b, rb], [t1b])
                act(xres[:, c, 0:TW], t1[:, 0:TW], AF.Identity, [t1b, Bpc], [Bx[c]],
                    scale=pc[:, PV_LNG + li, c:c + 1], bias=pc[:, PV_LNB + li, c:c + 1])
                ts("dve", xb[:, c, 0:TW], t1[:, 0:TW], pc[:, PV_LNG + li, c:c + 1], pc[:, PV_LNB + li, c:c + 1],
                   ALU.mult, ALU.add, [t1b, Bpc], [Bxb[c]])


        def ffn(fi, TW):
            for m in range(MC):
                sg, sgb = wget("A", UA_FFN + fi * 44 + m)
                ug = uA(sg)
                bank, bb = G.next()
                bank2, bb2 = G.next()
                for dc in range(DC):
                    mm(bank[:, 0:TW], ug[:, dc, 0:128], xb[:, dc, 0:TW], dc == 0, dc == DC - 1, [sgb, Bxb[dc]], [bb])
                for dc in range(DC):
                    mm(bank2[:, 0:TW], ug[:, dc, 128:256], xb[:, dc, 0:TW], dc == 0, dc == DC - 1, [sgb, Bxb[dc]], [bb2])
                t1, t1b = tmpA.next()
                act(t1[:, 0:TW], bank[:, 0:TW], AF.Silu, [bb], [t1b])
                tt("dve", hid[:, m, 0:TW], t1[:, 0:TW], bank2[:, 0:TW], ALU.mult, [t1b, bb2], [Bh[m]])
            for g in range(8):
                (bA, bAb) = G.next()
                (bB, bBb) = G.next()
                accs = [(bA[:, 0:TW], bAb), (bB[:, 0:TW], bBb)]
                for q in range(4):
                    sw, swb = wget("B", fi * 32 + g * 4 + q)
                    u = uB(sw)
                    for oo in range(2):
                        for mm_ in range(11):
                            m = q * 11 + mm_
                            mm(accs[oo][0], u[:, mm_, oo * 128:(oo + 1) * 128], hid[:, m, 0:TW],
                               q == 0 and mm_ == 0, q == 3 and mm_ == 10, [swb, Bh[m]], [accs[oo][1]])
                for oo in range(2):
                    c = g * 2 + oo
                    stt("dve", xres[:, c, 0:TW], xres[:, c, 0:TW], ALPHA, accs[oo][0], ALU.mult, ALU.add, [Bx[c], accs[oo][1]], [Bx[c]])

        def resid_evac(TW):
            def f(c, bank, bb):
                stt("dve", xres[:, c, 0:TW], xres[:, c, 0:TW], ALPHA, bank[:, 0:TW], ALU.mult, ALU.add, [Bx[c], bb], [Bx[c]])
            return f

        def rwkv(TW, ntok, nvcol, shift_out):
            nch = TW // 64
            nb = (TW + 127) // 128
            ntp = min(TW, 128)
            tt("dve", sA[:, :, 1:TW], xres[:, :, 0:TW - 1], xres[:, :, 1:TW], ALU.subtract, Bx, BsA)
            tt("dve", sA[:, :, 0:1], carry[:, :].unsqueeze(2), xres[:, :, 0:1], ALU.subtract, Bx + [Bcar], BsA)
            cp("pool", carry[:, :].unsqueeze(2), xres[:, :, ntok - 1:ntok], Bx, [Bcar])
            if shift_out is not None:
                P.dma("pool", shift_out, carry[:], reads=[Bcar], is_output=True)
            def sr(n):
                if TW == TWP:
                    stage(n)
            sr(201)
            for j in range(3):
                ul = [[(UA_RKV + j * 8 + blk, xb, Bxb), (UA_RKVS + j * 8 + blk, sA, BsA)] for blk in range(8)]

                def ev(c, bank, bb, j=j):
                    eng = "act" if c % 2 == 0 else "dve"
                    cp(eng, hid[:, j * 16 + c, 0:TW], bank[:, 0:TW], [bb], [Bh[j * 16 + c]])
                proj_fm(ul, TW, ev)
            if ntok < TW:
                mset("pool", hid[:, 16:32, ntok:TW], 0.0, Bh[16:32])
            sr(202)
            for which in range(2):
                (ua_, uab), (ub_, ubb) = (wget("A", UA_L1), wget("A", UA_L1S, hold=1)) if which == 0 else (wget("A", UA_G1), wget("A", UA_G1S, hold=1))
                for half in range(2):
                    bank, bb = G.next()
                    k = 0
                    for (st, sbf, src, sbufs) in [(ua_, uab, xb, Bxb), (ub_, ubb, sA, BsA)]:
                        u = uA(st)
                        for dc in range(DC):
                            mm(bank[:, 0:TW], u[:, dc, half * 128:(half + 1) * 128], src[:, dc, 0:TW], k == 0, k == 2 * DC - 1, [sbf, sbufs[dc]], [bb])
                            k += 1
                    if which == 0 and half == 0:
                        act(hTw[:, 0:TW], bank[:, 0:TW], AF.Tanh, [bb], [BhTw])
                        mset("pool", hTw[0:32, 0:TW], 0.0, [BhTw])
                        mset("pool", hTw[0:1, 0:TW], 1.0, [BhTw])
                    elif which == 0:
                        cp("dve", hTa[:, 0:TW], bank[:, 0:TW], [bb], [BhTa])
                    else:
                        act(hTg[:, half, 0:TW], bank[:, 0:TW], AF.Sigmoid, [bb], [BhTg])
            sr(203)
            for c in range(16):
                if c == 1:
                    sr(207)
                if c == 2:
                    sr(208)
                bar()
                rT = hid[:, c, 0:TW]; kT = hid[:, 16 + c, 0:TW]; vT = hid[:, 32 + c, 0:TW]
                Br, Bk, Bv = Bh[c], Bh[16 + c], Bh[32 + c]
                cs = slice(c * 128, (c + 1) * 128)
                a_, ab = RW["a"]; g_, gb = RW["g"]
                bank, bb = G.next()
                mm(bank[:, 0:TW], a2b[:, cs], hTa[:, 0:TW], True, True, [Ba2, BhTa], [bb])
                for kc in range(2):
                    mm(bank[:, 256:256 + TW], g2b[:, kc, cs], hTg[:, kc, 0:TW], kc == 0, kc == 1, [Bg2, BhTg], [bb])
                act(a_[:, 0:TW], bank[:, 0:TW], AF.Sigmoid, [bb, Bpc], [ab], bias=pc[:, PV_A0, c:c + 1], scale=1.0)
                cp("act", g_[:, 0:TW], bank[:, 256:256 + TW], [bb], [gb])
                bank, bb = G.next()
                for tb in range(nb):
                    mm(bank[0:ntp, tb * 128:(tb + 1) * 128], hTw[:, tb * 128:tb * 128 + ntp], w2aug[:, cs], True, True, [BhTw, Bw2], [bb])
                for tb in range(nb):
                    act(logd[0:ntp, tb, :], bank[0:ntp, tb * 128:(tb + 1) * 128], AF.Sigmoid, [bb], [Blogd])
                    ts("dve", logd[0:ntp, tb, :], logd[0:ntp, tb, :], negv[0:ntp, nvcol:nvcol + 1], None, ALU.mult, None, [Blogd, Bnegv], [Blogd])
                bank, bb = G.next()
                for tb in range(nb):
                    mm(bank[:, tb * 128:tb * 128 + ntp], logd[0:ntp, tb, :], cst[0:ntp, C_MK + 128:C_MK + 128 + ntp], True, True, [Blogd, Bc], [bb])
                    mm(bank[:, 256 + tb * 128:256 + tb * 128 + ntp], logd[0:ntp, tb, :], cst[0:ntp, C_MK:C_MK + ntp], True, True, [Blogd, Bc], [bb])
                pinc, pincb = RW["pinc"]; pexc, pexcb = RW["pexc"]; pinv, pinvb = RW["pinv"]
                act(pinc[:, 0:TW], bank[:, 0:TW], AF.Exp, [bb], [pincb])
                act(pexc[:, 0:TW], bank[:, 256:256 + TW], AF.Exp, [bb], [pexcb])
                act(pinv[:, 0:TW], bank[:, 0:TW], AF.Exp, [bb], [pinvb], scale=-1.0)
                sr(204)
                kkr, kkrb = RW["kkr"]; sq, sqb = RW["sq"]; rn, rnb = RW["rn"]; kk, kkb = RW["kk"]
                ts("dve", kkr[:, 0:TW], kT, pc[:, PV_KK, c:c + 1], None, ALU.mult, None, [Bk, Bpc], [kkrb])
                act(sq[:, 0:TW], kkr[:, 0:TW], AF.Square, [kkrb], [sqb])
                bank, bb = G.next()
                mm(bank[:, 0:TW], bones, sq[:, 0:TW], True, True, [Bc, sqb], [bb])
                ts("dve", rn[:, 0:TW], bank[:, 0:TW], 1e-24, None, ALU.max, None, [bb], [rnb])
                P.op("act", lambda e, o=rn, w=TW: e.sqrt(o[:, 0:w], o[:, 0:w]), [rnb], [rnb])
                recip(rn[:, 0:TW], rn[:, 0:TW], [rnb], [rnb])
                tt("dve", kk[:, 0:TW], kkr[:, 0:TW], rn[:, 0:TW], ALU.mult, [kkrb, rnb], [kkb])
                tmp, tmpb = RW["tmp"]; kmod, kmodb = RW["kmod"]; b_, bbf = RW["b"]; rk, rkb = RW["rk"]; bv, bvb = RW["bv"]
                ts("pool", tmp[:, 0:TW], a_[:, 0:TW], pc[:, PV_KA, c:c + 1], omka[:, c:c + 1], ALU.mult, ALU.add, [ab, Bpc, Bomka], [tmpb])
                tt("pool", kmod[:, 0:TW], kT, tmp[:, 0:TW], ALU.mult, [Bk, tmpb], [kmodb])
                tt("pool", b_[:, 0:TW], kk[:, 0:TW], a_[:, 0:TW], ALU.mult, [kkb, ab], [bbf])
                stt("dve", rk[:, 0:TW], rT, pc[:, PV_RK, c:c + 1], kmod[:, 0:TW], ALU.mult, ALU.mult, [Br, Bpc, kmodb], [rkb])
                mm(bank[:, 256:256 + TW], bones, rk[:, 0:TW], True, True, [Bc, rkb], [bb])
                tt("dve", bv[:, 0:TW], bank[:, 256:256 + TW], vT, ALU.mult, [bb, Bv], [bvb])
                (RK, RKb), (Kt, Ktb), (Bt, Btb), (Vt, Vtb) = RKfp[0], Ktfp[0], Btfp[0], Vtfp[0]
                for h in range(2):
                    hs = slice(h * 64, h * 64 + 64)

                    def v3(ap):
                        return ap.rearrange("p (j t) -> p j t", t=64)
                    tt("dve", RK[hs, 0:nch, 0, hs], v3(kk[hs, 0:TW]), v3(pexc[hs, 0:TW]), ALU.mult, [kkb, pexcb], [RKb])
                    tt("pool", RK[hs, 0:nch, 1, hs], v3(hid[hs, c, 0:TW]), v3(pinc[hs, 0:TW]), ALU.mult, [Br, pincb], [RKb])
                    tt("dve", Kt[hs, 0:nch, hs], v3(kmod[hs, 0:TW]), v3(pinv[hs, 0:TW]), ALU.mult, [kmodb, pinvb], [Ktb])
                    tt("pool", Bt[hs, 0:nch, hs], v3(b_[hs, 0:TW]), v3(pinv[hs, 0:TW]), ALU.mult, [bbf, pinvb], [Btb])
                    cp("act", Vt[hs, 0:nch, hs], v3(hid[hs, 32 + c, 0:TW]), [Bv], [Vtb])
                sr(205)
                yT, yTb = RW["yT"]
                H0 = Hst[:, c, :]
                for j in range(nch):
                    bank, bb = G.next()
                    tr(bank[:, 0:128], Kt[:, j, :], ident, [Ktb, Bc], [bb])
                    tr(bank[:, 128:256], Bt[:, j, :], ident, [Btb, Bc], [bb])
                    tr(bank[:, 256:384], Vt[:, j, :], ident, [Vtb, Bc], [bb])
                    KB, KBb = cmL[0]; VM, Vbdb = cmL[1]; Vbd = VM
                    cp("act", KB[:, 0:128], bank[:, 0:128], [bb], [KBb])
                    P.op("act", lambda e, o=KB, i=bank: e.mul(o[:, 128:256], i[:, 128:256], -1.0), [bb], [KBb])
                    cp("act", Vbd[:, 0:128], bank[:, 256:384], [bb], [Vbdb])
                    bank, bb = G.next()
                    mm(bank[:, 0:128], RK[:, j, 0, :], Bt[:, j, :], True, True, [RKb, Btb], [bb])
                    M0, M0b = _Shift(VM), Vbdb
                    tt("dve", M0[:, 0:128], bank[:, 0:128], mSLn, ALU.mult, [bb, Bc], [M0b])
                    bank, bb = G.next()
                    mm(bank[:, 0:256], Kt[:, j, :], RK[:, j, :, :].rearrange("p a b -> p (a b)"), True, True, [Ktb, RKb], [bb])
                    GK, GKb = cmL[2]
                    tt("dve", GK[:, 0:256], bank[:, 0:256], mK, ALU.mult, [bb, Bc], [GKb])
                    bank, bb = G.next()
                    mm(bank[:, 0:256], Bt[:, j, :], RK[:, j, :, :].rearrange("p a b -> p (a b)"), True, True, [Btb, RKb], [bb])
                    GB, GBb = cmL[3]
                    tt("dve", GB[:, 0:256], bank[:, 0:256], mBn, ALU.mult, [bb, Bc], [GBb])
                    bank, bb = G.next()
                    mm(bank[:, 0:128], RK[:, j, 0, :], H0, True, False, [RKb, BH[c]], [bb])
                    mm(bank[:, 0:128], GK[:, 0:128], Vbd[:, 0:128], False, True, [GKb, Vbdb], [bb])
                    X, Xb = cmX.next()
                    cp("act", X[:, 0:128], bank[:, 0:128], [bb], [Xb])
                    Mc, Mcb = M0, M0b
                    McT, McTb = GB, GBb
                    for lvl in range(6):
                        bank, bb = G.next()
                        mm(bank[:, 0:128], McT[:, 0:128], X[:, 0:128], True, True, [McTb, Xb], [bb])
                        if lvl < 5:
                            mm(bank[:, 128:256], Mc[:, 0:128], McT[:, 0:128], True, True, [Mcb, McTb], [bb])
                            if lvl < 4:
                                mm(bank[:, 256:384], McT[:, 0:128], Mc[:, 0:128], True, True, [Mcb, McTb], [bb])
                        Xn, Xnb = cmX.next()
                        tt("dve", Xn[:, 0:128], bank[:, 0:128], X[:, 0:128], ALU.add, [bb, Xb], [Xnb])
                        X, Xb = Xn, Xnb
                        if lvl < 5:
                            Mn, Mnb = cmM.next()
                            cp("dve", Mn[:, 0:128], bank[:, 128:256], [bb], [Mnb])
                            if lvl < 4:
                                cp("dve", Mn[:, 128:256], bank[:, 256:384], [bb], [Mnb])
                            McT, McTb = Mn, Mnb
                            Mc, Mcb = _Shift(Mn), Mnb
                    U, Ub = X, Xb
                    bank, bb = G.next()
                    mm(bank[:, 0:128], H0, RK[:, j, 1, :], True, False, [BH[c], RKb], [bb])
                    mm(bank[:, 0:128], Vbd[:, 0:128], GK[:, 128:256], False, False, [Vbdb, GKb], [bb])
                    mm(bank[:, 0:128], U[:, 0:128], GB[:, 128:256], False, True, [Ub, GBb], [bb])
                    mm(bank[:, 128:256], KB[:, 0:128], Vbd[:, 0:128], True, False, [KBb, Vbdb], [bb])
                    mm(bank[:, 128:256], KB[:, 128:256], U[:, 0:128], False, False, [KBb, Ub], [bb])
                    mm(bank[:, 128:256], ident, H0, False, True, [Bc, BH[c]], [bb])
                    cp("act", yT[0:64, j * 64:(j + 1) * 64], bank[0:64, 0:64], [bb], [yTb])
                    cp("act", yT[64:128, j * 64:(j + 1) * 64], bank[64:128, 64:128], [bb], [yTb])
                    act(H0, bank[:, 128:256], AF.Identity, [bb, pincb], [BH[c]], scale=pinc[:, j * 64 + 63:j * 64 + 64])
                sr(206)
                yc, ycb = RW["yc"]; rstd, rstdb = RW["rstd"]
                bank, bb = G.next()
                mm(bank[:, 0:TW], bones, yT[:, 0:TW], True, True, [Bc, yTb], [bb])
                stt("dve", yc[:, 0:TW], bank[:, 0:TW], -1.0 / 64, yT[:, 0:TW], ALU.mult, ALU.add, [bb, yTb], [ycb])
                act(sq[:, 0:TW], yc[:, 0:TW], AF.Square, [ycb], [sqb])
                mm(bank[:, 256:256 + TW], bones, sq[:, 0:TW], True, True, [Bc, sqb], [bb])
                ts("dve", rstd[:, 0:TW], bank[:, 256:256 + TW], 1.0 / 64, GN_EPS, ALU.mult, ALU.add, [bb], [rstdb])
                P.op("act", lambda e, o=rstd, w=TW: e.sqrt(o[:, 0:w], o[:, 0:w]), [rstdb], [rstdb])
                recip(rstd[:, 0:TW], rstd[:, 0:TW], [rstdb], [rstdb])
                tt("dve", yc[:, 0:TW], yc[:, 0:TW], rstd[:, 0:TW], ALU.mult, [ycb, rstdb], [ycb])
                ts("pool", yc[:, 0:TW], yc[:, 0:TW], pc[:, PV_LXG, c:c + 1], pc[:, PV_LXB, c:c + 1], ALU.mult, ALU.add, [ycb, Bpc], [ycb])
                tt("pool", yc[:, 0:TW], yc[:, 0:TW], bv[:, 0:TW], ALU.add, [ycb, bvb], [ycb])
                tt("dve", sB[:, c, 0:TW], yc[:, 0:TW], g_[:, 0:TW], ALU.mult, [ycb, gb], [BsB[c]])
            proj_fm([[(UA_WO + blk, sB, BsB)] for blk in range(8)], TW, resid_evac(TW))

        class _Shift:
            def __init__(self, base):
                self.base = base

            def __getitem__(self, idx):
                assert idx == (slice(None), slice(0, 128))
                return self.base[:, 128:256]

        def attention(TW, prompt, t0):
            nqb = (TW + 127) // 128
            qw = min(TW, 128)
            QT, BQ = sA, BsA
            if prompt:
                nkb = (t0 + TW) // 128
                groups = [("scr", n0, min(8, nkb - n0), 128) for n0 in range(0, nkb, 8)]
            else:
                groups = [("cache", 0, 8, 128), ("own", 0, 1, NSAMP)]

            def valid(qb, n):
                return (not prompt) or n <= (t0 // 128 + qb)
            total = {}
            for qb in range(nqb):
                total[qb] = sum(1 for (_, n0, nblk, _) in groups for n in range(n0, n0 + nblk) if valid(qb, n))
            for c in range(16):
                bar()
                cs = slice(c * 128, (c + 1) * 128)
                done = {(qb, h): 0 for qb in range(nqb) for h in range(2)}
                for (kind, n0, nblk, nk) in groups:
                    KT, KTb = ktp.next(); V, Vb = vtp.next()
                    if kind == "scr":
                        P.dma("pool", KT[:, 0:nblk * 128], KTscr[cs, n0 * 128:(n0 + nblk) * 128], reads=[B_KTscr], writes=[KTb])
                        P.dma("pool", V[:, 0:nblk, 0:128], Vscr[n0 * 128:(n0 + nblk) * 128, cs].rearrange("(n p) e -> p n e", p=128), reads=[B_Vscr], writes=[Vb])
                    elif kind == "cache":
                        P.dma("pool", ckst[:], ck[:, cs].rearrange("(n p) e -> p n e", p=128), writes=[Bckst])
                        for gq in range(2):
                            bank, bb = G.next()
                            for i in range(4):
                                tr(bank[:, i * 128:(i + 1) * 128], ckst[:, gq * 4 + i, :], ident, [Bckst, Bc], [bb])
                            cp("act" if gq == 0 else "dve", KT[:, gq * 512:(gq + 1) * 512], bank[:, 0:512], [bb], [KTb])
                        P.dma("pool", V[:, 0:8, 0:128], cv[:, cs].rearrange("(n p) e -> p n e", p=128), writes=[Vb])
                    else:
                        P.dma("pool", KT[:, 0:64], KTs[cs, :], reads=[B_KTs], writes=[KTb])
                        P.dma("pool", V[0:64, 0, 0:128], Vs[:, cs], reads=[B_Vs], writes=[Vb])
                    for qb in range(nqb):
                        vblocks = [n for n in range(n0, n0 + nblk) if valid(qb, n)]
                        if not vblocks:
                            continue
                        for h in range(2):
                            Ob, Obb = Obanks[qb * 2 + h]
                            hs = slice(h * 64, h * 64 + 64)
                            for b0 in range(0, len(vblocks), 4):
                                batch = vblocks[b0:b0 + 4]
                                bank, bb = G.next()
                                for i, n in enumerate(batch):
                                    nl = n - n0
                                    mm(bank[0:nk, i * 128:i * 128 + qw], KT[hs, nl * 128:nl * 128 + nk], QT[hs, c, qb * 128:qb * 128 + qw],
                                       True, True, [KTb, BQ[c]], [bb])
                                PT, PTb = ptp.next()
                                if qw == 128 and nk == 128:
                                    act(PT[:, 0:len(batch) * 128], bank[:, 0:len(batch) * 128], AF.Exp, [bb], [PTb], scale=0.125)
                                else:
                                    for i in range(len(batch)):
                                        act(PT[0:nk, i * 128:i * 128 + qw], bank[0:nk, i * 128:i * 128 + qw], AF.Exp, [bb], [PTb], scale=0.125)
                                for i, n in enumerate(batch):
                                    if prompt and n == t0 // 128 + qb:
                                        mset("pool", PT[64:128, i * 128:i * 128 + 64], 0.0, [PTb])
                                for i, n in enumerate(batch):
                                    nl = n - n0
                                    d_ = done[(qb, h)]
                                    mm(Ob[0:qw, 0:129], PT[0:nk, i * 128:i * 128 + qw], V[0:nk, nl, 0:129],
                                       d_ == 0, d_ == total[qb] - 1, [PTb, Vb], [Obb])
                                    done[(qb, h)] = d_ + 1
                for qb in range(nqb):
                    Ob, Obb = Obanks[qb * 2]
                    Ob2, Ob2b = Obanks[qb * 2 + 1]
                    o_, ob_ = osm.next()
                    sm, smb = osm.next()
                    recip(sm[0:qw, 0:1], Ob[0:qw, 128:129], [Obb], [smb])
                    recip(sm[0:qw, 1:2], Ob2[0:qw, 128:129], [Ob2b], [smb])
                    tt("dve", sm[0:qw, 2:3], sm[0:qw, 1:2], neglam[0:qw, :], ALU.mult, [smb, Blamc], [smb])
                    ts("dve", o_[0:qw, 0:128], Ob[0:qw, 0:128], sm[0:qw, 0:1], None, ALU.mult, None, [Obb, smb], [ob_])
                    stt("dve", o_[0:qw, 0:128], Ob2[0:qw, 0:128], sm[0:qw, 2:3], o_[0:qw, 0:128], ALU.mult, ALU.add, [Ob2b, smb, ob_], [ob_])
                    sq2, sq2b = osm.next()
                    tt("dve", sq2[0:qw, 0:128], o_[0:qw, 0:128], o_[0:qw, 0:128], ALU.mult, [ob_], [sq2b])
                    P.op("dve", lambda e, s_=sm, q_=sq2, w=qw: e.reduce_sum(s_[0:w, 3:4], q_[0:w, 0:128], axis=AX.X), [sq2b], [smb])
                    ts("dve", sm[0:qw, 4:5], sm[0:qw, 3:4], 1.0 / 128, LN_EPS, ALU.mult, ALU.add, [smb], [smb])
                    P.op("act", lambda e, s_=sm, w=qw: e.sqrt(s_[0:w, 4:5], s_[0:w, 4:5]), [smb], [smb])
                    recip(sm[0:qw, 5:6], sm[0:qw, 4:5], [smb], [smb])
                    stt("dve", tokst[0:qw, qb, cs], o_[0:qw, 0:128], sm[0:qw, 5:6], subg[0:qw, :], ALU.mult, ALU.mult, [ob_, smb, Bsubg], [Btk[qb][c]])
            for qb in range(nqb):
                for g4 in range(4):
                    bank, bb = G.next()
                    for i in range(4):
                        c = g4 * 4 + i
                        tr(bank[:, i * 128:(i + 1) * 128], tokst[:, qb, c * 128:(c + 1) * 128], ident, [Btk[qb][c], Bc], [bb])
                    for i in range(4):
                        c = g4 * 4 + i
                        cp("act" if g4 % 2 == 0 else "dve", sB[:, c, qb * 128:qb * 128 + qw], bank[:, i * 128:i * 128 + qw], [bb], [BsB[c]])

        def tile_pass(prompt, it):
            TW = TWP if prompt else 64
            ntok = TW if prompt else NSAMP
            nb = (TW + 127) // 128
            ntp = min(TW, 128)
            t0 = it * TWP

            def st_(n):
                stage(n + (100 if prompt else 0))
                bar()
            for b in range(nb):
                if prompt:
                    P.dma("pool", tokst[:, b, :], xp[t0 + b * 128:t0 + (b + 1) * 128, :], writes=Btk[b])
                else:
                    mset("pool", tokst[:, 0, :], 0.0, Btk[0])
                    P.dma("sp", tokst[0:NSAMP, 0, :], xs, writes=Btk[0])
                for g4 in range(4):
                    bank, bb = G.next()
                    for i in range(4):
                        c = g4 * 4 + i
                        tr(bank[:, i * 128:(i + 1) * 128], tokst[:, b, c * 128:(c + 1) * 128], ident, [Btk[b][c], Bc], [bb])
                    for i in range(4):
                        c = g4 * 4 + i
                        cp("dve", xres[:, c, b * 128:b * 128 + ntp], bank[:, i * 128:i * 128 + ntp], [bb], [Bx[c]])
                        cp("act", xb[:, c, b * 128:b * 128 + ntp], xres[:, c, b * 128:b * 128 + ntp], [Bx[c]], [Bxb[c]])
            st_(2)
            ffn(0, TW); layer_norm(0, TW)
            st_(3)
            rwkv(TW, ntok, 0 if prompt else 1, (shp if it == NT - 1 else None) if prompt else shs)
            st_(4)
            layer_norm(1, TW)
            ffn(1, TW); layer_norm(2, TW)
            st_(5)
            for blk in range(16):
                su, sub_ = wget("A", UA_KV + blk)
                u = uA(su)
                if blk < 8:
                    for half in range(2):
                        c = 2 * blk + half
                        bank, bb = G.next()
                        for dc in range(DC):
                            mm(bank[:, 0:TW], u[:, dc, half * 128:(half + 1) * 128], xb[:, dc, 0:TW], dc == 0, dc == DC - 1, [sub_, Bxb[dc]], [bb])
                        cp("act", sA[:, c, 0:TW], bank[:, 0:TW], [bb], [BsA[c]])
                for b in range(nb):
                    bank, bb = G.next()
                    for dc in range(DC):
                        mm(bank[0:ntp, 0:256], xb[:, dc, b * 128:b * 128 + ntp], u[:, dc, :], dc == 0, dc == DC - 1, [sub_, Bxb[dc]], [bb])
                    t1, t1b = tmpA.next()
                    cp("dve", t1[0:ntp, 0:256], bank[0:ntp, 0:256], [bb], [t1b])
                    col = (blk % 8) * 256
                    if prompt:
                        dst = (kp if blk < 8 else vp)[t0 + b * 128:t0 + (b + 1) * 128, col:col + 256]
                        P.dma("pool", dst, t1[:, 0:256], reads=[t1b], is_output=True)
                    else:
                        dst = (ksm if blk < 8 else vsm)[:, col:col + 256]
                        P.dma("pool", dst, t1[0:NSAMP, 0:256], reads=[t1b], is_output=True)
                    if blk >= 8:
                        t2, t2b = tmpB.next()
                        cp("act", t2[0:ntp, 0:256], t1[0:ntp, 0:256], [t1b], [t2b])
                        if prompt:
                            P.dma("pool", Vscr[t0 + b * 128:t0 + (b + 1) * 128, col:col + 256], t2[:, 0:256], reads=[t2b], writes=[B_Vscr])
                        else:
                            P.dma("pool", Vs[:, col:col + 256], t2[0:64, 0:256], reads=[t2b], writes=[B_Vs])
            if prompt:
                P.dma("pool", KTscr[:, t0:t0 + TW].rearrange("(c p) t -> p c t", p=128), sA[:, :, 0:TW], reads=BsA, writes=[B_KTscr])
            else:
                P.dma("pool", KTs.rearrange("(c p) t -> p c t", p=128), sA[:, :, 0:64], reads=BsA, writes=[B_KTs])
            st_(6)
            ffn(2, TW); layer_norm(3, TW)

            def evq(c, bank, bb):
                cp("act" if c % 2 == 0 else "dve", sA[:, c, 0:TW], bank[:, 0:TW], [bb], [BsA[c]])
            proj_fm([[(UA_Q + blk, xb, Bxb)] for blk in range(8)], TW, evq)
            st_(7)
            attention(TW, prompt, t0)
            st_(8)
            proj_fm([[(UA_DWO + blk, sB, BsB)] for blk in range(8)], TW, resid_evac(TW))
            layer_norm(4, TW)
            ffn(3, TW); layer_norm(5, TW)
            for b in range(nb):
                for g4 in range(4):
                    bank, bb = G.next()
                    for i in range(4):
                        c = g4 * 4 + i
                        tr(bank[0:ntp, i * 128:(i + 1) * 128], xres[:, c, b * 128:b * 128 + ntp], ident, [Bx[c], Bc], [bb])
                    cp("act" if g4 % 2 == 0 else "dve", tokst[0:ntp, b, g4 * 512:(g4 + 1) * 512], bank[0:ntp, 0:512], [bb], Btk[b][g4 * 4:g4 * 4 + 4])
                if prompt:
                    P.dma("pool", yp[t0 + b * 128:t0 + (b + 1) * 128, :], tokst[:, b, :], reads=Btk[b], is_output=True)
                else:
                    P.dma("pool", ys, tokst[0:NSAMP, 0, :], reads=Btk[0], is_output=True)

        def state_in():
            mset("pool", tokst[:, 0, :], 0.0, Btk[0])
            tk = tokst[:, 0, :].rearrange("p (c f) -> p c f", c=16)
            for h in range(2):
                hs = slice(h * 64, h * 64 + 64)
                P.dma("pool", tk[hs, :, h * 64:h * 64 + 64], swkv[hs, :, :], writes=Btk[0])
            for c in range(16):
                bank, bb = G.next()
                tr(bank[:, 0:128], tk[:, c, :], ident, Btk[0] + [Bc], [bb])
                cp("act" if c % 2 == 0 else "dve", Hst[:, c, :], bank[:, 0:128], [bb], [BH[c]])
            P.dma("pool", carry[:], sshift, writes=[Bcar])

        def state_out(dst):
            tk = tokst[:, 1, :].rearrange("p (c f) -> p c f", c=16)
            for c in range(16):
                bank, bb = G.next()
                tr(bank[:, 0:128], Hst[:, c, :], ident, [BH[c], Bc], [bb])
                cp("act" if c % 2 == 0 else "dve", tk[:, c, :], bank[:, 0:128], [bb], Btk[1])
            for h in range(2):
                hs = slice(h * 64, h * 64 + 64)
                P.dma("pool", dst[hs, :, :], tk[hs, :, h * 64:h * 64 + 64], reads=Btk[1], is_output=True)

        def main_all():
            stage(1)
            state_in()
            stage(11)
            tile_pass(False, 0)
            stage(9)
            state_out(wkvs)
            stage(10)
            for c in range(16):
                mset("pool", Hst[:, c, :], 0.0, [BH[c]])
            mset("pool", carry[:], 0.0, [Bcar])
            for it in range(NT):
                tile_pass(True, it)
            state_out(wkvp)

        keep = P.dry
        P.dry = True
        main_all()
        P.dry = keep
        REAL[0] = True
        main_all()
        P.dry = False
        print('OPCOUNTS', P.cnt, flush=True)
        P.emit()
    return nc


def _unitsA(W):
    n = W.shape[1] // 256
    return np.ascontiguousarray(W.reshape(16, 128, n, 256).transpose(2, 1, 0, 3)).reshape(n, 128, 4096)


def _unitsB(W):
    return np.ascontiguousarray(W.reshape(4, 11, 128, 8, 256).transpose(3, 0, 2, 1, 4)).reshape(32, 128, BW)


def _col(v):
    return np.ascontiguousarray(v.reshape(16, 128).T)


def _consts():
    c = np.zeros((128, C_W), np.float32)
    c[:, C_ID:C_ID + 128] = np.eye(128, dtype=np.float32)
    blk = np.zeros((128, 128), np.float32)
    blk[:64, :64] = 1; blk[64:, 64:] = 1
    c[:, C_BO:C_BO + 128] = blk
    i = np.arange(128)
    same = (i[:, None] // 64) == (i[None, :] // 64)
    r, cc = i[:, None] % 64, i[None, :] % 64
    SL = (same & (cc < r)).astype(np.float32)
    SU = (same & (r < cc)).astype(np.float32)
    UI = (same & (r <= cc)).astype(np.float32)
    c[:, C_MSL:C_MSL + 128] = -SL
    c[:, C_MK:C_MK + 128] = SU; c[:, C_MK + 128:C_MK + 256] = UI
    c[:, C_MB:C_MB + 128] = -SU; c[:, C_MB + 128:C_MB + 256] = -UI
    j = np.arange(64)
    c[:, C_TI:C_TI + 64] = np.tile((j[:, None] <= j[None, :]).astype(np.float32), (2, 1))
    c[:, C_TE:C_TE + 64] = np.tile((j[:, None] < j[None, :]).astype(np.float32), (2, 1))
    return c


def _shared_inputs(inp):
    f = lambda a: np.asarray(a, np.float32)
    pv = np.zeros((128, NPV, 16), np.float32)
    ln_g, ln_b = f(inp["ln_g"]).reshape(6, D), f(inp["ln_b"]).reshape(6, D)
    for i in range(6):
        pv[:, PV_LNG + i] = _col(ln_g[i]); pv[:, PV_LNB + i] = _col(ln_b[i])
    pv[:, PV_A0] = _col(f(inp["rwkv_a0"])[0]); pv[:, PV_KK] = _col(f(inp["rwkv_k_k"])[0])
    pv[:, PV_KA] = _col(f(inp["rwkv_k_a"])[0]); pv[:, PV_RK] = _col(f(inp["rwkv_r_k"])[0].reshape(D))
    pv[:, PV_LXG] = _col(f(inp["rwkv_lnx_g"])[0]); pv[:, PV_LXB] = _col(f(inp["rwkv_lnx_b"])[0])
    mu = f(inp["rwkv_mu"])[0]
    for j in range(6):
        pv[:, PV_MU + j] = _col(mu[j])
    wA = np.empty((NSA, 128, 4096), np.float32)
    w_in = f(inp["ffn_w_in"]); w_out = f(inp["ffn_w_out"])
    for i in range(2):
        for s in range(2):
            wgu = np.concatenate([w_in[i, s][:, :FF].reshape(D, MC, 128), w_in[i, s][:, FF:].reshape(D, MC, 128)], axis=2).reshape(D, 2 * FF)
            wA[SA_FFN + (i * 2 + s) * 44: SA_FFN + (i * 2 + s + 1) * 44] = _unitsA(wgu)
    rkvw = f(inp["rwkv_w_rkv"])[0]
    for j in range(3):
        wA[SA_RKV + j * 8: SA_RKV + (j + 1) * 8] = _unitsA(rkvw[j])
    L1 = np.zeros((D, 256), np.float32)
    L1[:, 32:128] = f(inp["rwkv_w1"])[0]; L1[:, 128:224] = f(inp["rwkv_a1"])[0]
    wA[SA_L1:SA_L1 + 1] = _unitsA(L1)
    wA[SA_G1:SA_G1 + 1] = _unitsA(f(inp["rwkv_g1"])[0])
    wA[SA_WO:SA_WO + 8] = _unitsA(f(inp["rwkv_w_o"])[0])
    wA[SA_KV:SA_KV + 16] = _unitsA(f(inp["kv_w"]))
    wA[SA_Q:SA_Q + 8] = _unitsA(f(inp["diff_w_q"])[0])
    wA[SA_DWO:SA_DWO + 8] = _unitsA(f(inp["diff_w_o"])[0])
    wB = np.empty((NUB, 128, BW), np.float32)
    for i in range(2):
        for s in range(2):
            wB[(i * 2 + s) * 32:(i * 2 + s + 1) * 32] = _unitsB(w_out[i, s])
    w2aug = np.zeros((128, D), np.float32)
    w2aug[0] = f(inp["rwkv_w0"])[0]; w2aug[32:128] = f(inp["rwkv_w2"])[0]
    a2p = np.zeros((128, D), np.float32); a2p[0:96] = f(inp["rwkv_a2"])[0]
    g2p = np.ascontiguousarray(f(inp["rwkv_g2"])[0].reshape(2, 128, D).transpose(1, 0, 2))
    return {"pcols": pv, "wA_src": wA, "wB_src": wB, "w2aug": w2aug, "a2p": a2p, "g2p": g2p, "consts": _consts(),
            "dlam": f(inp["diff_lambda"]).reshape(1, 256), "subg": f(inp["diff_subln_g"]).reshape(1, 128)}


def _state_layout(S):
    return np.ascontiguousarray(S.reshape(16, 2, 64, 64).transpose(1, 2, 0, 3)).reshape(128, 16, 64)


def _state_unlayout(A):
    return np.ascontiguousarray(A.reshape(2, 64, 16, 64).transpose(2, 0, 1, 3)).reshape(32, 64, 64)


_NC_CACHE = {}


def kernel(**inp):
    f = lambda a: np.asarray(a, np.float32)
    x_prompt = f(inp["x_prompt"]); x_sample = f(inp["x_sample"])
    NB, SEQ = x_prompt.shape[0], x_prompt.shape[1]
    shared = _shared_inputs(inp)
    ck = f(inp["cache_k"]).reshape(NB, PAST, D); cv = f(inp["cache_v"]).reshape(NB, PAST, D)
    swkv = f(inp["state_wkv"])[0]; sshift = f(inp["state_shift"])[0]
    in_maps = []
    for i in range(NB):
        m = dict(shared)
        m.update({"xp": x_prompt[i], "xs": x_sample[i], "ck": ck[i], "cv": cv[i],
                  "swkv": _state_layout(swkv[i]), "sshift": _col(sshift[i])})
        in_maps.append(m)
    if SEQ not in _NC_CACHE:
        _NC_CACHE[SEQ] = build(SEQ)
    nc = _NC_CACHE[SEQ]
    res = run_bass_kernel_spmd(nc, in_maps, core_ids=list(range(NB)))
    R = res.results
    g = lambda k: np.stack([np.asarray(R[i][k], np.float32) for i in range(NB)])
    y_prompt = g("yp"); y_sample = g("ys")
    k_prompt = g("kp").reshape(NB, SEQ, 32, 64); v_prompt = g("vp").reshape(NB, SEQ, 16, 128)
    k_sample = g("ksm").reshape(NB, NSAMP, 32, 64); v_sample = g("vsm").reshape(NB, NSAMP, 16, 128)
    wkv_prompt = np.stack([_state_unlayout(np.asarray(R[i]["wkvp"], np.float32)) for i in range(NB)])[None]
    wkv_sample = np.stack([_state_unlayout(np.asarray(R[i]["wkvs"], np.float32)) for i in range(NB)])[None]
    unc = lambda a: np.ascontiguousarray(a.T).reshape(D)
    shift_prompt = np.stack([unc(np.asarray(R[i]["shp"], np.float32)) for i in range(NB)])[None]
    shift_sample = np.stack([unc(np.asarray(R[i]["shs"], np.float32)) for i in range(NB)])[None]
    return (y_prompt, y_sample, k_prompt, v_prompt, wkv_prompt, shift_prompt,
            k_sample, v_sample, wkv_sample, shift_sample)
```

```python
import math
from contextlib import ExitStack
import numpy as np
import concourse.bass as bass
import concourse.mybir as mybir
from concourse.bass_utils import run_bass_kernel_spmd

F32 = mybir.dt.float32
BF16 = mybir.dt.bfloat16
ALU = mybir.AluOpType
AF = mybir.ActivationFunctionType
AX = mybir.AxisListType

ENGS = ("pe", "act", "dve", "pool", "sp")
N_DMA_SEMS = 48
DMA_HALF = 24

D = 2048
DC = 16
FF = 5632
MC = 44
TWP = 256
PAST = 1024
NSAMP = 16
ALPHA = (2.0 * 2) ** 0.25
LN_EPS = 1e-5
GN_EPS = 64e-5
LAMBDA_INIT = 0.8 - 0.6 * math.exp(-0.3 * 1)
NEG_E = -math.exp(-0.5)

PV_LNG, PV_LNB = 0, 6
PV_A0, PV_KK, PV_KA, PV_RK, PV_LXG, PV_LXB, PV_MU = 12, 13, 14, 15, 16, 17, 18
NPV = 24
C_ID, C_BO, C_MSL, C_MK, C_MB, C_TI, C_TE, C_W = 0, 128, 256, 384, 640, 896, 960, 1024

UA_FFN = 0
UA_RKV = 176
UA_RKVS = 200
UA_L1, UA_G1, UA_L1S, UA_G1S = 224, 225, 226, 227
UA_WO, UA_KV, UA_Q, UA_DWO = 228, 236, 252, 260
NUA = 268
SA_FFN, SA_RKV, SA_L1, SA_G1, SA_WO, SA_KV, SA_Q, SA_DWO, NSA = 0, 176, 200, 201, 202, 210, 226, 234, 242
NUB = 128
BW = 2816


class Buf:
    __slots__ = ("name", "lw", "rd")

    def __init__(self, name=""):
        self.name = name
        self.lw = None
        self.rd = {}


class Prog:
    def __init__(self, nc):
        self.nc = nc
        self.streams = {e: [] for e in ENGS}
        self.cnt = {e: 0 for e in ENGS}
        self.known = {e: {} for e in ENGS}
        self.sem = {}
        for e in ("pe", "act", "dve", "pool"):
            self.sem[e] = nc.alloc_semaphore(name="c_" + e)
        self.dsem = [nc.alloc_semaphore(name="d_%d" % i) for i in range(N_DMA_SEMS)]
        self.dval = [0] * N_DMA_SEMS
        self.drr = {"sp": 0, "pool": 0}
        self.out_events = {}
        self.n_ops = 0
        self.dry = False

    def _deps(self, eng, reads, writes):
        deps = {}
        for b in list(reads) + list(writes):
            if b.lw is not None:
                k, v = b.lw
                if deps.get(k, 0) < v:
                    deps[k] = v
        for b in writes:
            for k, v in b.rd.items():
                if deps.get(k, 0) < v:
                    deps[k] = v
        waits = []
        kn = self.known[eng]
        for k, v in deps.items():
            if eng == "pe" and k == "pe":
                continue
            if kn.get(k, 0) >= v:
                continue
            kn[k] = v
            waits.append((k, v))
        return waits

    def _commit(self, ev, reads, writes):
        k, v = ev
        for b in reads:
            if b.rd.get(k, 0) < v:
                b.rd[k] = v
        for b in writes:
            b.lw = ev
            b.rd = {}

    def _semh(self, k):
        return self.sem[k] if isinstance(k, str) else self.dsem[k]

    def op(self, eng, fn, reads=(), writes=()):
        if self.dry:
            return
        waits = self._deps(eng, reads, writes)
        self.cnt[eng] += 1
        ev = (eng, self.cnt[eng])
        self.streams[eng].append((waits, fn, (eng, 1)))
        self._commit(ev, reads, writes)
        self.n_ops += 1

    def dma(self, q, out_ap, in_ap, reads=(), writes=(), is_output=False, **kw):
        if self.dry:
            return
        i = self.drr[q] + (0 if q == "sp" else DMA_HALF)
        self.drr[q] = (self.drr[q] + 1) % DMA_HALF
        waits = self._deps(q, reads, writes)
        kn = self.known[q]
        if self.dval[i] > 0 and kn.get(i, 0) < self.dval[i]:
            kn[i] = self.dval[i]
            waits.append((i, self.dval[i]))
        self.dval[i] += 16
        ev = (i, self.dval[i])

        def fn(e, out_ap=out_ap, in_ap=in_ap, kw=kw):
            return e.dma_start(out=out_ap, in_=in_ap, **kw)
        self.streams[q].append((waits, fn, (i, 16)))
        self._commit(ev, reads, writes)
        if is_output:
            self.out_events[i] = self.dval[i]
        self.n_ops += 1

    def barrier(self):
        for e in ENGS:
            waits = []
            kn = self.known[e]
            for k in ("pe", "act", "dve", "pool"):
                if k != e and kn.get(k, 0) < self.cnt[k]:
                    kn[k] = self.cnt[k]
                    waits.append((k, self.cnt[k]))
            for i in range(N_DMA_SEMS):
                if kn.get(i, 0) < self.dval[i]:
                    kn[i] = self.dval[i]
                    waits.append((i, self.dval[i]))
            self.streams[e].append((waits, None, None))

    def finish(self):
        waits = []
        for i, v in self.out_events.items():
            if self.known["sp"].get(i, 0) < v:
                waits.append((i, v))
        self.streams["sp"].append((waits, None, None))

    def _replay(self, eng, e):
        for waits, fn, inc in self.streams[eng]:
            for k, v in waits:
                e.wait_ge(self._semh(k), v)
            if fn is not None:
                ins = fn(e)
                ins.then_inc(self._semh(inc[0]), inc[1])

    def emit(self):
        self.finish()
        nc = self.nc
        with nc.Block() as block:
            @block.tensor
            def _(e):
                self._replay("pe", e)

            @block.scalar
            def _(e):
                self._replay("act", e)

            @block.vector
            def _(e):
                self._replay("dve", e)

            @block.gpsimd
            def _(e):
                self._replay("pool", e)

            @block.sync
            def _(e):
                self._replay("sp", e)


class Rot:
    def __init__(self, items):
        self.items = items
        self.i = 0

    def next(self):
        it = self.items[self.i % len(self.items)]
        self.i += 1
        return it


def build(SEQ):
    import os
    STOP = int(os.environ.get('MK_STOP', '99'))
    REAL = [False]
    NT = SEQ // TWP
    nc = bass.Bass("TRN2", target_bir_lowering=False)

    def din(name, shape, dt=F32):
        return nc.dram_tensor(name, shape, dt, kind="ExternalInput").ap()

    def dout(name, shape):
        return nc.dram_tensor(name, shape, F32, kind="ExternalOutput").ap()

    def dscr(name, shape, dt):
        return nc.dram_tensor(name, shape, dt, kind="Internal").ap()

    xp = din("xp", [SEQ, D]); xs = din("xs", [NSAMP, D])
    ck = din("ck", [PAST, D]); cv = din("cv", [PAST, D])
    swkv = din("swkv", [128, 16, 64]); sshift = din("sshift", [128, 16])
    pcols_d = din("pcols", [128, NPV, 16])
    wA_src = din("wA_src", [NSA, 128, 4096]); wB_src = din("wB_src", [NUB, 128, BW])
    w2aug_d = din("w2aug", [128, D]); a2p_d = din("a2p", [128, D]); g2p_d = din("g2p", [128, 2, D])
    consts_d = din("consts", [128, C_W]); dlam_d = din("dlam", [1, 256]); subg_d = din("subg", [1, 128])

    yp = dout("yp", [SEQ, D]); ys = dout("ys", [NSAMP, D])
    kp = dout("kp", [SEQ, D]); vp = dout("vp", [SEQ, D])
    wkvp = dout("wkvp", [128, 16, 64]); shp = dout("shp", [128, 16])
    ksm = dout("ksm", [NSAMP, D]); vsm = dout("vsm", [NSAMP, D])
    wkvs = dout("wkvs", [128, 16, 64]); shs = dout("shs", [128, 16])

    wA_parts = [dscr("wA%d" % i, [67, 128, 4096], BF16) for i in range(4)]
    wA = [wA_parts[i // 67][i % 67] for i in range(NUA)]
    wB = dscr("wB", [NUB, 128, BW], BF16)
    KTscr = dscr("KTscr", [D, SEQ], BF16); Vscr = dscr("Vscr", [SEQ, D], BF16)
    KTs = dscr("KTs", [D, 64], BF16); Vs = dscr("Vs", [64, D], BF16)
    B_wA = [Buf() for _ in range(NUA)]; B_wB = [Buf() for _ in range(NUB)]
    B_KTscr = Buf(); B_Vscr = Buf(); B_KTs = Buf(); B_Vs = Buf()

    P = Prog(nc)

    def stage(n):
        if REAL[0] and STOP == n:
            P.dry = True

    def bar():
        if not P.dry:
            P.barrier()

    def mm(out, lhsT, rhs, start, stop, R, W):
        P.op("pe", lambda e: e.matmul(out, lhsT, rhs, start=start, stop=stop), R, W)

    def tr(out, in_, ident, R, W):
        P.op("pe", lambda e: e.transpose(out, in_, ident), R, W)

    def tt(eng, out, a, b, op, R, W):
        P.op(eng, lambda e: e.tensor_tensor(out, a, b, op), R, W)

    def ts(eng, out, a, s1, s2, op0, op1, R, W):
        if op1 is None:
            P.op(eng, lambda e: e.tensor_scalar(out, a, s1, None, op0), R, W)
        else:
            P.op(eng, lambda e: e.tensor_scalar(out, a, s1, s2, op0, op1), R, W)

    def stt(eng, out, in0, scalar, in1, op0, op1, R, W):
        P.op(eng, lambda e: e.scalar_tensor_tensor(out, in0, scalar, in1, op0, op1), R, W)

    def act(out, in_, func, R, W, **kw):
        P.op("act", lambda e: e.activation(out, in_, func, **kw), R, W)

    def cp(eng, out, in_, R, W):
        if eng == "act":
            P.op("act", lambda e: e.copy(out, in_), R, W)
        else:
            P.op(eng, lambda e: e.tensor_copy(out, in_), R, W)

    def mset(eng, ap, val, W):
        P.op(eng, lambda e: e.memset(ap, val), (), W)

    def recip(out, in_, R, W):
        P.op("dve", lambda e: e.reciprocal(out, in_), R, W)

    with ExitStack() as es0:
        def sb0(name, shape, dt=F32):
            return es0.enter_context(nc.sbuf_tensor(name, shape, dt))
        mu_t = sb0("mu_t", [128, 6, 16]); B_mu = Buf()
        P.dma("sp", mu_t[:], pcols_d[:, PV_MU:PV_MU + 6, :], writes=[B_mu])
        st32 = [(sb0("st32_%d" % i, [128, 4096]), Buf()) for i in range(3)]
        st16 = [(sb0("st16_%d" % i, [128, 4096], BF16), Buf()) for i in range(4)]
        r32 = Rot(st32); r16 = Rot(st16); reng = Rot(["dve", "act", "pool"])
        rq = Rot(["sp", "pool"])

        def conv_A(src_idx, dst_idx, scale=None):
            t32, b32 = r32.next()
            P.dma("sp", t32[:, 0:4096], wA_src[src_idx], writes=[b32])
            t16, b16 = r16.next()
            eng = reng.next()
            cp(eng, t16[:, 0:4096], t32[:, 0:4096], [b32], [b16])
            P.dma(rq.next(), wA[dst_idx], t16[:, 0:4096], reads=[b16], writes=[B_wA[dst_idx]])
            if scale is not None:
                dsts, specs = scale
                t16s, b16s = r16.next()
                for dc in range(DC):
                    for (c0, c1, mj) in specs:
                        eng = "dve" if (dc % 2 == 0) else "pool"
                        ts(eng, t16s[:, dc * 256 + c0: dc * 256 + c1], t32[:, dc * 256 + c0: dc * 256 + c1],
                           mu_t[:, mj, dc:dc + 1], None, ALU.mult, None, [b32, B_mu], [b16s])
                P.dma(rq.next(), wA[dsts], t16s[:, 0:4096], reads=[b16s], writes=[B_wA[dsts]])

        for u in range(176):
            conv_A(SA_FFN + u, UA_FFN + u)
        mu_of = [0, 2, 3]
        for j in range(3):
            for blk in range(8):
                conv_A(SA_RKV + j * 8 + blk, UA_RKV + j * 8 + blk, (UA_RKVS + j * 8 + blk, [(0, 256, mu_of[j])]))
        conv_A(SA_L1, UA_L1, (UA_L1S, [(0, 128, 1), (128, 256, 4)]))
        conv_A(SA_G1, UA_G1, (UA_G1S, [(0, 256, 5)]))
        for blk in range(8):
            conv_A(SA_WO + blk, UA_WO + blk)
        for blk in range(16):
            conv_A(SA_KV + blk, UA_KV + blk)
        for blk in range(8):
            conv_A(SA_Q + blk, UA_Q + blk)
        for blk in range(8):
            conv_A(SA_DWO + blk, UA_DWO + blk)
        for u in range(NUB):
            t32, b32 = r32.next()
            P.dma("sp", t32[:, 0:BW], wB_src[u], writes=[b32])
            t16, b16 = r16.next()
            if u % 2 == 0:
                P.op("act", lambda e, o=t16, i=t32: e.mul(o[:, 0:BW], i[:, 0:BW], 0.5), [b32], [b16])
            else:
                ts("dve", t16[:, 0:BW], t32[:, 0:BW], 0.5, None, ALU.mult, None, [b32], [b16])
            P.dma(rq.next(), wB[u], t16[:, 0:BW], reads=[b16], writes=[B_wB[u]])
        P.barrier()
    if STOP == 0:
        P.dry = True

    es = ExitStack()

    def sb(name, shape, dt=F32):
        return es.enter_context(nc.sbuf_tensor(name, shape, dt))

    with es:
        xres = sb("xres", [128, DC, TWP]); Bx = [Buf() for _ in range(DC)]
        xb = sb("xb", [128, DC, TWP], BF16); Bxb = [Buf() for _ in range(DC)]
        hid = sb("hid", [128, 48, TWP], BF16); Bh = [Buf() for _ in range(48)]
        sA = sb("sA", [128, DC, TWP], BF16); BsA = [Buf() for _ in range(DC)]
        sB = sb("sB", [128, DC, TWP], BF16); BsB = [Buf() for _ in range(DC)]
        tokst = sb("tokst", [128, 2, D]); Btk = [[Buf() for _ in range(DC)] for _ in range(2)]
        wslots = [(sb("wslot%d" % i, [128, 4096], BF16), Buf()) for i in range(3)]
        NS = len(wslots)
        banks = [(es.enter_context(nc.psum_tensor("bank%d" % i, [128, 512], F32)), Buf()) for i in range(8)]
        G = Rot([banks[i] for i in (0, 1, 2, 3)])
        Obanks = [banks[4], banks[5], banks[6], banks[7]]
        cst = sb("cst", [128, C_W]); Bc = Buf()
        pc = sb("pc", [128, NPV, 16]); Bpc = Buf()
        omka = sb("omka", [128, 16]); Bomka = Buf()
        w2aug = sb("w2aug_s", [128, D]); Bw2 = Buf()
        a2b = sb("a2b", [128, D], BF16); Ba2 = Buf()
        g2b = sb("g2b", [128, 2, D], BF16); Bg2 = Buf()
        Hst = sb("Hst", [128, 16, 128]); BH = [Buf() for _ in range(16)]
        carry = sb("carry", [128, 16]); Bcar = Buf()
        negv = sb("negv", [128, 2]); Bnegv = Buf()
        lamt = sb("lamt", [128, 256]); Blam = Buf()
        lamc = sb("lamc", [128, 8]); Blamc = Buf()
        subg = sb("subg_s", [128, 128]); Bsubg = Buf()
        hTw = sb("hTw", [128, TWP]); BhTw = Buf()
        hTa = sb("hTa", [128, TWP], BF16); BhTa = Buf()
        hTg = sb("hTg", [128, 2, TWP], BF16); BhTg = Buf()
        stat = [(sb("stat%d" % i, [128, TWP]), Buf()) for i in range(4)]
        tmpA = Rot([(sb("tmpA%d" % i, [128, TWP]), Buf()) for i in range(3)])
        tmpB = Rot([(sb("tmpB%d" % i, [128, TWP], BF16), Buf()) for i in range(3)])
        RW = {n: (sb("rw_" + n, [128, TWP]), Buf()) for n in
              ["a", "g", "pinc", "pexc", "pinv", "kkr", "rn", "kk", "kmod", "b", "bv", "yT", "yc"]}
        RW["sq"], RW["tmp"], RW["rk"], RW["rstd"] = stat[0], stat[1], stat[2], stat[3]
        logd = sb("logd", [128, 2, 128]); Blogd = Buf()
        NCHM = TWP // 64
        RKfp = [(sb("RKfp%d" % i, [128, NCHM, 2, 128]), Buf()) for i in range(1)]
        Ktfp = [(sb("Ktfp%d" % i, [128, NCHM, 128]), Buf()) for i in range(1)]
        Btfp = [(sb("Btfp%d" % i, [128, NCHM, 128]), Buf()) for i in range(1)]
        Vtfp = [(sb("Vtfp%d" % i, [128, NCHM, 128]), Buf()) for i in range(1)]
        cmL = [(sb("cmL%d" % i, [128, 256]), Buf()) for i in range(4)]
        cmX = Rot([(sb("cmX%d" % i, [128, 128]), Buf()) for i in range(3)])
        cmM = Rot([(sb("cmM%d" % i, [128, 256]), Buf()) for i in range(3)])
        ktp = Rot([(sb("ktp%d" % i, [128, 1024], BF16), Buf()) for i in range(2)])
        vtp_items = [(sb("vtp%d" % i, [128, 8, 129], BF16), Buf()) for i in range(2)]
        vtp = Rot(vtp_items)
        ptp = Rot([(sb("ptp%d" % i, [128, 512], BF16), Buf()) for i in range(3)])
        ckst = sb("ckst", [128, 8, 128]); Bckst = Buf()
        osm = Rot([(sb("osm%d" % i, [128, 136]), Buf()) for i in range(4)])

        ones_full = sb("ones_full", [128, 128]); Bones = Buf()
        mset("pool", ones_full[:], 1.0, [Bones])
        ident = cst[:, C_ID:C_ID + 128]
        bones = cst[:, C_BO:C_BO + 128]
        mSLn = cst[:, C_MSL:C_MSL + 128]
        mK = cst[:, C_MK:C_MK + 256]
        mBn = cst[:, C_MB:C_MB + 256]
        triI = cst[:, C_TI:C_TI + 64]
        triE = cst[:, C_TE:C_TE + 64]

        P.dma("sp", cst[:], consts_d, writes=[Bc])
        P.dma("sp", pc[:], pcols_d, writes=[Bpc])
        P.dma("sp", w2aug[:], w2aug_d, writes=[Bw2])
        P.dma("sp", tokst[:, 0, :], a2p_d, writes=Btk[0])
        cp("dve", a2b[:], tokst[:, 0, :], Btk[0], [Ba2])
        for kc in range(2):
            P.dma("sp", tokst[:, 1, :], g2p_d[:, kc, :], writes=Btk[1])
            cp("dve", g2b[:, kc, :], tokst[:, 1, :], Btk[1], [Bg2])
        ts("dve", omka[:], pc[:, PV_KA, :], -1.0, 1.0, ALU.mult, ALU.add, [Bpc], [Bomka])
        mset("pool", negv[:], NEG_E, [Bnegv])
        P.op("pool", lambda e: e.affine_select(negv[:, 1:2], negv[:, 1:2], pattern=[[0, 1]], compare_op=ALU.is_gt,
                                               fill=0.0, base=NSAMP, channel_multiplier=-1), [Bnegv], [Bnegv])
        for (t_, b_) in RKfp + Ktfp + Btfp + Vtfp:
            mset("pool", t_[:], 0.0, [b_])
        for (t_, b_) in vtp_items:
            mset("pool", t_[:], 1.0, [b_])
        P.dma("sp", lamt[:], dlam_d.partition_broadcast(128), writes=[Blam])
        P.dma("sp", subg[:], subg_d.partition_broadcast(128), writes=[Bsubg])
        tt("dve", lamt[:, 0:64], lamt[:, 0:64], lamt[:, 64:128], ALU.mult, [Blam], [Blam])
        tt("dve", lamt[:, 128:192], lamt[:, 128:192], lamt[:, 192:256], ALU.mult, [Blam], [Blam])
        P.op("dve", lambda e: e.reduce_sum(lamc[:, 0:1], lamt[:, 0:64], axis=AX.X), [Blam], [Blamc])
        P.op("dve", lambda e: e.reduce_sum(lamc[:, 1:2], lamt[:, 128:192], axis=AX.X), [Blam], [Blamc])
        act(lamc[:, 2:4], lamc[:, 0:2], AF.Exp, [Blamc], [Blamc])
        tt("dve", lamc[:, 4:5], lamc[:, 3:4], lamc[:, 2:3], ALU.subtract, [Blamc], [Blamc])
        ts("dve", lamc[:, 5:6], lamc[:, 4:5], -LAMBDA_INIT, None, ALU.add, None, [Blamc], [Blamc])
        ts("dve", subg[:], subg[:], 1.0 - LAMBDA_INIT, None, ALU.mult, None, [Bsubg], [Bsubg])
        neglam = lamc[:, 5:6]

        class WS:
            seq = []
            pos = 0
            issued = 0

        def wissue(i):
            kind, idx = WS.seq[i]
            t_, b_ = wslots[i % NS]
            if kind == "A":
                P.dma("sp", t_[:, 0:4096], wA[idx], reads=[B_wA[idx]], writes=[b_])
            else:
                P.dma("sp", t_[:, 0:BW], wB[idx], reads=[B_wB[idx]], writes=[b_])

        def wget(kind, idx, hold=0):
            if P.dry:
                WS.seq.append((kind, idx))
                return wslots[0]
            assert WS.seq[WS.pos] == (kind, idx)
            while WS.issued < min(len(WS.seq), WS.pos + NS - hold):
                wissue(WS.issued)
                WS.issued += 1
            r = wslots[WS.pos % NS]
            WS.pos += 1
            return r

        def uA(slot):
            return slot[:, 0:4096].rearrange("p (c f) -> p c f", c=DC)

        def uB(slot):
            return slot[:, 0:BW].rearrange("p (m d) -> p m d", m=11)

        def proj_fm(unit_list, TW, evac):
            for blk, parts in enumerate(unit_list):
                slots = [(wget("A", ui, hold=pi), src, sbufs) for pi, (ui, src, sbufs) in enumerate(parts)]
                for half in range(2):
                    bank, bb = G.next()
                    n = len(slots) * DC
                    k = 0
                    for (st, sbf), src, sbufs in slots:
                        u = uA(st)
                        for dc in range(DC):
                            mm(bank[:, 0:TW], u[:, dc, half * 128:(half + 1) * 128], src[:, dc, 0:TW],
                               k == 0, k == n - 1, [sbf, sbufs[dc]], [bb])
                            k += 1
                    evac(2 * blk + half, bank, bb)

        def layer_norm(li, TW):
            S, Sb = G.next()
            S2, S2b = G.next()
            for c in range(DC):
                sq, sqb = tmpA.next()
                act(sq[:, 0:TW], xres[:, c, 0:TW], AF.Square, [Bx[c]], [sqb])
                mm(S[:, 0:TW], ones_full[:], xres[:, c, 0:TW], c == 0, c == DC - 1, [Bones, Bx[c]], [Sb])
                mm(S2[:, 0:TW], ones_full[:], sq[:, 0:TW], c == 0, c == DC - 1, [Bones, sqb], [S2b])
            (m_, mb), (q_, qb_), (v_, vb), (r_, rb) = stat[0], stat[1], stat[2], stat[3]
            ts("dve", m_[:, 0:TW], S[:, 0:TW], 1.0 / D, None, ALU.mult, None, [Sb], [mb])
            tt("dve", q_[:, 0:TW], m_[:, 0:TW], m_[:, 0:TW], ALU.mult, [mb], [qb_])
            ts("dve", v_[:, 0:TW], S2[:, 0:TW], 1.0 / D, LN_EPS, ALU.mult, ALU.add, [S2b], [vb])
            tt("dve", v_[:, 0:TW], v_[:, 0:TW], q_[:, 0:TW], ALU.subtract, [vb, qb_], [vb])
            P.op("act", lambda e: e.sqrt(v_[:, 0:TW], v_[:, 0:TW]), [vb], [vb])
            recip(r_[:, 0:TW], v_[:, 0:TW], [vb], [rb])
            for c in range(DC):
                t1, t1b = tmpA.next()
                tt("dve", t1[:, 0:TW], xres[:, c, 0:TW], m_[:, 0:TW], ALU.subtract, [Bx[c], mb], [t1b])
                tt("pool", t1[:, 0:TW], t1[:, 0:TW], r_[:, 0:TW], ALU.mult, [t1b, rb], [t1b])
                act(xres[:, c, 0:TW], t1[:, 0:TW], AF.Identity, [t1b, Bpc], [Bx[c]],
                    scale=pc[:, PV_LNG + li, c:c + 1], bias=pc[:, PV_LNB + li, c:c + 1])
                ts("dve", xb[:, c, 0:TW], t1[:, 0:TW], pc[:, PV_LNG + li, c:c + 1], pc[:, PV_LNB + li, c:c + 1],
                   ALU.mult, ALU.add, [t1b, Bpc], [Bxb[c]])


        def ffn(fi, TW):
            for m in range(MC):
                sg, sgb = wget("A", UA_FFN + fi * 44 + m)
                ug = uA(sg)
                bank, bb = G.next()
                bank2, bb2 = G.next()
                for dc in range(DC):
                    mm(bank[:, 0:TW], ug[:, dc, 0:128], xb[:, dc, 0:TW], dc == 0, dc == DC - 1, [sgb, Bxb[dc]], [bb])
                for dc in range(DC):
                    mm(bank2[:, 0:TW], ug[:, dc, 128:256], xb[:, dc, 0:TW], dc == 0, dc == DC - 1, [sgb, Bxb[dc]], [bb2])
                t1, t1b = tmpA.next()
                act(t1[:, 0:TW], bank[:, 0:TW], AF.Silu, [bb], [t1b])
                tt("dve", hid[:, m, 0:TW], t1[:, 0:TW], bank2[:, 0:TW], ALU.mult, [t1b, bb2], [Bh[m]])
            for g in range(8):
                (bA, bAb) = G.next()
                (bB, bBb) = G.next()
                accs = [(bA[:, 0:TW], bAb), (bB[:, 0:TW], bBb)]
                for q in range(4):
                    sw, swb = wget("B", fi * 32 + g * 4 + q)
                    u = uB(sw)
                    for oo in range(2):
                        for mm_ in range(11):
                            m = q * 11 + mm_
                            mm(accs[oo][0], u[:, mm_, oo * 128:(oo + 1) * 128], hid[:, m, 0:TW],
                               q == 0 and mm_ == 0, q == 3 and mm_ == 10, [swb, Bh[m]], [accs[oo][1]])
                for oo in range(2):
                    c = g * 2 + oo
                    stt("dve", xres[:, c, 0:TW], xres[:, c, 0:TW], ALPHA, accs[oo][0], ALU.mult, ALU.add, [Bx[c], accs[oo][1]], [Bx[c]])

        def resid_evac(TW):
            def f(c, bank, bb):
                stt("dve", xres[:, c, 0:TW], xres[:, c, 0:TW], ALPHA, bank[:, 0:TW], ALU.mult, ALU.add, [Bx[c], bb], [Bx[c]])
            return f

        def rwkv(TW, ntok, nvcol, shift_out):
            nch = TW // 64
            nb = (TW + 127) // 128
            ntp = min(TW, 128)
            tt("dve", sA[:, :, 1:TW], xres[:, :, 0:TW - 1], xres[:, :, 1:TW], ALU.subtract, Bx, BsA)
            tt("dve", sA[:, :, 0:1], carry[:, :].unsqueeze(2), xres[:, :, 0:1], ALU.subtract, Bx + [Bcar], BsA)
            cp("pool", carry[:, :].unsqueeze(2), xres[:, :, ntok - 1:ntok], Bx, [Bcar])
            if shift_out is not None:
                P.dma("pool", shift_out, carry[:], reads=[Bcar], is_output=True)
            def sr(n):
                if TW == TWP:
                    stage(n)
            sr(201)
            for j in range(3):
                ul = [[(UA_RKV + j * 8 + blk, xb, Bxb), (UA_RKVS + j * 8 + blk, sA, BsA)] for blk in range(8)]

                def ev(c, bank, bb, j=j):
                    eng = "act" if c % 2 == 0 else "dve"
                    cp(eng, hid[:, j * 16 + c, 0:TW], bank[:, 0:TW], [bb], [Bh[j * 16 + c]])
                proj_fm(ul, TW, ev)
            if ntok < TW:
                mset("pool", hid[:, 16:32, ntok:TW], 0.0, Bh[16:32])
            sr(202)
            for which in range(2):
                (ua_, uab), (ub_, ubb) = (wget("A", UA_L1), wget("A", UA_L1S, hold=1)) if which == 0 else (wget("A", UA_G1), wget("A", UA_G1S, hold=1))
                for half in range(2):
                    bank, bb = G.next()
                    k = 0
                    for (st, sbf, src, sbufs) in [(ua_, uab, xb, Bxb), (ub_, ubb, sA, BsA)]:
                        u = uA(st)
                        for dc in range(DC):
                            mm(bank[:, 0:TW], u[:, dc, half * 128:(half + 1) * 128], src[:, dc, 0:TW], k == 0, k == 2 * DC - 1, [sbf, sbufs[dc]], [bb])
                            k += 1
                    if which == 0 and half == 0:
                        act(hTw[:, 0:TW], bank[:, 0:TW], AF.Tanh, [bb], [BhTw])
                        mset("pool", hTw[0:32, 0:TW], 0.0, [BhTw])
                        mset("pool", hTw[0:1, 0:TW], 1.0, [BhTw])
                    elif which == 0:
                        cp("dve", hTa[:, 0:TW], bank[:, 0:TW], [bb], [BhTa])
                    else:
                        act(hTg[:, half, 0:TW], bank[:, 0:TW], AF.Sigmoid, [bb], [BhTg])
            sr(203)
            for c in range(16):
                if c == 1:
                    sr(207)
                if c == 2:
                    sr(208)
                bar()
                rT = hid[:, c, 0:TW]; kT = hid[:, 16 + c, 0:TW]; vT = hid[:, 32 + c, 0:TW]
                Br, Bk, Bv = Bh[c], Bh[16 + c], Bh[32 + c]
                cs = slice(c * 128, (c + 1) * 128)
                a_, ab = RW["a"]; g_, gb = RW["g"]
                bank, bb = G.next()
                mm(bank[:, 0:TW], a2b[:, cs], hTa[:, 0:TW], True, True, [Ba2, BhTa], [bb])
                for kc in range(2):
                    mm(bank[:, 256:256 + TW], g2b[:, kc, cs], hTg[:, kc, 0:TW], kc == 0, kc == 1, [Bg2, BhTg], [bb])
                act(a_[:, 0:TW], bank[:, 0:TW], AF.Sigmoid, [bb, Bpc], [ab], bias=pc[:, PV_A0, c:c + 1], scale=1.0)
                cp("act", g_[:, 0:TW], bank[:, 256:256 + TW], [bb], [gb])
                bank, bb = G.next()
                for tb in range(nb):
                    mm(bank[0:ntp, tb * 128:(tb + 1) * 128], hTw[:, tb * 128:tb * 128 + ntp], w2aug[:, cs], True, True, [BhTw, Bw2], [bb])
                for tb in range(nb):
                    act(logd[0:ntp, tb, :], bank[0:ntp, tb * 128:(tb + 1) * 128], AF.Sigmoid, [bb], [Blogd])
                    ts("dve", logd[0:ntp, tb, :], logd[0:ntp, tb, :], negv[0:ntp, nvcol:nvcol + 1], None, ALU.mult, None, [Blogd, Bnegv], [Blogd])
                bank, bb = G.next()
                for tb in range(nb):
                    mm(bank[:, tb * 128:tb * 128 + ntp], logd[0:ntp, tb, :], cst[0:ntp, C_MK + 128:C_MK + 128 + ntp], True, True, [Blogd, Bc], [bb])
                    mm(bank[:, 256 + tb * 128:256 + tb * 128 + ntp], logd[0:ntp, tb, :], cst[0:ntp, C_MK:C_MK + ntp], True, True, [Blogd, Bc], [bb])
                pinc, pincb = RW["pinc"]; pexc, pexcb = RW["pexc"]; pinv, pinvb = RW["pinv"]
                act(pinc[:, 0:TW], bank[:, 0:TW], AF.Exp, [bb], [pincb])
                act(pexc[:, 0:TW], bank[:, 256:256 + TW], AF.Exp, [bb], [pexcb])
                act(pinv[:, 0:TW], bank[:, 0:TW], AF.Exp, [bb], [pinvb], scale=-1.0)
                sr(204)
                kkr, kkrb = RW["kkr"]; sq, sqb = RW["sq"]; rn, rnb = RW["rn"]; kk, kkb = RW["kk"]
                ts("dve", kkr[:, 0:TW], kT, pc[:, PV_KK, c:c + 1], None, ALU.mult, None, [Bk, Bpc], [kkrb])
                act(sq[:, 0:TW], kkr[:, 0:TW], AF.Square, [kkrb], [sqb])
                bank, bb = G.next()
                mm(bank[:, 0:TW], bones, sq[:, 0:TW], True, True, [Bc, sqb], [bb])
                ts("dve", rn[:, 0:TW], bank[:, 0:TW], 1e-24, None, ALU.max, None, [bb], [rnb])
                P.op("act", lambda e, o=rn, w=TW: e.sqrt(o[:, 0:w], o[:, 0:w]), [rnb], [rnb])
                recip(rn[:, 0:TW], rn[:, 0:TW], [rnb], [rnb])
                tt("dve", kk[:, 0:TW], kkr[:, 0:TW], rn[:, 0:TW], ALU.mult, [kkrb, rnb], [kkb])
                tmp, tmpb = RW["tmp"]; kmod, kmodb = RW["kmod"]; b_, bbf = RW["b"]; rk, rkb = RW["rk"]; bv, bvb = RW["bv"]
                ts("pool", tmp[:, 0:TW], a_[:, 0:TW], pc[:, PV_KA, c:c + 1], omka[:, c:c + 1], ALU.mult, ALU.add, [ab, Bpc, Bomka], [tmpb])
                tt("pool", kmod[:, 0:TW], kT, tmp[:, 0:TW], ALU.mult, [Bk, tmpb], [kmodb])
                tt("pool", b_[:, 0:TW], kk[:, 0:TW], a_[:, 0:TW], ALU.mult, [kkb, ab], [bbf])
                stt("dve", rk[:, 0:TW], rT, pc[:, PV_RK, c:c + 1], kmod[:, 0:TW], ALU.mult, ALU.mult, [Br, Bpc, kmodb], [rkb])
                mm(bank[:, 256:256 + TW], bones, rk[:, 0:TW], True, True, [Bc, rkb], [bb])
                tt("dve", bv[:, 0:TW], bank[:, 256:256 + TW], vT, ALU.mult, [bb, Bv], [bvb])
                (RK, RKb), (Kt, Ktb), (Bt, Btb), (Vt, Vtb) = RKfp[0], Ktfp[0], Btfp[0], Vtfp[0]
                for h in range(2):
                    hs = slice(h * 64, h * 64 + 64)

                    def v3(ap):
                        return ap.rearrange("p (j t) -> p j t", t=64)
                    tt("dve", RK[hs, 0:nch, 0, hs], v3(kk[hs, 0:TW]), v3(pexc[hs, 0:TW]), ALU.mult, [kkb, pexcb], [RKb])
                    tt("pool", RK[hs, 0:nch, 1, hs], v3(hid[hs, c, 0:TW]), v3(pinc[hs, 0:TW]), ALU.mult, [Br, pincb], [RKb])
                    tt("dve", Kt[hs, 0:nch, hs], v3(kmod[hs, 0:TW]), v3(pinv[hs, 0:TW]), ALU.mult, [kmodb, pinvb], [Ktb])
                    tt("pool", Bt[hs, 0:nch, hs], v3(b_[hs, 0:TW]), v3(pinv[hs, 0:TW]), ALU.mult, [bbf, pinvb], [Btb])
                    cp("act", Vt[hs, 0:nch, hs], v3(hid[hs, 32 + c, 0:TW]), [Bv], [Vtb])
                sr(205)
                yT, yTb = RW["yT"]
                H0 = Hst[:, c, :]
                for j in range(nch):
                    bank, bb = G.next()
                    tr(bank[:, 0:128], Kt[:, j, :], ident, [Ktb, Bc], [bb])
                    tr(bank[:, 128:256], Bt[:, j, :], ident, [Btb, Bc], [bb])
                    tr(bank[:, 256:384], Vt[:, j, :], ident, [Vtb, Bc], [bb])
                    KB, KBb = cmL[0]; VM, Vbdb = cmL[1]; Vbd = VM
                    cp("act", KB[:, 0:128], bank[:, 0:128], [bb], [KBb])
                    P.op("act", lambda e, o=KB, i=bank: e.mul(o[:, 128:256], i[:, 128:256], -1.0), [bb], [KBb])
                    cp("act", Vbd[:, 0:128], bank[:, 256:384], [bb], [Vbdb])
                    bank, bb = G.next()
                    mm(bank[:, 0:128], RK[:, j, 0, :], Bt[:, j, :], True, True, [RKb, Btb], [bb])
                    M0, M0b = _Shift(VM), Vbdb
                    tt("dve", M0[:, 0:128], bank[:, 0:128], mSLn, ALU.mult, [bb, Bc], [M0b])
                    bank, bb = G.next()
                    mm(bank[:, 0:256], Kt[:, j, :], RK[:, j, :, :].rearrange("p a b -> p (a b)"), True, True, [Ktb, RKb], [bb])
                    GK, GKb = cmL[2]
                    tt("dve", GK[:, 0:256], bank[:, 0:256], mK, ALU.mult, [bb, Bc], [GKb])
                    bank, bb = G.next()
                    mm(bank[:, 0:256], Bt[:, j, :], RK[:, j, :, :].rearrange("p a b -> p (a b)"), True, True, [Btb, RKb], [bb])
                    GB, GBb = cmL[3]
                    tt("dve", GB[:, 0:256], bank[:, 0:256], mBn, ALU.mult, [bb, Bc], [GBb])
                    bank, bb = G.next()
                    mm(bank[:, 0:128], RK[:, j, 0, :], H0, True, False, [RKb, BH[c]], [bb])
                    mm(bank[:, 0:128], GK[:, 0:128], Vbd[:, 0:128], False, True, [GKb, Vbdb], [bb])
                    X, Xb = cmX.next()
                    cp("act", X[:, 0:128], bank[:, 0:128], [bb], [Xb])
                    Mc, Mcb = M0, M0b
                    McT, McTb = GB, GBb
                    for lvl in range(6):
                        bank, bb = G.next()
                        mm(bank[:, 0:128], McT[:, 0:128], X[:, 0:128], True, True, [McTb, Xb], [bb])
                        if lvl < 5:
                            mm(bank[:, 128:256], Mc[:, 0:128], McT[:, 0:128], True, True, [Mcb, McTb], [bb])
                            if lvl < 4:
                                mm(bank[:, 256:384], McT[:, 0:128], Mc[:, 0:128], True, True, [Mcb, McTb], [bb])
                        Xn, Xnb = cmX.next()
                        tt("dve", Xn[:, 0:128], bank[:, 0:128], X[:, 0:128], ALU.add, [bb, Xb], [Xnb])
                        X, Xb = Xn, Xnb
                        if lvl < 5:
                            Mn, Mnb = cmM.next()
                            cp("dve", Mn[:, 0:128], bank[:, 128:256], [bb], [Mnb])
                            if lvl < 4:
                                cp("dve", Mn[:, 128:256], bank[:, 256:384], [bb], [Mnb])
                            McT, McTb = Mn, Mnb
                            Mc, Mcb = _Shift(Mn), Mnb
                    U, Ub = X, Xb
                    bank, bb = G.next()
                    mm(bank[:, 0:128], H0, RK[:, j, 1, :], True, False, [BH[c], RKb], [bb])
                    mm(bank[:, 0:128], Vbd[:, 0:128], GK[:, 128:256], False, False, [Vbdb, GKb], [bb])
                    mm(bank[:, 0:128], U[:, 0:128], GB[:, 128:256], False, True, [Ub, GBb], [bb])
                    mm(bank[:, 128:256], KB[:, 0:128], Vbd[:, 0:128], True, False, [KBb, Vbdb], [bb])
                    mm(bank[:, 128:256], KB[:, 128:256], U[:, 0:128], False, False, [KBb, Ub], [bb])
                    mm(bank[:, 128:256], ident, H0, False, True, [Bc, BH[c]], [bb])
                    cp("act", yT[0:64, j * 64:(j + 1) * 64], bank[0:64, 0:64], [bb], [yTb])
                    cp("act", yT[64:128, j * 64:(j + 1) * 64], bank[64:128, 64:128], [bb], [yTb])
                    act(H0, bank[:, 128:256], AF.Identity, [bb, pincb], [BH[c]], scale=pinc[:, j * 64 + 63:j * 64 + 64])
                sr(206)
                yc, ycb = RW["yc"]; rstd, rstdb = RW["rstd"]
                bank, bb = G.next()
                mm(bank[:, 0:TW], bones, yT[:, 0:TW], True, True, [Bc, yTb], [bb])
                stt("dve", yc[:, 0:TW], bank[:, 0:TW], -1.0 / 64, yT[:, 0:TW], ALU.mult, ALU.add, [bb, yTb], [ycb])
                act(sq[:, 0:TW], yc[:, 0:TW], AF.Square, [ycb], [sqb])
                mm(bank[:, 256:256 + TW], bones, sq[:, 0:TW], True, True, [Bc, sqb], [bb])
                ts("dve", rstd[:, 0:TW], bank[:, 256:256 + TW], 1.0 / 64, GN_EPS, ALU.mult, ALU.add, [bb], [rstdb])
                P.op("act", lambda e, o=rstd, w=TW: e.sqrt(o[:, 0:w], o[:, 0:w]), [rstdb], [rstdb])
                recip(rstd[:, 0:TW], rstd[:, 0:TW], [rstdb], [rstdb])
                tt("dve", yc[:, 0:TW], yc[:, 0:TW], rstd[:, 0:TW], ALU.mult, [ycb, rstdb], [ycb])
                ts("pool", yc[:, 0:TW], yc[:, 0:TW], pc[:, PV_LXG, c:c + 1], pc[:, PV_LXB, c:c + 1], ALU.mult, ALU.add, [ycb, Bpc], [ycb])
                tt("pool", yc[:, 0:TW], yc[:, 0:TW], bv[:, 0:TW], ALU.add, [ycb, bvb], [ycb])
                tt("dve", sB[:, c, 0:TW], yc[:, 0:TW], g_[:, 0:TW], ALU.mult, [ycb, gb], [BsB[c]])
            proj_fm([[(UA_WO + blk, sB, BsB)] for blk in range(8)], TW, resid_evac(TW))

        class _Shift:
            def __init__(self, base):
                self.base = base

            def __getitem__(self, idx):
                assert idx == (slice(None), slice(0, 128))
                return self.base[:, 128:256]

        def attention(TW, prompt, t0):
            nqb = (TW + 127) // 128
            qw = min(TW, 128)
            QT, BQ = sA, BsA
            if prompt:
                nkb = (t0 + TW) // 128
                groups = [("scr", n0, min(8, nkb - n0), 128) for n0 in range(0, nkb, 8)]
            else:
                groups = [("cache", 0, 8, 128), ("own", 0, 1, NSAMP)]

            def valid(qb, n):
                return (not prompt) or n <= (t0 // 128 + qb)
            total = {}
            for qb in range(nqb):
                total[qb] = sum(1 for (_, n0, nblk, _) in groups for n in range(n0, n0 + nblk) if valid(qb, n))
            for c in range(16):
                bar()
                cs = slice(c * 128, (c + 1) * 128)
                done = {(qb, h): 0 for qb in range(nqb) for h in range(2)}
                for (kind, n0, nblk, nk) in groups:
                    KT, KTb = ktp.next(); V, Vb = vtp.next()
                    if kind == "scr":
                        P.dma("pool", KT[:, 0:nblk * 128], KTscr[cs, n0 * 128:(n0 + nblk) * 128], reads=[B_KTscr], writes=[KTb])
                        P.dma("pool", V[:, 0:nblk, 0:128], Vscr[n0 * 128:(n0 + nblk) * 128, cs].rearrange("(n p) e -> p n e", p=128), reads=[B_Vscr], writes=[Vb])
                    elif kind == "cache":
                        P.dma("pool", ckst[:], ck[:, cs].rearrange("(n p) e -> p n e", p=128), writes=[Bckst])
                        for gq in range(2):
                            bank, bb = G.next()
                            for i in range(4):
                                tr(bank[:, i * 128:(i + 1) * 128], ckst[:, gq * 4 + i, :], ident, [Bckst, Bc], [bb])
                            cp("act" if gq == 0 else "dve", KT[:, gq * 512:(gq + 1) * 512], bank[:, 0:512], [bb], [KTb])
                        P.dma("pool", V[:, 0:8, 0:128], cv[:, cs].rearrange("(n p) e -> p n e", p=128), writes=[Vb])
                    else:
                        P.dma("pool", KT[:, 0:64], KTs[cs, :], reads=[B_KTs], writes=[KTb])
                        P.dma("pool", V[0:64, 0, 0:128], Vs[:, cs], reads=[B_Vs], writes=[Vb])
                    pend = [None]
                    for qb in range(nqb):
                        vblocks = [n for n in range(n0, n0 + nblk) if valid(qb, n)]
                        if not vblocks:
                            continue
                        for h in range(2):
                            Ob, Obb = Obanks[qb * 2 + h]
                            hs = slice(h * 64, h * 64 + 64)
                            for b0 in range(0, len(vblocks), 4):
                                batch = vblocks[b0:b0 + 4]
                                bank, bb = G.next()
                                for i, n in enumerate(batch):
                                    nl = n - n0
                                    mm(bank[0:nk, i * 128:i * 128 + qw], KT[hs, nl * 128:nl * 128 + nk], QT[hs, c, qb * 128:qb * 128 + qw],
                                       True, True, [KTb, BQ[c]], [bb])
                                PT, PTb = ptp.next()
                                if qw == 128 and nk == 128:
                                    act(PT[:, 0:len(batch) * 128], bank[:, 0:len(batch) * 128], AF.Exp, [bb], [PTb], scale=0.125)
                                else:
                                    for i in range(len(batch)):
                                        act(PT[0:nk, i * 128:i * 128 + qw], bank[0:nk, i * 128:i * 128 + qw], AF.Exp, [bb], [PTb], scale=0.125)
                                for i, n in enumerate(batch):
                                    if prompt and n == t0 // 128 + qb:
                                        mset("pool", PT[64:128, i * 128:i * 128 + 64], 0.0, [PTb])
                                def pv(batch=batch, PT=PT, PTb=PTb, Ob=Ob, Obb=Obb, qb=qb, h=h):
                                    for i, n in enumerate(batch):
                                        nl = n - n0
                                        d_ = done[(qb, h)]
                                        mm(Ob[0:qw, 0:129], PT[0:nk, i * 128:i * 128 + qw], V[0:nk, nl, 0:129],
                                           d_ == 0, d_ == total[qb] - 1, [PTb, Vb], [Obb])
                                        done[(qb, h)] = d_ + 1
                                if pend[0] is not None:
                                    pend[0]()
                                pend[0] = pv
                    if pend[0] is not None:
                        pend[0]()
                        pend[0] = None
                for qb in range(nqb):
                    Ob, Obb = Obanks[qb * 2]
                    Ob2, Ob2b = Obanks[qb * 2 + 1]
                    o_, ob_ = osm.next()
                    sm, smb = osm.next()
                    recip(sm[0:qw, 0:1], Ob[0:qw, 128:129], [Obb], [smb])
                    recip(sm[0:qw, 1:2], Ob2[0:qw, 128:129], [Ob2b], [smb])
                    tt("dve", sm[0:qw, 2:3], sm[0:qw, 1:2], neglam[0:qw, :], ALU.mult, [smb, Blamc], [smb])
                    ts("dve", o_[0:qw, 0:128], Ob[0:qw, 0:128], sm[0:qw, 0:1], None, ALU.mult, None, [Obb, smb], [ob_])
                    stt("dve", o_[0:qw, 0:128], Ob2[0:qw, 0:128], sm[0:qw, 2:3], o_[0:qw, 0:128], ALU.mult, ALU.add, [Ob2b, smb, ob_], [ob_])
                    sq2, sq2b = osm.next()
                    tt("dve", sq2[0:qw, 0:128], o_[0:qw, 0:128], o_[0:qw, 0:128], ALU.mult, [ob_], [sq2b])
                    P.op("dve", lambda e, s_=sm, q_=sq2, w=qw: e.reduce_sum(s_[0:w, 3:4], q_[0:w, 0:128], axis=AX.X), [sq2b], [smb])
                    ts("dve", sm[0:qw, 4:5], sm[0:qw, 3:4], 1.0 / 128, LN_EPS, ALU.mult, ALU.add, [smb], [smb])
                    P.op("act", lambda e, s_=sm, w=qw: e.sqrt(s_[0:w, 4:5], s_[0:w, 4:5]), [smb], [smb])
                    recip(sm[0:qw, 5:6], sm[0:qw, 4:5], [smb], [smb])
                    stt("dve", tokst[0:qw, qb, cs], o_[0:qw, 0:128], sm[0:qw, 5:6], subg[0:qw, :], ALU.mult, ALU.mult, [ob_, smb, Bsubg], [Btk[qb][c]])
            for qb in range(nqb):
                for g4 in range(4):
                    bank, bb = G.next()
                    for i in range(4):
                        c = g4 * 4 + i
                        tr(bank[:, i * 128:(i + 1) * 128], tokst[:, qb, c * 128:(c + 1) * 128], ident, [Btk[qb][c], Bc], [bb])
                    for i in range(4):
                        c = g4 * 4 + i
                        cp("act" if g4 % 2 == 0 else "dve", sB[:, c, qb * 128:qb * 128 + qw], bank[:, i * 128:i * 128 + qw], [bb], [BsB[c]])

        def tile_pass(prompt, it):
            TW = TWP if prompt else 64
            ntok = TW if prompt else NSAMP
            nb = (TW + 127) // 128
            ntp = min(TW, 128)
            t0 = it * TWP

            def st_(n):
                stage(n + (100 if prompt else 0))
                bar()
            for b in range(nb):
                if prompt:
                    P.dma("pool", tokst[:, b, :], xp[t0 + b * 128:t0 + (b + 1) * 128, :], writes=Btk[b])
                else:
                    mset("pool", tokst[:, 0, :], 0.0, Btk[0])
                    P.dma("sp", tokst[0:NSAMP, 0, :], xs, writes=Btk[0])
                for g4 in range(4):
                    bank, bb = G.next()
                    for i in range(4):
                        c = g4 * 4 + i
                        tr(bank[:, i * 128:(i + 1) * 128], tokst[:, b, c * 128:(c + 1) * 128], ident, [Btk[b][c], Bc], [bb])
                    for i in range(4):
                        c = g4 * 4 + i
                        cp("dve", xres[:, c, b * 128:b * 128 + ntp], bank[:, i * 128:i * 128 + ntp], [bb], [Bx[c]])
                        cp("act", xb[:, c, b * 128:b * 128 + ntp], xres[:, c, b * 128:b * 128 + ntp], [Bx[c]], [Bxb[c]])
            st_(2)
            ffn(0, TW); layer_norm(0, TW)
            st_(3)
            rwkv(TW, ntok, 0 if prompt else 1, (shp if it == NT - 1 else None) if prompt else shs)
            st_(4)
            layer_norm(1, TW)
            ffn(1, TW); layer_norm(2, TW)
            st_(5)
            for blk in range(16):
                su, sub_ = wget("A", UA_KV + blk)
                u = uA(su)
                if blk < 8:
                    for half in range(2):
                        c = 2 * blk + half
                        bank, bb = G.next()
                        for dc in range(DC):
                            mm(bank[:, 0:TW], u[:, dc, half * 128:(half + 1) * 128], xb[:, dc, 0:TW], dc == 0, dc == DC - 1, [sub_, Bxb[dc]], [bb])
                        cp("act", sA[:, c, 0:TW], bank[:, 0:TW], [bb], [BsA[c]])
                for b in range(nb):
                    bank, bb = G.next()
                    for dc in range(DC):
                        mm(bank[0:ntp, 0:256], xb[:, dc, b * 128:b * 128 + ntp], u[:, dc, :], dc == 0, dc == DC - 1, [sub_, Bxb[dc]], [bb])
                    t1, t1b = tmpA.next()
                    cp("dve", t1[0:ntp, 0:256], bank[0:ntp, 0:256], [bb], [t1b])
                    col = (blk % 8) * 256
                    if prompt:
                        dst = (kp if blk < 8 else vp)[t0 + b * 128:t0 + (b + 1) * 128, col:col + 256]
                        P.dma("pool", dst, t1[:, 0:256], reads=[t1b], is_output=True)
                    else:
                        dst = (ksm if blk < 8 else vsm)[:, col:col + 256]
                        P.dma("pool", dst, t1[0:NSAMP, 0:256], reads=[t1b], is_output=True)
                    if blk >= 8:
                        t2, t2b = tmpB.next()
                        cp("act", t2[0:ntp, 0:256], t1[0:ntp, 0:256], [t1b], [t2b])
                        if prompt:
                            P.dma("pool", Vscr[t0 + b * 128:t0 + (b + 1) * 128, col:col + 256], t2[:, 0:256], reads=[t2b], writes=[B_Vscr])
                        else:
                            P.dma("pool", Vs[:, col:col + 256], t2[0:64, 0:256], reads=[t2b], writes=[B_Vs])
            if prompt:
                P.dma("pool", KTscr[:, t0:t0 + TW].rearrange("(c p) t -> p c t", p=128), sA[:, :, 0:TW], reads=BsA, writes=[B_KTscr])
            else:
                P.dma("pool", KTs.rearrange("(c p) t -> p c t", p=128), sA[:, :, 0:64], reads=BsA, writes=[B_KTs])
            st_(6)
            ffn(2, TW); layer_norm(3, TW)

            def evq(c, bank, bb):
                cp("act" if c % 2 == 0 else "dve", sA[:, c, 0:TW], bank[:, 0:TW], [bb], [BsA[c]])
            proj_fm([[(UA_Q + blk, xb, Bxb)] for blk in range(8)], TW, evq)
            st_(7)
            attention(TW, prompt, t0)
            st_(8)
            proj_fm([[(UA_DWO + blk, sB, BsB)] for blk in range(8)], TW, resid_evac(TW))
            layer_norm(4, TW)
            ffn(3, TW); layer_norm(5, TW)
            for b in range(nb):
                for g4 in range(4):
                    bank, bb = G.next()
                    for i in range(4):
                        c = g4 * 4 + i
                        tr(bank[0:ntp, i * 128:(i + 1) * 128], xres[:, c, b * 128:b * 128 + ntp], ident, [Bx[c], Bc], [bb])
                    cp("act" if g4 % 2 == 0 else "dve", tokst[0:ntp, b, g4 * 512:(g4 + 1) * 512], bank[0:ntp, 0:512], [bb], Btk[b][g4 * 4:g4 * 4 + 4])
                if prompt:
                    P.dma("pool", yp[t0 + b * 128:t0 + (b + 1) * 128, :], tokst[:, b, :], reads=Btk[b], is_output=True)
                else:
                    P.dma("pool", ys, tokst[0:NSAMP, 0, :], reads=Btk[0], is_output=True)

        def state_in():
            mset("pool", tokst[:, 0, :], 0.0, Btk[0])
            tk = tokst[:, 0, :].rearrange("p (c f) -> p c f", c=16)
            for h in range(2):
                hs = slice(h * 64, h * 64 + 64)
                P.dma("pool", tk[hs, :, h * 64:h * 64 + 64], swkv[hs, :, :], writes=Btk[0])
            for c in range(16):
                bank, bb = G.next()
                tr(bank[:, 0:128], tk[:, c, :], ident, Btk[0] + [Bc], [bb])
                cp("act" if c % 2 == 0 else "dve", Hst[:, c, :], bank[:, 0:128], [bb], [BH[c]])
            P.dma("pool", carry[:], sshift, writes=[Bcar])

        def state_out(dst):
            tk = tokst[:, 1, :].rearrange("p (c f) -> p c f", c=16)
            for c in range(16):
                bank, bb = G.next()
                tr(bank[:, 0:128], Hst[:, c, :], ident, [BH[c], Bc], [bb])
                cp("act" if c % 2 == 0 else "dve", tk[:, c, :], bank[:, 0:128], [bb], Btk[1])
            for h in range(2):
                hs = slice(h * 64, h * 64 + 64)
                P.dma("pool", dst[hs, :, :], tk[hs, :, h * 64:h * 64 + 64], reads=Btk[1], is_output=True)

        def main_all():
            stage(1)
            state_in()
            stage(11)
            tile_pass(False, 0)
            stage(9)
            state_out(wkvs)
            stage(10)
            for c in range(16):
                mset("pool", Hst[:, c, :], 0.0, [BH[c]])
            mset("pool", carry[:], 0.0, [Bcar])
            for it in range(NT):
                tile_pass(True, it)
            state_out(wkvp)

        keep = P.dry
        P.dry = True
        main_all()
        P.dry = keep
        REAL[0] = True
        main_all()
        P.dry = False
        print('OPCOUNTS', P.cnt, flush=True)
        P.emit()
    return nc


def _unitsA(W):
    n = W.shape[1] // 256
    return np.ascontiguousarray(W.reshape(16, 128, n, 256).transpose(2, 1, 0, 3)).reshape(n, 128, 4096)


def _unitsB(W):
    return np.ascontiguousarray(W.reshape(4, 11, 128, 8, 256).transpose(3, 0, 2, 1, 4)).reshape(32, 128, BW)


def _col(v):
    return np.ascontiguousarray(v.reshape(16, 128).T)


def _consts():
    c = np.zeros((128, C_W), np.float32)
    c[:, C_ID:C_ID + 128] = np.eye(128, dtype=np.float32)
    blk = np.zeros((128, 128), np.float32)
    blk[:64, :64] = 1; blk[64:, 64:] = 1
    c[:, C_BO:C_BO + 128] = blk
    i = np.arange(128)
    same = (i[:, None] // 64) == (i[None, :] // 64)
    r, cc = i[:, None] % 64, i[None, :] % 64
    SL = (same & (cc < r)).astype(np.float32)
    SU = (same & (r < cc)).astype(np.float32)
    UI = (same & (r <= cc)).astype(np.float32)
    c[:, C_MSL:C_MSL + 128] = -SL
    c[:, C_MK:C_MK + 128] = SU; c[:, C_MK + 128:C_MK + 256] = UI
    c[:, C_MB:C_MB + 128] = -SU; c[:, C_MB + 128:C_MB + 256] = -UI
    j = np.arange(64)
    c[:, C_TI:C_TI + 64] = np.tile((j[:, None] <= j[None, :]).astype(np.float32), (2, 1))
    c[:, C_TE:C_TE + 64] = np.tile((j[:, None] < j[None, :]).astype(np.float32), (2, 1))
    return c


def _shared_inputs(inp):
    f = lambda a: np.asarray(a, np.float32)
    pv = np.zeros((128, NPV, 16), np.float32)
    ln_g, ln_b = f(inp["ln_g"]).reshape(6, D), f(inp["ln_b"]).reshape(6, D)
    for i in range(6):
        pv[:, PV_LNG + i] = _col(ln_g[i]); pv[:, PV_LNB + i] = _col(ln_b[i])
    pv[:, PV_A0] = _col(f(inp["rwkv_a0"])[0]); pv[:, PV_KK] = _col(f(inp["rwkv_k_k"])[0])
    pv[:, PV_KA] = _col(f(inp["rwkv_k_a"])[0]); pv[:, PV_RK] = _col(f(inp["rwkv_r_k"])[0].reshape(D))
    pv[:, PV_LXG] = _col(f(inp["rwkv_lnx_g"])[0]); pv[:, PV_LXB] = _col(f(inp["rwkv_lnx_b"])[0])
    mu = f(inp["rwkv_mu"])[0]
    for j in range(6):
        pv[:, PV_MU + j] = _col(mu[j])
    wA = np.empty((NSA, 128, 4096), np.float32)
    w_in = f(inp["ffn_w_in"]); w_out = f(inp["ffn_w_out"])
    for i in range(2):
        for s in range(2):
            wgu = np.concatenate([w_in[i, s][:, :FF].reshape(D, MC, 128), w_in[i, s][:, FF:].reshape(D, MC, 128)], axis=2).reshape(D, 2 * FF)
            wA[SA_FFN + (i * 2 + s) * 44: SA_FFN + (i * 2 + s + 1) * 44] = _unitsA(wgu)
    rkvw = f(inp["rwkv_w_rkv"])[0]
    for j in range(3):
        wA[SA_RKV + j * 8: SA_RKV + (j + 1) * 8] = _unitsA(rkvw[j])
    L1 = np.zeros((D, 256), np.float32)
    L1[:, 32:128] = f(inp["rwkv_w1"])[0]; L1[:, 128:224] = f(inp["rwkv_a1"])[0]
    wA[SA_L1:SA_L1 + 1] = _unitsA(L1)
    wA[SA_G1:SA_G1 + 1] = _unitsA(f(inp["rwkv_g1"])[0])
    wA[SA_WO:SA_WO + 8] = _unitsA(f(inp["rwkv_w_o"])[0])
    wA[SA_KV:SA_KV + 16] = _unitsA(f(inp["kv_w"]))
    wA[SA_Q:SA_Q + 8] = _unitsA(f(inp["diff_w_q"])[0])
    wA[SA_DWO:SA_DWO + 8] = _unitsA(f(inp["diff_w_o"])[0])
    wB = np.empty((NUB, 128, BW), np.float32)
    for i in range(2):
        for s in range(2):
            wB[(i * 2 + s) * 32:(i * 2 + s + 1) * 32] = _unitsB(w_out[i, s])
    w2aug = np.zeros((128, D), np.float32)
    w2aug[0] = f(inp["rwkv_w0"])[0]; w2aug[32:128] = f(inp["rwkv_w2"])[0]
    a2p = np.zeros((128, D), np.float32); a2p[0:96] = f(inp["rwkv_a2"])[0]
    g2p = np.ascontiguousarray(f(inp["rwkv_g2"])[0].reshape(2, 128, D).transpose(1, 0, 2))
    return {"pcols": pv, "wA_src": wA, "wB_src": wB, "w2aug": w2aug, "a2p": a2p, "g2p": g2p, "consts": _consts(),
            "dlam": f(inp["diff_lambda"]).reshape(1, 256), "subg": f(inp["diff_subln_g"]).reshape(1, 128)}


def _state_layout(S):
    return np.ascontiguousarray(S.reshape(16, 2, 64, 64).transpose(1, 2, 0, 3)).reshape(128, 16, 64)


def _state_unlayout(A):
    return np.ascontiguousarray(A.reshape(2, 64, 16, 64).transpose(2, 0, 1, 3)).reshape(32, 64, 64)


_NC_CACHE = {}


def kernel(**inp):
    f = lambda a: np.asarray(a, np.float32)
    x_prompt = f(inp["x_prompt"]); x_sample = f(inp["x_sample"])
    NB, SEQ = x_prompt.shape[0], x_prompt.shape[1]
    shared = _shared_inputs(inp)
    ck = f(inp["cache_k"]).reshape(NB, PAST, D); cv = f(inp["cache_v"]).reshape(NB, PAST, D)
    swkv = f(inp["state_wkv"])[0]; sshift = f(inp["state_shift"])[0]
    in_maps = []
    for i in range(NB):
        m = dict(shared)
        m.update({"xp": x_prompt[i], "xs": x_sample[i], "ck": ck[i], "cv": cv[i],
                  "swkv": _state_layout(swkv[i]), "sshift": _col(sshift[i])})
        in_maps.append(m)
    if SEQ not in _NC_CACHE:
        _NC_CACHE[SEQ] = build(SEQ)
    nc = _NC_CACHE[SEQ]
    res = run_bass_kernel_spmd(nc, in_maps, core_ids=list(range(NB)))
    R = res.results
    g = lambda k: np.stack([np.asarray(R[i][k], np.float32) for i in range(NB)])
    y_prompt = g("yp"); y_sample = g("ys")
    k_prompt = g("kp").reshape(NB, SEQ, 32, 64); v_prompt = g("vp").reshape(NB, SEQ, 16, 128)
    k_sample = g("ksm").reshape(NB, NSAMP, 32, 64); v_sample = g("vsm").reshape(NB, NSAMP, 16, 128)
    wkv_prompt = np.stack([_state_unlayout(np.asarray(R[i]["wkvp"], np.float32)) for i in range(NB)])[None]
    wkv_sample = np.stack([_state_unlayout(np.asarray(R[i]["wkvs"], np.float32)) for i in range(NB)])[None]
    unc = lambda a: np.ascontiguousarray(a.T).reshape(D)
    shift_prompt = np.stack([unc(np.asarray(R[i]["shp"], np.float32)) for i in range(NB)])[None]
    shift_sample = np.stack([unc(np.asarray(R[i]["shs"], np.float32)) for i in range(NB)])[None]
    return (y_prompt, y_sample, k_prompt, v_prompt, wkv_prompt, shift_prompt,
            k_sample, v_sample, wkv_sample, shift_sample)
```

```python
import math
from contextlib import ExitStack
import numpy as np
import concourse.bass as bass
import concourse.mybir as mybir
from concourse.bass_utils import run_bass_kernel_spmd

F32 = mybir.dt.float32
BF16 = mybir.dt.bfloat16
ALU = mybir.AluOpType
AF = mybir.ActivationFunctionType
AX = mybir.AxisListType

ENGS = ("pe", "act", "dve", "pool", "sp")
N_DMA_SEMS = 48
DMA_HALF = 24

D = 2048
DC = 16
FF = 5632
MC = 44
TWP = 256
PAST = 1024
NSAMP = 16
ALPHA = (2.0 * 2) ** 0.25
LN_EPS = 1e-5
GN_EPS = 64e-5
LAMBDA_INIT = 0.8 - 0.6 * math.exp(-0.3 * 1)
NEG_E = -math.exp(-0.5)

PV_LNG, PV_LNB = 0, 6
PV_A0, PV_KK, PV_KA, PV_RK, PV_LXG, PV_LXB, PV_MU = 12, 13, 14, 15, 16, 17, 18
NPV = 24
C_ID, C_BO, C_MSL, C_MK, C_MB, C_TI, C_TE, C_W = 0, 128, 256, 384, 640, 896, 960, 1024

UA_FFN = 0
UA_RKV = 176
UA_RKVS = 200
UA_L1, UA_G1, UA_L1S, UA_G1S = 224, 225, 226, 227
UA_WO, UA_KV, UA_Q, UA_DWO = 228, 236, 252, 260
NUA = 268
SA_FFN, SA_RKV, SA_L1, SA_G1, SA_WO, SA_KV, SA_Q, SA_DWO, NSA = 0, 176, 200, 201, 202, 210, 226, 234, 242
NUB = 128
BW = 2816


class Buf:
    __slots__ = ("name", "lw", "rd")

    def __init__(self, name=""):
        self.name = name
        self.lw = None
        self.rd = {}


class Prog:
    def __init__(self, nc):
        self.nc = nc
        self.streams = {e: [] for e in ENGS}
        self.cnt = {e: 0 for e in ENGS}
        self.known = {e: {} for e in ENGS}
        self.sem = {}
        for e in ("pe", "act", "dve", "pool"):
            self.sem[e] = nc.alloc_semaphore(name="c_" + e)
        self.dsem = [nc.alloc_semaphore(name="d_%d" % i) for i in range(N_DMA_SEMS)]
        self.dval = [0] * N_DMA_SEMS
        self.drr = {"sp": 0, "pool": 0}
        self.out_events = {}
        self.n_ops = 0
        self.dry = False

    def _deps(self, eng, reads, writes):
        deps = {}
        for b in list(reads) + list(writes):
            if b.lw is not None:
                k, v = b.lw
                if deps.get(k, 0) < v:
                    deps[k] = v
        for b in writes:
            for k, v in b.rd.items():
                if deps.get(k, 0) < v:
                    deps[k] = v
        waits = []
        kn = self.known[eng]
        for k, v in deps.items():
            if eng == "pe" and k == "pe":
                continue
            if kn.get(k, 0) >= v:
                continue
            kn[k] = v
            waits.append((k, v))
        return waits

    def _commit(self, ev, reads, writes):
        k, v = ev
        for b in reads:
            if b.rd.get(k, 0) < v:
                b.rd[k] = v
        for b in writes:
            b.lw = ev
            b.rd = {}

    def _semh(self, k):
        return self.sem[k] if isinstance(k, str) else self.dsem[k]

    def op(self, eng, fn, reads=(), writes=()):
        if self.dry:
            return
        waits = self._deps(eng, reads, writes)
        self.cnt[eng] += 1
        ev = (eng, self.cnt[eng])
        self.streams[eng].append((waits, fn, (eng, 1)))
        self._commit(ev, reads, writes)
        self.n_ops += 1

    def dma(self, q, out_ap, in_ap, reads=(), writes=(), is_output=False, **kw):
        if self.dry:
            return
        i = self.drr[q] + (0 if q == "sp" else DMA_HALF)
        self.drr[q] = (self.drr[q] + 1) % DMA_HALF
        waits = self._deps(q, reads, writes)
        kn = self.known[q]
        if self.dval[i] > 0 and kn.get(i, 0) < self.dval[i]:
            kn[i] = self.dval[i]
            waits.append((i, self.dval[i]))
        self.dval[i] += 16
        ev = (i, self.dval[i])

        def fn(e, out_ap=out_ap, in_ap=in_ap, kw=kw):
            return e.dma_start(out=out_ap, in_=in_ap, **kw)
        self.streams[q].append((waits, fn, (i, 16)))
        self._commit(ev, reads, writes)
        if is_output:
            self.out_events[i] = self.dval[i]
        self.n_ops += 1

    def barrier(self):
        for e in ENGS:
            waits = []
            kn = self.known[e]
            for k in ("pe", "act", "dve", "pool"):
                if k != e and kn.get(k, 0) < self.cnt[k]:
                    kn[k] = self.cnt[k]
                    waits.append((k, self.cnt[k]))
            for i in range(N_DMA_SEMS):
                if kn.get(i, 0) < self.dval[i]:
                    kn[i] = self.dval[i]
                    waits.append((i, self.dval[i]))
            self.streams[e].append((waits, None, None))

    def finish(self):
        waits = []
        for i, v in self.out_events.items():
            if self.known["sp"].get(i, 0) < v:
                waits.append((i, v))
        self.streams["sp"].append((waits, None, None))

    def _replay(self, eng, e):
        for waits, fn, inc in self.streams[eng]:
            for k, v in waits:
                e.wait_ge(self._semh(k), v)
            if fn is not None:
                ins = fn(e)
                ins.then_inc(self._semh(inc[0]), inc[1])

    def emit(self):
        self.finish()
        nc = self.nc
        with nc.Block() as block:
            @block.tensor
            def _(e):
                self._replay("pe", e)

            @block.scalar
            def _(e):
                self._replay("act", e)

            @block.vector
            def _(e):
                self._replay("dve", e)

            @block.gpsimd
            def _(e):
                self._replay("pool", e)

            @block.sync
            def _(e):
                self._replay("sp", e)


class Rot:
    def __init__(self, items):
        self.items = items
        self.i = 0

    def next(self):
        it = self.items[self.i % len(self.items)]
        self.i += 1
        return it


def build(SEQ):
    import os
    STOP = int(os.environ.get('MK_STOP', '99'))
    REAL = [False]
    NT = SEQ // TWP
    nc = bass.Bass("TRN2", target_bir_lowering=False)

    def din(name, shape, dt=F32):
        return nc.dram_tensor(name, shape, dt, kind="ExternalInput").ap()

    def dout(name, shape):
        return nc.dram_tensor(name, shape, F32, kind="ExternalOutput").ap()

    def dscr(name, shape, dt):
        return nc.dram_tensor(name, shape, dt, kind="Internal").ap()

    xp = din("xp", [SEQ, D]); xs = din("xs", [NSAMP, D])
    ck = din("ck", [PAST, D]); cv = din("cv", [PAST, D])
    swkv = din("swkv", [128, 16, 64]); sshift = din("sshift", [128, 16])
    pcols_d = din("pcols", [128, NPV, 16])
    wA_src = din("wA_src", [NSA, 128, 4096]); wB_src = din("wB_src", [NUB, 128, BW])
    w2aug_d = din("w2aug", [128, D]); a2p_d = din("a2p", [128, D]); g2p_d = din("g2p", [128, 2, D])
    consts_d = din("consts", [128, C_W]); dlam_d = din("dlam", [1, 256]); subg_d = din("subg", [1, 128])

    yp = dout("yp", [SEQ, D]); ys = dout("ys", [NSAMP, D])
    kp = dout("kp", [SEQ, D]); vp = dout("vp", [SEQ, D])
    wkvp = dout("wkvp", [128, 16, 64]); shp = dout("shp", [128, 16])
    ksm = dout("ksm", [NSAMP, D]); vsm = dout("vsm", [NSAMP, D])
    wkvs = dout("wkvs", [128, 16, 64]); shs = dout("shs", [128, 16])

    wA_parts = [dscr("wA%d" % i, [67, 128, 4096], BF16) for i in range(4)]
    wA = [wA_parts[i // 67][i % 67] for i in range(NUA)]
    wB = dscr("wB", [NUB, 128, BW], BF16)
    KTscr = dscr("KTscr", [D, SEQ], BF16); Vscr = dscr("Vscr", [SEQ, D], BF16)
    KTs = dscr("KTs", [D, 64], BF16); Vs = dscr("Vs", [64, D], BF16)
    B_wA = [Buf() for _ in range(NUA)]; B_wB = [Buf() for _ in range(NUB)]
    B_KTscr = Buf(); B_Vscr = Buf(); B_KTs = Buf(); B_Vs = Buf()

    P = Prog(nc)

    def stage(n):
        if REAL[0] and STOP == n:
            P.dry = True

    def bar():
        if not P.dry:
            P.barrier()

    def mm(out, lhsT, rhs, start, stop, R, W):
        P.op("pe", lambda e: e.matmul(out, lhsT, rhs, start=start, stop=stop), R, W)

    def tr(out, in_, ident, R, W):
        P.op("pe", lambda e: e.transpose(out, in_, ident), R, W)

    def tt(eng, out, a, b, op, R, W):
        P.op(eng, lambda e: e.tensor_tensor(out, a, b, op), R, W)

    def ts(eng, out, a, s1, s2, op0, op1, R, W):
        if op1 is None:
            P.op(eng, lambda e: e.tensor_scalar(out, a, s1, None, op0), R, W)
        else:
            P.op(eng, lambda e: e.tensor_scalar(out, a, s1, s2, op0, op1), R, W)

    def stt(eng, out, in0, scalar, in1, op0, op1, R, W):
        P.op(eng, lambda e: e.scalar_tensor_tensor(out, in0, scalar, in1, op0, op1), R, W)

    def act(out, in_, func, R, W, **kw):
        P.op("act", lambda e: e.activation(out, in_, func, **kw), R, W)

    def cp(eng, out, in_, R, W):
        if eng == "act":
            P.op("act", lambda e: e.copy(out, in_), R, W)
        else:
            P.op(eng, lambda e: e.tensor_copy(out, in_), R, W)

    def mset(eng, ap, val, W):
        P.op(eng, lambda e: e.memset(ap, val), (), W)

    def recip(out, in_, R, W):
        P.op("dve", lambda e: e.reciprocal(out, in_), R, W)

    with ExitStack() as es0:
        def sb0(name, shape, dt=F32):
            return es0.enter_context(nc.sbuf_tensor(name, shape, dt))
        mu_t = sb0("mu_t", [128, 6, 16]); B_mu = Buf()
        P.dma("sp", mu_t[:], pcols_d[:, PV_MU:PV_MU + 6, :], writes=[B_mu])
        st32 = [(sb0("st32_%d" % i, [128, 4096]), Buf()) for i in range(3)]
        st16 = [(sb0("st16_%d" % i, [128, 4096], BF16), Buf()) for i in range(4)]
        r32 = Rot(st32); r16 = Rot(st16); reng = Rot(["dve", "act", "pool"])
        rq = Rot(["sp", "pool"])

        def conv_A(src_idx, dst_idx, scale=None):
            t32, b32 = r32.next()
            P.dma("sp", t32[:, 0:4096], wA_src[src_idx], writes=[b32])
            t16, b16 = r16.next()
            eng = reng.next()
            cp(eng, t16[:, 0:4096], t32[:, 0:4096], [b32], [b16])
            P.dma(rq.next(), wA[dst_idx], t16[:, 0:4096], reads=[b16], writes=[B_wA[dst_idx]])
            if scale is not None:
                dsts, specs = scale
                t16s, b16s = r16.next()
                for dc in range(DC):
                    for (c0, c1, mj) in specs:
                        eng = "dve" if (dc % 2 == 0) else "pool"
                        ts(eng, t16s[:, dc * 256 + c0: dc * 256 + c1], t32[:, dc * 256 + c0: dc * 256 + c1],
                           mu_t[:, mj, dc:dc + 1], None, ALU.mult, None, [b32, B_mu], [b16s])
                P.dma(rq.next(), wA[dsts], t16s[:, 0:4096], reads=[b16s], writes=[B_wA[dsts]])

        for u in range(176):
            conv_A(SA_FFN + u, UA_FFN + u)
        mu_of = [0, 2, 3]
        for j in range(3):
            for blk in range(8):
                conv_A(SA_RKV + j * 8 + blk, UA_RKV + j * 8 + blk, (UA_RKVS + j * 8 + blk, [(0, 256, mu_of[j])]))
        conv_A(SA_L1, UA_L1, (UA_L1S, [(0, 128, 1), (128, 256, 4)]))
        conv_A(SA_G1, UA_G1, (UA_G1S, [(0, 256, 5)]))
        for blk in range(8):
            conv_A(SA_WO + blk, UA_WO + blk)
        for blk in range(16):
            conv_A(SA_KV + blk, UA_KV + blk)
        for blk in range(8):
            conv_A(SA_Q + blk, UA_Q + blk)
        for blk in range(8):
            conv_A(SA_DWO + blk, UA_DWO + blk)
        for u in range(NUB):
            t32, b32 = r32.next()
            P.dma("sp", t32[:, 0:BW], wB_src[u], writes=[b32])
            t16, b16 = r16.next()
            if u % 2 == 0:
                P.op("act", lambda e, o=t16, i=t32: e.mul(o[:, 0:BW], i[:, 0:BW], 0.5), [b32], [b16])
            else:
                ts("dve", t16[:, 0:BW], t32[:, 0:BW], 0.5, None, ALU.mult, None, [b32], [b16])
            P.dma(rq.next(), wB[u], t16[:, 0:BW], reads=[b16], writes=[B_wB[u]])
        P.barrier()
    if STOP == 0:
        P.dry = True

    es = ExitStack()

    def sb(name, shape, dt=F32):
        return es.enter_context(nc.sbuf_tensor(name, shape, dt))

    with es:
        xres = sb("xres", [128, DC, TWP]); Bx = [Buf() for _ in range(DC)]
        xb = sb("xb", [128, DC, TWP], BF16); Bxb = [Buf() for _ in range(DC)]
        hid = sb("hid", [128, 48, TWP], BF16); Bh = [Buf() for _ in range(48)]
        sA = sb("sA", [128, DC, TWP], BF16); BsA = [Buf() for _ in range(DC)]
        sB = sb("sB", [128, DC, TWP], BF16); BsB = [Buf() for _ in range(DC)]
        tokst = sb("tokst", [128, 2, D]); Btk = [[Buf() for _ in range(DC)] for _ in range(2)]
        wslots = [(sb("wslot%d" % i, [128, 4096], BF16), Buf()) for i in range(3)]
        NS = len(wslots)
        banks = [(es.enter_context(nc.psum_tensor("bank%d" % i, [128, 512], F32)), Buf()) for i in range(8)]
        G = Rot([banks[i] for i in (0, 1, 2, 3)])
        Obanks = [banks[4], banks[5], banks[6], banks[7]]
        cst = sb("cst", [128, C_W]); Bc = Buf()
        pc = sb("pc", [128, NPV, 16]); Bpc = Buf()
        omka = sb("omka", [128, 16]); Bomka = Buf()
        w2aug = sb("w2aug_s", [128, D]); Bw2 = Buf()
        a2b = sb("a2b", [128, D], BF16); Ba2 = Buf()
        g2b = sb("g2b", [128, 2, D], BF16); Bg2 = Buf()
        Hst = sb("Hst", [128, 16, 128]); BH = [Buf() for _ in range(16)]
        carry = sb("carry", [128, 16]); Bcar = Buf()
        negv = sb("negv", [128, 2]); Bnegv = Buf()
        lamt = sb("lamt", [128, 256]); Blam = Buf()
        lamc = sb("lamc", [128, 8]); Blamc = Buf()
        subg = sb("subg_s", [128, 128]); Bsubg = Buf()
        hTw = sb("hTw", [128, TWP]); BhTw = Buf()
        hTa = sb("hTa", [128, TWP], BF16); BhTa = Buf()
        hTg = sb("hTg", [128, 2, TWP], BF16); BhTg = Buf()
        stat = [(sb("stat%d" % i, [128, TWP]), Buf()) for i in range(4)]
        tmpA = Rot([(sb("tmpA%d" % i, [128, TWP]), Buf()) for i in range(3)])
        tmpB = Rot([(sb("tmpB%d" % i, [128, TWP], BF16), Buf()) for i in range(3)])
        RW = {n: (sb("rw_" + n, [128, TWP]), Buf()) for n in
              ["a", "g", "pinc", "pexc", "pinv", "kkr", "rn", "kk", "kmod", "b", "bv", "yT", "yc"]}
        RW["sq"], RW["tmp"], RW["rk"], RW["rstd"] = stat[0], stat[1], stat[2], stat[3]
        logd = sb("logd", [128, 2, 128]); Blogd = Buf()
        NCHM = TWP // 64
        RKfp = [(sb("RKfp%d" % i, [128, NCHM, 2, 128]), Buf()) for i in range(1)]
        Ktfp = [(sb("Ktfp%d" % i, [128, NCHM, 128]), Buf()) for i in range(1)]
        Btfp = [(sb("Btfp%d" % i, [128, NCHM, 128]), Buf()) for i in range(1)]
        Vtfp = [(sb("Vtfp%d" % i, [128, NCHM, 128]), Buf()) for i in range(1)]
        cmL = [(sb("cmL%d" % i, [128, 256]), Buf()) for i in range(4)]
        cmX = Rot([(sb("cmX%d" % i, [128, 128]), Buf()) for i in range(3)])
        cmM = Rot([(sb("cmM%d" % i, [128, 256]), Buf()) for i in range(3)])
        ktp = Rot([(sb("ktp%d" % i, [128, 1024], BF16), Buf()) for i in range(2)])
        vtp_items = [(sb("vtp%d" % i, [128, 8, 129], BF16), Buf()) for i in range(2)]
        vtp = Rot(vtp_items)
        ptp = Rot([(sb("ptp%d" % i, [128, 512], BF16), Buf()) for i in range(3)])
        ckst = sb("ckst", [128, 8, 128]); Bckst = Buf()
        osm = Rot([(sb("osm%d" % i, [128, 136]), Buf()) for i in range(4)])

        ones_full = sb("ones_full", [128, 128]); Bones = Buf()
        mset("pool", ones_full[:], 1.0, [Bones])
        ident = cst[:, C_ID:C_ID + 128]
        bones = cst[:, C_BO:C_BO + 128]
        mSLn = cst[:, C_MSL:C_MSL + 128]
        mK = cst[:, C_MK:C_MK + 256]
        mBn = cst[:, C_MB:C_MB + 256]
        triI = cst[:, C_TI:C_TI + 64]
        triE = cst[:, C_TE:C_TE + 64]

        P.dma("sp", cst[:], consts_d, writes=[Bc])
        P.dma("sp", pc[:], pcols_d, writes=[Bpc])
        P.dma("sp", w2aug[:], w2aug_d, writes=[Bw2])
        P.dma("sp", tokst[:, 0, :], a2p_d, writes=Btk[0])
        cp("dve", a2b[:], tokst[:, 0, :], Btk[0], [Ba2])
        for kc in range(2):
            P.dma("sp", tokst[:, 1, :], g2p_d[:, kc, :], writes=Btk[1])
            cp("dve", g2b[:, kc, :], tokst[:, 1, :], Btk[1], [Bg2])
        ts("dve", omka[:], pc[:, PV_KA, :], -1.0, 1.0, ALU.mult, ALU.add, [Bpc], [Bomka])
        mset("pool", negv[:], NEG_E, [Bnegv])
        P.op("pool", lambda e: e.affine_select(negv[:, 1:2], negv[:, 1:2], pattern=[[0, 1]], compare_op=ALU.is_gt,
                                               fill=0.0, base=NSAMP, channel_multiplier=-1), [Bnegv], [Bnegv])
        for (t_, b_) in RKfp + Ktfp + Btfp + Vtfp:
            mset("pool", t_[:], 0.0, [b_])
        for (t_, b_) in vtp_items:
            mset("pool", t_[:], 1.0, [b_])
        P.dma("sp", lamt[:], dlam_d.partition_broadcast(128), writes=[Blam])
        P.dma("sp", subg[:], subg_d.partition_broadcast(128), writes=[Bsubg])
        tt("dve", lamt[:, 0:64], lamt[:, 0:64], lamt[:, 64:128], ALU.mult, [Blam], [Blam])
        tt("dve", lamt[:, 128:192], lamt[:, 128:192], lamt[:, 192:256], ALU.mult, [Blam], [Blam])
        P.op("dve", lambda e: e.reduce_sum(lamc[:, 0:1], lamt[:, 0:64], axis=AX.X), [Blam], [Blamc])
        P.op("dve", lambda e: e.reduce_sum(lamc[:, 1:2], lamt[:, 128:192], axis=AX.X), [Blam], [Blamc])
        act(lamc[:, 2:4], lamc[:, 0:2], AF.Exp, [Blamc], [Blamc])
        tt("dve", lamc[:, 4:5], lamc[:, 3:4], lamc[:, 2:3], ALU.subtract, [Blamc], [Blamc])
        ts("dve", lamc[:, 5:6], lamc[:, 4:5], -LAMBDA_INIT, None, ALU.add, None, [Blamc], [Blamc])
        ts("dve", subg[:], subg[:], 1.0 - LAMBDA_INIT, None, ALU.mult, None, [Bsubg], [Bsubg])
        neglam = lamc[:, 5:6]

        class WS:
            seq = []
            pos = 0
            issued = 0

        def wissue(i):
            kind, idx = WS.seq[i]
            t_, b_ = wslots[i % NS]
            if kind == "A":
                P.dma("sp", t_[:, 0:4096], wA[idx], reads=[B_wA[idx]], writes=[b_])
            else:
                P.dma("sp", t_[:, 0:BW], wB[idx], reads=[B_wB[idx]], writes=[b_])

        def wget(kind, idx, hold=0):
            if P.dry:
                WS.seq.append((kind, idx))
                return wslots[0]
            assert WS.seq[WS.pos] == (kind, idx)
            while WS.issued < min(len(WS.seq), WS.pos + NS - hold):
                wissue(WS.issued)
                WS.issued += 1
            r = wslots[WS.pos % NS]
            WS.pos += 1
            return r

        def uA(slot):
            return slot[:, 0:4096].rearrange("p (c f) -> p c f", c=DC)

        def uB(slot):
            return slot[:, 0:BW].rearrange("p (m d) -> p m d", m=11)

        def proj_fm(unit_list, TW, evac):
            for blk, parts in enumerate(unit_list):
                slots = [(wget("A", ui, hold=pi), src, sbufs) for pi, (ui, src, sbufs) in enumerate(parts)]
                for half in range(2):
                    bank, bb = G.next()
                    n = len(slots) * DC
                    k = 0
                    for (st, sbf), src, sbufs in slots:
                        u = uA(st)
                        for dc in range(DC):
                            mm(bank[:, 0:TW], u[:, dc, half * 128:(half + 1) * 128], src[:, dc, 0:TW],
                               k == 0, k == n - 1, [sbf, sbufs[dc]], [bb])
                            k += 1
                    evac(2 * blk + half, bank, bb)

        def layer_norm(li, TW):
            S, Sb = G.next()
            S2, S2b = G.next()
            for c in range(DC):
                sq, sqb = tmpA.next()
                act(sq[:, 0:TW], xres[:, c, 0:TW], AF.Square, [Bx[c]], [sqb])
                mm(S[:, 0:TW], ones_full[:], xres[:, c, 0:TW], c == 0, c == DC - 1, [Bones, Bx[c]], [Sb])
                mm(S2[:, 0:TW], ones_full[:], sq[:, 0:TW], c == 0, c == DC - 1, [Bones, sqb], [S2b])
            (m_, mb), (q_, qb_), (v_, vb), (r_, rb) = stat[0], stat[1], stat[2], stat[3]
            ts("dve", m_[:, 0:TW], S[:, 0:TW], 1.0 / D, None, ALU.mult, None, [Sb], [mb])
            tt("dve", q_[:, 0:TW], m_[:, 0:TW], m_[:, 0:TW], ALU.mult, [mb], [qb_])
            ts("dve", v_[:, 0:TW], S2[:, 0:TW], 1.0 / D, LN_EPS, ALU.mult, ALU.add, [S2b], [vb])
            tt("dve", v_[:, 0:TW], v_[:, 0:TW], q_[:, 0:TW], ALU.subtract, [vb, qb_], [vb])
            P.op("act", lambda e: e.sqrt(v_[:, 0:TW], v_[:, 0:TW]), [vb], [vb])
            recip(r_[:, 0:TW], v_[:, 0:TW], [vb], [rb])
            for c in range(DC):
                t1, t1b = tmpA.next()
                tt("dve", t1[:, 0:TW], xres[:, c, 0:TW], m_[:, 0:TW], ALU.subtract, [Bx[c], mb], [t1b])
                tt("pool", t1[:, 0:TW], t1[:, 0:TW], r_[:, 0:TW], ALU.mult, [t1b, rb], [t1b])
                act(xres[:, c, 0:TW], t1[:, 0:TW], AF.Identity, [t1b, Bpc], [Bx[c]],
                    scale=pc[:, PV_LNG + li, c:c + 1], bias=pc[:, PV_LNB + li, c:c + 1])
                ts("dve", xb[:, c, 0:TW], t1[:, 0:TW], pc[:, PV_LNG + li, c:c + 1], pc[:, PV_LNB + li, c:c + 1],
                   ALU.mult, ALU.add, [t1b, Bpc], [Bxb[c]])


        def ffn(fi, TW):
            for m in range(MC):
                sg, sgb = wget("A", UA_FFN + fi * 44 + m)
                ug = uA(sg)
                bank, bb = G.next()
                bank2, bb2 = G.next()
                for dc in range(DC):
                    mm(bank[:, 0:TW], ug[:, dc, 0:128], xb[:, dc, 0:TW], dc == 0, dc == DC - 1, [sgb, Bxb[dc]], [bb])
                for dc in range(DC):
                    mm(bank2[:, 0:TW], ug[:, dc, 128:256], xb[:, dc, 0:TW], dc == 0, dc == DC - 1, [sgb, Bxb[dc]], [bb2])
                t1, t1b = tmpA.next()
                act(t1[:, 0:TW], bank[:, 0:TW], AF.Silu, [bb], [t1b])
                tt("dve", hid[:, m, 0:TW], t1[:, 0:TW], bank2[:, 0:TW], ALU.mult, [t1b, bb2], [Bh[m]])
            for g in range(8):
                (bA, bAb) = G.next()
                (bB, bBb) = G.next()
                accs = [(bA[:, 0:TW], bAb), (bB[:, 0:TW], bBb)]
                for q in range(4):
                    sw, swb = wget("B", fi * 32 + g * 4 + q)
                    u = uB(sw)
                    for oo in range(2):
                        for mm_ in range(11):
                            m = q * 11 + mm_
                            mm(accs[oo][0], u[:, mm_, oo * 128:(oo + 1) * 128], hid[:, m, 0:TW],
                               q == 0 and mm_ == 0, q == 3 and mm_ == 10, [swb, Bh[m]], [accs[oo][1]])
                for oo in range(2):
                    c = g * 2 + oo
                    stt("dve", xres[:, c, 0:TW], xres[:, c, 0:TW], ALPHA, accs[oo][0], ALU.mult, ALU.add, [Bx[c], accs[oo][1]], [Bx[c]])

        def resid_evac(TW):
            def f(c, bank, bb):
                stt("dve", xres[:, c, 0:TW], xres[:, c, 0:TW], ALPHA, bank[:, 0:TW], ALU.mult, ALU.add, [Bx[c], bb], [Bx[c]])
            return f

        def rwkv(TW, ntok, nvcol, shift_out):
            nch = TW // 64
            nb = (TW + 127) // 128
            ntp = min(TW, 128)
            tt("dve", sA[:, :, 1:TW], xres[:, :, 0:TW - 1], xres[:, :, 1:TW], ALU.subtract, Bx, BsA)
            tt("dve", sA[:, :, 0:1], carry[:, :].unsqueeze(2), xres[:, :, 0:1], ALU.subtract, Bx + [Bcar], BsA)
            cp("pool", carry[:, :].unsqueeze(2), xres[:, :, ntok - 1:ntok], Bx, [Bcar])
            if shift_out is not None:
                P.dma("pool", shift_out, carry[:], reads=[Bcar], is_output=True)
            def sr(n):
                if TW == TWP:
                    stage(n)
            sr(201)
            for j in range(3):
                ul = [[(UA_RKV + j * 8 + blk, xb, Bxb), (UA_RKVS + j * 8 + blk, sA, BsA)] for blk in range(8)]

                def ev(c, bank, bb, j=j):
                    eng = "act" if c % 2 == 0 else "dve"
                    cp(eng, hid[:, j * 16 + c, 0:TW], bank[:, 0:TW], [bb], [Bh[j * 16 + c]])
                proj_fm(ul, TW, ev)
            if ntok < TW:
                mset("pool", hid[:, 16:32, ntok:TW], 0.0, Bh[16:32])
            sr(202)
            for which in range(2):
                (ua_, uab), (ub_, ubb) = (wget("A", UA_L1), wget("A", UA_L1S, hold=1)) if which == 0 else (wget("A", UA_G1), wget("A", UA_G1S, hold=1))
                for half in range(2):
                    bank, bb = G.next()
                    k = 0
                    for (st, sbf, src, sbufs) in [(ua_, uab, xb, Bxb), (ub_, ubb, sA, BsA)]:
                        u = uA(st)
                        for dc in range(DC):
                            mm(bank[:, 0:TW], u[:, dc, half * 128:(half + 1) * 128], src[:, dc, 0:TW], k == 0, k == 2 * DC - 1, [sbf, sbufs[dc]], [bb])
                            k += 1
                    if which == 0 and half == 0:
                        act(hTw[:, 0:TW], bank[:, 0:TW], AF.Tanh, [bb], [BhTw])
                        mset("pool", hTw[0:32, 0:TW], 0.0, [BhTw])
                        mset("pool", hTw[0:1, 0:TW], 1.0, [BhTw])
                    elif which == 0:
                        cp("dve", hTa[:, 0:TW], bank[:, 0:TW], [bb], [BhTa])
                    else:
                        act(hTg[:, half, 0:TW], bank[:, 0:TW], AF.Sigmoid, [bb], [BhTg])
            sr(203)
            for c in range(16):
                if c == 1:
                    sr(207)
                if c == 2:
                    sr(208)
                bar()
                rT = hid[:, c, 0:TW]; kT = hid[:, 16 + c, 0:TW]; vT = hid[:, 32 + c, 0:TW]
                Br, Bk, Bv = Bh[c], Bh[16 + c], Bh[32 + c]
                cs = slice(c * 128, (c + 1) * 128)
                a_, ab = RW["a"]; g_, gb = RW["g"]
                bank, bb = G.next()
                mm(bank[:, 0:TW], a2b[:, cs], hTa[:, 0:TW], True, True, [Ba2, BhTa], [bb])
                for kc in range(2):
                    mm(bank[:, 256:256 + TW], g2b[:, kc, cs], hTg[:, kc, 0:TW], kc == 0, kc == 1, [Bg2, BhTg], [bb])
                act(a_[:, 0:TW], bank[:, 0:TW], AF.Sigmoid, [bb, Bpc], [ab], bias=pc[:, PV_A0, c:c + 1], scale=1.0)
                cp("act", g_[:, 0:TW], bank[:, 256:256 + TW], [bb], [gb])
                bank, bb = G.next()
                for tb in range(nb):
                    mm(bank[0:ntp, tb * 128:(tb + 1) * 128], hTw[:, tb * 128:tb * 128 + ntp], w2aug[:, cs], True, True, [BhTw, Bw2], [bb])
                for tb in range(nb):
                    act(logd[0:ntp, tb, :], bank[0:ntp, tb * 128:(tb + 1) * 128], AF.Sigmoid, [bb], [Blogd])
                    ts("dve", logd[0:ntp, tb, :], logd[0:ntp, tb, :], negv[0:ntp, nvcol:nvcol + 1], None, ALU.mult, None, [Blogd, Bnegv], [Blogd])
                bank, bb = G.next()
                for tb in range(nb):
                    mm(bank[:, tb * 128:tb * 128 + ntp], logd[0:ntp, tb, :], cst[0:ntp, C_MK + 128:C_MK + 128 + ntp], True, True, [Blogd, Bc], [bb])
                    mm(bank[:, 256 + tb * 128:256 + tb * 128 + ntp], logd[0:ntp, tb, :], cst[0:ntp, C_MK:C_MK + ntp], True, True, [Blogd, Bc], [bb])
                pinc, pincb = RW["pinc"]; pexc, pexcb = RW["pexc"]; pinv, pinvb = RW["pinv"]
                act(pinc[:, 0:TW], bank[:, 0:TW], AF.Exp, [bb], [pincb])
                act(pexc[:, 0:TW], bank[:, 256:256 + TW], AF.Exp, [bb], [pexcb])
                act(pinv[:, 0:TW], bank[:, 0:TW], AF.Exp, [bb], [pinvb], scale=-1.0)
                sr(204)
                kkr, kkrb = RW["kkr"]; sq, sqb = RW["sq"]; rn, rnb = RW["rn"]; kk, kkb = RW["kk"]
                ts("dve", kkr[:, 0:TW], kT, pc[:, PV_KK, c:c + 1], None, ALU.mult, None, [Bk, Bpc], [kkrb])
                act(sq[:, 0:TW], kkr[:, 0:TW], AF.Square, [kkrb], [sqb])
                bank, bb = G.next()
                mm(bank[:, 0:TW], bones, sq[:, 0:TW], True, True, [Bc, sqb], [bb])
                ts("dve", rn[:, 0:TW], bank[:, 0:TW], 1e-24, None, ALU.max, None, [bb], [rnb])
                P.op("act", lambda e, o=rn, w=TW: e.sqrt(o[:, 0:w], o[:, 0:w]), [rnb], [rnb])
                recip(rn[:, 0:TW], rn[:, 0:TW], [rnb], [rnb])
                tt("dve", kk[:, 0:TW], kkr[:, 0:TW], rn[:, 0:TW], ALU.mult, [kkrb, rnb], [kkb])
                tmp, tmpb = RW["tmp"]; kmod, kmodb = RW["kmod"]; b_, bbf = RW["b"]; rk, rkb = RW["rk"]; bv, bvb = RW["bv"]
                ts("pool", tmp[:, 0:TW], a_[:, 0:TW], pc[:, PV_KA, c:c + 1], omka[:, c:c + 1], ALU.mult, ALU.add, [ab, Bpc, Bomka], [tmpb])
                tt("pool", kmod[:, 0:TW], kT, tmp[:, 0:TW], ALU.mult, [Bk, tmpb], [kmodb])
                tt("pool", b_[:, 0:TW], kk[:, 0:TW], a_[:, 0:TW], ALU.mult, [kkb, ab], [bbf])
                stt("dve", rk[:, 0:TW], rT, pc[:, PV_RK, c:c + 1], kmod[:, 0:TW], ALU.mult, ALU.mult, [Br, Bpc, kmodb], [rkb])
                mm(bank[:, 256:256 + TW], bones, rk[:, 0:TW], True, True, [Bc, rkb], [bb])
                tt("dve", bv[:, 0:TW], bank[:, 256:256 + TW], vT, ALU.mult, [bb, Bv], [bvb])
                (RK, RKb), (Kt, Ktb), (Bt, Btb), (Vt, Vtb) = RKfp[0], Ktfp[0], Btfp[0], Vtfp[0]
                for h in range(2):
                    hs = slice(h * 64, h * 64 + 64)

                    def v3(ap):
                        return ap.rearrange("p (j t) -> p j t", t=64)
                    tt("dve", RK[hs, 0:nch, 0, hs], v3(kk[hs, 0:TW]), v3(pexc[hs, 0:TW]), ALU.mult, [kkb, pexcb], [RKb])
                    tt("pool", RK[hs, 0:nch, 1, hs], v3(hid[hs, c, 0:TW]), v3(pinc[hs, 0:TW]), ALU.mult, [Br, pincb], [RKb])
                    tt("dve", Kt[hs, 0:nch, hs], v3(kmod[hs, 0:TW]), v3(pinv[hs, 0:TW]), ALU.mult, [kmodb, pinvb], [Ktb])
                    tt("pool", Bt[hs, 0:nch, hs], v3(b_[hs, 0:TW]), v3(pinv[hs, 0:TW]), ALU.mult, [bbf, pinvb], [Btb])
                    cp("act", Vt[hs, 0:nch, hs], v3(hid[hs, 32 + c, 0:TW]), [Bv], [Vtb])
                sr(205)
                yT, yTb = RW["yT"]
                H0 = Hst[:, c, :]
                for j in range(nch):
                    bank, bb = G.next()
                    tr(bank[:, 0:128], Kt[:, j, :], ident, [Ktb, Bc], [bb])
                    tr(bank[:, 128:256], Bt[:, j, :], ident, [Btb, Bc], [bb])
                    tr(bank[:, 256:384], Vt[:, j, :], ident, [Vtb, Bc], [bb])
                    KB, KBb = cmL[0]; VM, Vbdb = cmL[1]; Vbd = VM
                    cp("act", KB[:, 0:128], bank[:, 0:128], [bb], [KBb])
                    P.op("act", lambda e, o=KB, i=bank: e.mul(o[:, 128:256], i[:, 128:256], -1.0), [bb], [KBb])
                    cp("act", Vbd[:, 0:128], bank[:, 256:384], [bb], [Vbdb])
                    bank, bb = G.next()
                    mm(bank[:, 0:128], RK[:, j, 0, :], Bt[:, j, :], True, True, [RKb, Btb], [bb])
                    M0, M0b = _Shift(VM), Vbdb
                    tt("dve", M0[:, 0:128], bank[:, 0:128], mSLn, ALU.mult, [bb, Bc], [M0b])
                    bank, bb = G.next()
                    mm(bank[:, 0:256], Kt[:, j, :], RK[:, j, :, :].rearrange("p a b -> p (a b)"), True, True, [Ktb, RKb], [bb])
                    GK, GKb = cmL[2]
                    tt("dve", GK[:, 0:256], bank[:, 0:256], mK, ALU.mult, [bb, Bc], [GKb])
                    bank, bb = G.next()
                    mm(bank[:, 0:256], Bt[:, j, :], RK[:, j, :, :].rearrange("p a b -> p (a b)"), True, True, [Btb, RKb], [bb])
                    GB, GBb = cmL[3]
                    tt("dve", GB[:, 0:256], bank[:, 0:256], mBn, ALU.mult, [bb, Bc], [GBb])
                    bank, bb = G.next()
                    mm(bank[:, 0:128], RK[:, j, 0, :], H0, True, False, [RKb, BH[c]], [bb])
                    mm(bank[:, 0:128], GK[:, 0:128], Vbd[:, 0:128], False, True, [GKb, Vbdb], [bb])
                    X, Xb = cmX.next()
                    cp("act", X[:, 0:128], bank[:, 0:128], [bb], [Xb])
                    Mc, Mcb = M0, M0b
                    McT, McTb = GB, GBb
                    for lvl in range(6):
                        bank, bb = G.next()
                        mm(bank[:, 0:128], McT[:, 0:128], X[:, 0:128], True, True, [McTb, Xb], [bb])
                        if lvl < 5:
                            mm(bank[:, 128:256], Mc[:, 0:128], McT[:, 0:128], True, True, [Mcb, McTb], [bb])
                            if lvl < 4:
                                mm(bank[:, 256:384], McT[:, 0:128], Mc[:, 0:128], True, True, [Mcb, McTb], [bb])
                        Xn, Xnb = cmX.next()
                        tt("dve", Xn[:, 0:128], bank[:, 0:128], X[:, 0:128], ALU.add, [bb, Xb], [Xnb])
                        X, Xb = Xn, Xnb
                        if lvl < 5:
                            Mn, Mnb = cmM.next()
                            cp("dve", Mn[:, 0:128], bank[:, 128:256], [bb], [Mnb])
                            if lvl < 4:
                                cp("dve", Mn[:, 128:256], bank[:, 256:384], [bb], [Mnb])
                            McT, McTb = Mn, Mnb
                            Mc, Mcb = _Shift(Mn), Mnb
                    U, Ub = X, Xb
                    bank, bb = G.next()
                    mm(bank[:, 0:128], H0, RK[:, j, 1, :], True, False, [BH[c], RKb], [bb])
                    mm(bank[:, 0:128], Vbd[:, 0:128], GK[:, 128:256], False, False, [Vbdb, GKb], [bb])
                    mm(bank[:, 0:128], U[:, 0:128], GB[:, 128:256], False, True, [Ub, GBb], [bb])
                    mm(bank[:, 128:256], KB[:, 0:128], Vbd[:, 0:128], True, False, [KBb, Vbdb], [bb])
                    mm(bank[:, 128:256], KB[:, 128:256], U[:, 0:128], False, False, [KBb, Ub], [bb])
                    mm(bank[:, 128:256], ident, H0, False, True, [Bc, BH[c]], [bb])
                    cp("act", yT[0:64, j * 64:(j + 1) * 64], bank[0:64, 0:64], [bb], [yTb])
                    cp("act", yT[64:128, j * 64:(j + 1) * 64], bank[64:128, 64:128], [bb], [yTb])
                    act(H0, bank[:, 128:256], AF.Identity, [bb, pincb], [BH[c]], scale=pinc[:, j * 64 + 63:j * 64 + 64])
                sr(206)
                yc, ycb = RW["yc"]; rstd, rstdb = RW["rstd"]
                bank, bb = G.next()
                mm(bank[:, 0:TW], bones, yT[:, 0:TW], True, True, [Bc, yTb], [bb])
                stt("dve", yc[:, 0:TW], bank[:, 0:TW], -1.0 / 64, yT[:, 0:TW], ALU.mult, ALU.add, [bb, yTb], [ycb])
                act(sq[:, 0:TW], yc[:, 0:TW], AF.Square, [ycb], [sqb])
                mm(bank[:, 256:256 + TW], bones, sq[:, 0:TW], True, True, [Bc, sqb], [bb])
                ts("dve", rstd[:, 0:TW], bank[:, 256:256 + TW], 1.0 / 64, GN_EPS, ALU.mult, ALU.add, [bb], [rstdb])
                P.op("act", lambda e, o=rstd, w=TW: e.sqrt(o[:, 0:w], o[:, 0:w]), [rstdb], [rstdb])
                recip(rstd[:, 0:TW], rstd[:, 0:TW], [rstdb], [rstdb])
                tt("dve", yc[:, 0:TW], yc[:, 0:TW], rstd[:, 0:TW], ALU.mult, [ycb, rstdb], [ycb])
                ts("pool", yc[:, 0:TW], yc[:, 0:TW], pc[:, PV_LXG, c:c + 1], pc[:, PV_LXB, c:c + 1], ALU.mult, ALU.add, [ycb, Bpc], [ycb])
                tt("pool", yc[:, 0:TW], yc[:, 0:TW], bv[:, 0:TW], ALU.add, [ycb, bvb], [ycb])
                tt("dve", sB[:, c, 0:TW], yc[:, 0:TW], g_[:, 0:TW], ALU.mult, [ycb, gb], [BsB[c]])
            proj_fm([[(UA_WO + blk, sB, BsB)] for blk in range(8)], TW, resid_evac(TW))

        class _Shift:
            def __init__(self, base):
                self.base = base

            def __getitem__(self, idx):
                assert idx == (slice(None), slice(0, 128))
                return self.base[:, 128:256]

        def attention(TW, prompt, t0):
            nqb = (TW + 127) // 128
            qw = min(TW, 128)
            QT, BQ = sA, BsA
            if prompt:
                nkb = (t0 + TW) // 128
                groups = [("scr", n0, min(8, nkb - n0), 128) for n0 in range(0, nkb, 8)]
            else:
                groups = [("cache", 0, 8, 128), ("own", 0, 1, NSAMP)]

            def valid(qb, n):
                return (not prompt) or n <= (t0 // 128 + qb)
            total = {}
            for qb in range(nqb):
                total[qb] = sum(1 for (_, n0, nblk, _) in groups for n in range(n0, n0 + nblk) if valid(qb, n))
            for c in range(16):
                cs = slice(c * 128, (c + 1) * 128)
                done = {(qb, h): 0 for qb in range(nqb) for h in range(2)}
                for (kind, n0, nblk, nk) in groups:
                    KT, KTb = ktp.next(); V, Vb = vtp.next()
                    if kind == "scr":
                        P.dma("pool", KT[:, 0:nblk * 128], KTscr[cs, n0 * 128:(n0 + nblk) * 128], reads=[B_KTscr], writes=[KTb])
                        P.dma("pool", V[:, 0:nblk, 0:128], Vscr[n0 * 128:(n0 + nblk) * 128, cs].rearrange("(n p) e -> p n e", p=128), reads=[B_Vscr], writes=[Vb])
                    elif kind == "cache":
                        P.dma("pool", ckst[:], ck[:, cs].rearrange("(n p) e -> p n e", p=128), writes=[Bckst])
                        for gq in range(2):
                            bank, bb = G.next()
                            for i in range(4):
                                tr(bank[:, i * 128:(i + 1) * 128], ckst[:, gq * 4 + i, :], ident, [Bckst, Bc], [bb])
                            cp("act" if gq == 0 else "dve", KT[:, gq * 512:(gq + 1) * 512], bank[:, 0:512], [bb], [KTb])
                        P.dma("pool", V[:, 0:8, 0:128], cv[:, cs].rearrange("(n p) e -> p n e", p=128), writes=[Vb])
                    else:
                        P.dma("pool", KT[:, 0:64], KTs[cs, :], reads=[B_KTs], writes=[KTb])
                        P.dma("pool", V[0:64, 0, 0:128], Vs[:, cs], reads=[B_Vs], writes=[Vb])
                    pend = [None]
                    for qb in range(nqb):
                        vblocks = [n for n in range(n0, n0 + nblk) if valid(qb, n)]
                        if not vblocks:
                            continue
                        for h in range(2):
                            Ob, Obb = Obanks[qb * 2 + h]
                            hs = slice(h * 64, h * 64 + 64)
                            for b0 in range(0, len(vblocks), 4):
                                batch = vblocks[b0:b0 + 4]
                                bank, bb = G.next()
                                for i, n in enumerate(batch):
                                    nl = n - n0
                                    mm(bank[0:nk, i * 128:i * 128 + qw], KT[hs, nl * 128:nl * 128 + nk], QT[hs, c, qb * 128:qb * 128 + qw],
                                       True, True, [KTb, BQ[c]], [bb])
                                PT, PTb = ptp.next()
                                if qw == 128 and nk == 128:
                                    act(PT[:, 0:len(batch) * 128], bank[:, 0:len(batch) * 128], AF.Exp, [bb], [PTb], scale=0.125)
                                else:
                                    for i in range(len(batch)):
                                        act(PT[0:nk, i * 128:i * 128 + qw], bank[0:nk, i * 128:i * 128 + qw], AF.Exp, [bb], [PTb], scale=0.125)
                                for i, n in enumerate(batch):
                                    if prompt and n == t0 // 128 + qb:
                                        mset("pool", PT[64:128, i * 128:i * 128 + 64], 0.0, [PTb])
                                def pv(batch=batch, PT=PT, PTb=PTb, Ob=Ob, Obb=Obb, qb=qb, h=h):
                                    for i, n in enumerate(batch):
                                        nl = n - n0
                                        d_ = done[(qb, h)]
                                        mm(Ob[0:qw, 0:129], PT[0:nk, i * 128:i * 128 + qw], V[0:nk, nl, 0:129],
                                           d_ == 0, d_ == total[qb] - 1, [PTb, Vb], [Obb])
                                        done[(qb, h)] = d_ + 1
                                if pend[0] is not None:
                                    pend[0]()
                                pend[0] = pv
                    if pend[0] is not None:
                        pend[0]()
                        pend[0] = None
                for qb in range(nqb):
                    Ob, Obb = Obanks[qb * 2]
                    Ob2, Ob2b = Obanks[qb * 2 + 1]
                    o_, ob_ = osm.next()
                    sm, smb = osm.next()
                    recip(sm[0:qw, 0:1], Ob[0:qw, 128:129], [Obb], [smb])
                    recip(sm[0:qw, 1:2], Ob2[0:qw, 128:129], [Ob2b], [smb])
                    tt("dve", sm[0:qw, 2:3], sm[0:qw, 1:2], neglam[0:qw, :], ALU.mult, [smb, Blamc], [smb])
                    ts("dve", o_[0:qw, 0:128], Ob[0:qw, 0:128], sm[0:qw, 0:1], None, ALU.mult, None, [Obb, smb], [ob_])
                    stt("dve", o_[0:qw, 0:128], Ob2[0:qw, 0:128], sm[0:qw, 2:3], o_[0:qw, 0:128], ALU.mult, ALU.add, [Ob2b, smb, ob_], [ob_])
                    sq2, sq2b = osm.next()
                    tt("dve", sq2[0:qw, 0:128], o_[0:qw, 0:128], o_[0:qw, 0:128], ALU.mult, [ob_], [sq2b])
                    P.op("dve", lambda e, s_=sm, q_=sq2, w=qw: e.reduce_sum(s_[0:w, 3:4], q_[0:w, 0:128], axis=AX.X), [sq2b], [smb])
                    ts("dve", sm[0:qw, 4:5], sm[0:qw, 3:4], 1.0 / 128, LN_EPS, ALU.mult, ALU.add, [smb], [smb])
                    P.op("act", lambda e, s_=sm, w=qw: e.sqrt(s_[0:w, 4:5], s_[0:w, 4:5]), [smb], [smb])
                    recip(sm[0:qw, 5:6], sm[0:qw, 4:5], [smb], [smb])
                    stt("dve", tokst[0:qw, qb, cs], o_[0:qw, 0:128], sm[0:qw, 5:6], subg[0:qw, :], ALU.mult, ALU.mult, [ob_, smb, Bsubg], [Btk[qb][c]])
            for qb in range(nqb):
                for g4 in range(4):
                    bank, bb = G.next()
                    for i in range(4):
                        c = g4 * 4 + i
                        tr(bank[:, i * 128:(i + 1) * 128], tokst[:, qb, c * 128:(c + 1) * 128], ident, [Btk[qb][c], Bc], [bb])
                    for i in range(4):
                        c = g4 * 4 + i
                        cp("act" if g4 % 2 == 0 else "dve", sB[:, c, qb * 128:qb * 128 + qw], bank[:, i * 128:i * 128 + qw], [bb], [BsB[c]])

        def tile_pass(prompt, it):
            TW = TWP if prompt else 64
            ntok = TW if prompt else NSAMP
            nb = (TW + 127) // 128
            ntp = min(TW, 128)
            t0 = it * TWP

            def st_(n):
                stage(n + (100 if prompt else 0))
                bar()
            for b in range(nb):
                if prompt:
                    P.dma("pool", tokst[:, b, :], xp[t0 + b * 128:t0 + (b + 1) * 128, :], writes=Btk[b])
                else:
                    mset("pool", tokst[:, 0, :], 0.0, Btk[0])
                    P.dma("sp", tokst[0:NSAMP, 0, :], xs, writes=Btk[0])
                for g4 in range(4):
                    bank, bb = G.next()
                    for i in range(4):
                        c = g4 * 4 + i
                        tr(bank[:, i * 128:(i + 1) * 128], tokst[:, b, c * 128:(c + 1) * 128], ident, [Btk[b][c], Bc], [bb])
                    for i in range(4):
                        c = g4 * 4 + i
                        cp("dve", xres[:, c, b * 128:b * 128 + ntp], bank[:, i * 128:i * 128 + ntp], [bb], [Bx[c]])
                        cp("act", xb[:, c, b * 128:b * 128 + ntp], xres[:, c, b * 128:b * 128 + ntp], [Bx[c]], [Bxb[c]])
            st_(2)
            ffn(0, TW); layer_norm(0, TW)
            st_(3)
            rwkv(TW, ntok, 0 if prompt else 1, (shp if it == NT - 1 else None) if prompt else shs)
            st_(4)
            layer_norm(1, TW)
            ffn(1, TW); layer_norm(2, TW)
            st_(5)
            for blk in range(16):
                su, sub_ = wget("A", UA_KV + blk)
                u = uA(su)
                if blk < 8:
                    for half in range(2):
                        c = 2 * blk + half
                        bank, bb = G.next()
                        for dc in range(DC):
                            mm(bank[:, 0:TW], u[:, dc, half * 128:(half + 1) * 128], xb[:, dc, 0:TW], dc == 0, dc == DC - 1, [sub_, Bxb[dc]], [bb])
                        cp("act", sA[:, c, 0:TW], bank[:, 0:TW], [bb], [BsA[c]])
                for b in range(nb):
                    bank, bb = G.next()
                    for dc in range(DC):
                        mm(bank[0:ntp, 0:256], xb[:, dc, b * 128:b * 128 + ntp], u[:, dc, :], dc == 0, dc == DC - 1, [sub_, Bxb[dc]], [bb])
                    t1, t1b = tmpA.next()
                    cp("dve", t1[0:ntp, 0:256], bank[0:ntp, 0:256], [bb], [t1b])
                    col = (blk % 8) * 256
                    if prompt:
                        dst = (kp if blk < 8 else vp)[t0 + b * 128:t0 + (b + 1) * 128, col:col + 256]
                        P.dma("pool", dst, t1[:, 0:256], reads=[t1b], is_output=True)
                    else:
                        dst = (ksm if blk < 8 else vsm)[:, col:col + 256]
                        P.dma("pool", dst, t1[0:NSAMP, 0:256], reads=[t1b], is_output=True)
                    if blk >= 8:
                        t2, t2b = tmpB.next()
                        cp("act", t2[0:ntp, 0:256], t1[0:ntp, 0:256], [t1b], [t2b])
                        if prompt:
                            P.dma("pool", Vscr[t0 + b * 128:t0 + (b + 1) * 128, col:col + 256], t2[:, 0:256], reads=[t2b], writes=[B_Vscr])
                        else:
                            P.dma("pool", Vs[:, col:col + 256], t2[0:64, 0:256], reads=[t2b], writes=[B_Vs])
            if prompt:
                P.dma("pool", KTscr[:, t0:t0 + TW].rearrange("(c p) t -> p c t", p=128), sA[:, :, 0:TW], reads=BsA, writes=[B_KTscr])
            else:
                P.dma("pool", KTs.rearrange("(c p) t -> p c t", p=128), sA[:, :, 0:64], reads=BsA, writes=[B_KTs])
            st_(6)
            ffn(2, TW); layer_norm(3, TW)

            def evq(c, bank, bb):
                cp("act" if c % 2 == 0 else "dve", sA[:, c, 0:TW], bank[:, 0:TW], [bb], [BsA[c]])
            proj_fm([[(UA_Q + blk, xb, Bxb)] for blk in range(8)], TW, evq)
            st_(7)
            attention(TW, prompt, t0)
            st_(8)
            proj_fm([[(UA_DWO + blk, sB, BsB)] for blk in range(8)], TW, resid_evac(TW))
            layer_norm(4, TW)
            ffn(3, TW); layer_norm(5, TW)
            for b in range(nb):
                for g4 in range(4):
                    bank, bb = G.next()
                    for i in range(4):
                        c = g4 * 4 + i
                        tr(bank[0:ntp, i * 128:(i + 1) * 128], xres[:, c, b * 128:b * 128 + ntp], ident, [Bx[c], Bc], [bb])
                    cp("act" if g4 % 2 == 0 else "dve", tokst[0:ntp, b, g4 * 512:(g4 + 1) * 512], bank[0:ntp, 0:512], [bb], Btk[b][g4 * 4:g4 * 4 + 4])
                if prompt:
                    P.dma("pool", yp[t0 + b * 128:t0 + (b + 1) * 128, :], tokst[:, b, :], reads=Btk[b], is_output=True)
                else:
                    P.dma("pool", ys, tokst[0:NSAMP, 0, :], reads=Btk[0], is_output=True)

        def state_in():
            mset("pool", tokst[:, 0, :], 0.0, Btk[0])
            tk = tokst[:, 0, :].rearrange("p (c f) -> p c f", c=16)
            for h in range(2):
                hs = slice(h * 64, h * 64 + 64)
                P.dma("pool", tk[hs, :, h * 64:h * 64 + 64], swkv[hs, :, :], writes=Btk[0])
            for c in range(16):
                bank, bb = G.next()
                tr(bank[:, 0:128], tk[:, c, :], ident, Btk[0] + [Bc], [bb])
                cp("act" if c % 2 == 0 else "dve", Hst[:, c, :], bank[:, 0:128], [bb], [BH[c]])
            P.dma("pool", carry[:], sshift, writes=[Bcar])

        def state_out(dst):
            tk = tokst[:, 1, :].rearrange("p (c f) -> p c f", c=16)
            for c in range(16):
                bank, bb = G.next()
                tr(bank[:, 0:128], Hst[:, c, :], ident, [BH[c], Bc], [bb])
                cp("act" if c % 2 == 0 else "dve", tk[:, c, :], bank[:, 0:128], [bb], Btk[1])
            for h in range(2):
                hs = slice(h * 64, h * 64 + 64)
                P.dma("pool", dst[hs, :, :], tk[hs, :, h * 64:h * 64 + 64], reads=Btk[1], is_output=True)

        def main_all():
            stage(1)
            state_in()
            stage(11)
            tile_pass(False, 0)
            stage(9)
            state_out(wkvs)
            stage(10)
            for c in range(16):
                mset("pool", Hst[:, c, :], 0.0, [BH[c]])
            mset("pool", carry[:], 0.0, [Bcar])
            for it in range(NT):
                tile_pass(True, it)
            state_out(wkvp)

        keep = P.dry
        P.dry = True
        main_all()
        P.dry = keep
        REAL[0] = True
        main_all()
        P.dry = False
        print('OPCOUNTS', P.cnt, flush=True)
        P.emit()
    return nc


def _unitsA(W):
    n = W.shape[1] // 256
    return np.ascontiguousarray(W.reshape(16, 128, n, 256).transpose(2, 1, 0, 3)).reshape(n, 128, 4096)


def _unitsB(W):
    return np.ascontiguousarray(W.reshape(4, 11, 128, 8, 256).transpose(3, 0, 2, 1, 4)).reshape(32, 128, BW)


def _col(v):
    return np.ascontiguousarray(v.reshape(16, 128).T)


def _consts():
    c = np.zeros((128, C_W), np.float32)
    c[:, C_ID:C_ID + 128] = np.eye(128, dtype=np.float32)
    blk = np.zeros((128, 128), np.float32)
    blk[:64, :64] = 1; blk[64:, 64:] = 1
    c[:, C_BO:C_BO + 128] = blk
    i = np.arange(128)
    same = (i[:, None] // 64) == (i[None, :] // 64)
    r, cc = i[:, None] % 64, i[None, :] % 64
    SL = (same & (cc < r)).astype(np.float32)
    SU = (same & (r < cc)).astype(np.float32)
    UI = (same & (r <= cc)).astype(np.float32)
    c[:, C_MSL:C_MSL + 128] = -SL
    c[:, C_MK:C_MK + 128] = SU; c[:, C_MK + 128:C_MK + 256] = UI
    c[:, C_MB:C_MB + 128] = -SU; c[:, C_MB + 128:C_MB + 256] = -UI
    j = np.arange(64)
    c[:, C_TI:C_TI + 64] = np.tile((j[:, None] <= j[None, :]).astype(np.float32), (2, 1))
    c[:, C_TE:C_TE + 64] = np.tile((j[:, None] < j[None, :]).astype(np.float32), (2, 1))
    return c


def _shared_inputs(inp):
    f = lambda a: np.asarray(a, np.float32)
    pv = np.zeros((128, NPV, 16), np.float32)
    ln_g, ln_b = f(inp["ln_g"]).reshape(6, D), f(inp["ln_b"]).reshape(6, D)
    for i in range(6):
        pv[:, PV_LNG + i] = _col(ln_g[i]); pv[:, PV_LNB + i] = _col(ln_b[i])
    pv[:, PV_A0] = _col(f(inp["rwkv_a0"])[0]); pv[:, PV_KK] = _col(f(inp["rwkv_k_k"])[0])
    pv[:, PV_KA] = _col(f(inp["rwkv_k_a"])[0]); pv[:, PV_RK] = _col(f(inp["rwkv_r_k"])[0].reshape(D))
    pv[:, PV_LXG] = _col(f(inp["rwkv_lnx_g"])[0]); pv[:, PV_LXB] = _col(f(inp["rwkv_lnx_b"])[0])
    mu = f(inp["rwkv_mu"])[0]
    for j in range(6):
        pv[:, PV_MU + j] = _col(mu[j])
    wA = np.empty((NSA, 128, 4096), np.float32)
    w_in = f(inp["ffn_w_in"]); w_out = f(inp["ffn_w_out"])
    for i in range(2):
        for s in range(2):
            wgu = np.concatenate([w_in[i, s][:, :FF].reshape(D, MC, 128), w_in[i, s][:, FF:].reshape(D, MC, 128)], axis=2).reshape(D, 2 * FF)
            wA[SA_FFN + (i * 2 + s) * 44: SA_FFN + (i * 2 + s + 1) * 44] = _unitsA(wgu)
    rkvw = f(inp["rwkv_w_rkv"])[0]
    for j in range(3):
        wA[SA_RKV + j * 8: SA_RKV + (j + 1) * 8] = _unitsA(rkvw[j])
    L1 = np.zeros((D, 256), np.float32)
    L1[:, 32:128] = f(inp["rwkv_w1"])[0]; L1[:, 128:224] = f(inp["rwkv_a1"])[0]
    wA[SA_L1:SA_L1 + 1] = _unitsA(L1)
    wA[SA_G1:SA_G1 + 1] = _unitsA(f(inp["rwkv_g1"])[0])
    wA[SA_WO:SA_WO + 8] = _unitsA(f(inp["rwkv_w_o"])[0])
    wA[SA_KV:SA_KV + 16] = _unitsA(f(inp["kv_w"]))
    wA[SA_Q:SA_Q + 8] = _unitsA(f(inp["diff_w_q"])[0])
    wA[SA_DWO:SA_DWO + 8] = _unitsA(f(inp["diff_w_o"])[0])
    wB = np.empty((NUB, 128, BW), np.float32)
    for i in range(2):
        for s in range(2):
            wB[(i * 2 + s) * 32:(i * 2 + s + 1) * 32] = _unitsB(w_out[i, s])
    w2aug = np.zeros((128, D), np.float32)
    w2aug[0] = f(inp["rwkv_w0"])[0]; w2aug[32:128] = f(inp["rwkv_w2"])[0]
    a2p = np.zeros((128, D), np.float32); a2p[0:96] = f(inp["rwkv_a2"])[0]
    g2p = np.ascontiguousarray(f(inp["rwkv_g2"])[0].reshape(2, 128, D).transpose(1, 0, 2))
    return {"pcols": pv, "wA_src": wA, "wB_src": wB, "w2aug": w2aug, "a2p": a2p, "g2p": g2p, "consts": _consts(),
            "dlam": f(inp["diff_lambda"]).reshape(1, 256), "subg": f(inp["diff_subln_g"]).reshape(1, 128)}


def _state_layout(S):
    return np.ascontiguousarray(S.reshape(16, 2, 64, 64).transpose(1, 2, 0, 3)).reshape(128, 16, 64)


def _state_unlayout(A):
    return np.ascontiguousarray(A.reshape(2, 64, 16, 64).transpose(2, 0, 1, 3)).reshape(32, 64, 64)


_NC_CACHE = {}


def kernel(**inp):
    f = lambda a: np.asarray(a, np.float32)
    x_prompt = f(inp["x_prompt"]); x_sample = f(inp["x_sample"])
    NB, SEQ = x_prompt.shape[0], x_prompt.shape[1]
    shared = _shared_inputs(inp)
    ck = f(inp["cache_k"]).reshape(NB, PAST, D); cv = f(inp["cache_v"]).reshape(NB, PAST, D)
    swkv = f(inp["state_wkv"])[0]; sshift = f(inp["state_shift"])[0]
    in_maps = []
    for i in range(NB):
        m = dict(shared)
        m.update({"xp": x_prompt[i], "xs": x_sample[i], "ck": ck[i], "cv": cv[i],
                  "swkv": _state_layout(swkv[i]), "sshift": _col(sshift[i])})
        in_maps.append(m)
    if SEQ not in _NC_CACHE:
        _NC_CACHE[SEQ] = build(SEQ)
    nc = _NC_CACHE[SEQ]
    res = run_bass_kernel_spmd(nc, in_maps, core_ids=list(range(NB)))
    R = res.results
    g = lambda k: np.stack([np.asarray(R[i][k], np.float32) for i in range(NB)])
    y_prompt = g("yp"); y_sample = g("ys")
    k_prompt = g("kp").reshape(NB, SEQ, 32, 64); v_prompt = g("vp").reshape(NB, SEQ, 16, 128)
    k_sample = g("ksm").reshape(NB, NSAMP, 32, 64); v_sample = g("vsm").reshape(NB, NSAMP, 16, 128)
    wkv_prompt = np.stack([_state_unlayout(np.asarray(R[i]["wkvp"], np.float32)) for i in range(NB)])[None]
    wkv_sample = np.stack([_state_unlayout(np.asarray(R[i]["wkvs"], np.float32)) for i in range(NB)])[None]
    unc = lambda a: np.ascontiguousarray(a.T).reshape(D)
    shift_prompt = np.stack([unc(np.asarray(R[i]["shp"], np.float32)) for i in range(NB)])[None]
    shift_sample = np.stack([unc(np.asarray(R[i]["shs"], np.float32)) for i in range(NB)])[None]
    return (y_prompt, y_sample, k_prompt, v_prompt, wkv_prompt, shift_prompt,
            k_sample, v_sample, wkv_sample, shift_sample)
```
